# Optimizing a Trainium2 kernel written in Bass

```python
import math
import jax, jax.numpy as jnp
from jax import lax
import numpy as np

D_MODEL = 1024
BATCH = 32
SEQ = 256
DEPTH = 2
DEC_BATCH = 8
DEC_SEQ = 1024
PAST_LEN = 512

GRID_W = 64
D_FF = 2816
CONV_W = 3
W_A = 512
H_D = 4
DK_D = 128
DV_D = 128
CHUNK_D = 64
H_G = 4
DK_G = 64
DV_G = 128
GLA_RANK = 16
GLA_TAU = 16.0
CHUNK_G = 16
N_BRANCH = 3
N_ADA = 9
ALPHA = float((2 * DEPTH) ** 0.25)
BETA_INIT = float((8 * DEPTH) ** -0.25)
LN_EPS = 1e-5
RMS_EPS = 1e-6
PROJ_WIDTHS = (W_A, W_A, W_A,
               H_D * DK_D, H_D * DK_D, H_D * DV_D, H_D * DV_D, 2 * H_D, 2 * H_D,
               H_G * DK_G, H_G * DK_G, H_G * DV_G, H_G * DV_G, 2 * GLA_RANK,
               N_BRANCH * D_MODEL)
D_PROJ = sum(PROJ_WIDTHS)
PROJ_SPLITS = tuple(int(s) for s in np.cumsum(PROJ_WIDTHS)[:-1])

kernel_name = 'hybrid_diffusion_conv_deltanet_gla_step'

F32 = jnp.float32


def layer_norm(x, g, b):
    xf = x.astype(F32)
    mu = jnp.mean(xf, -1, keepdims=True)
    var = jnp.mean(jnp.square(xf - mu), -1, keepdims=True)
    return ((xf - mu) * lax.rsqrt(var + LN_EPS) * g.astype(F32) + b.astype(F32)).astype(x.dtype)


def rms_norm(x, g):
    return x * lax.rsqrt(jnp.mean(jnp.square(x), -1, keepdims=True) + RMS_EPS) * g.astype(F32)


def l2_normalize(x):
    return x * lax.rsqrt(jnp.sum(jnp.square(x), -1, keepdims=True) + RMS_EPS)


def swiglu(x, w1, w2):
    gate, up = jnp.split(x @ w1, 2, axis=-1)
    return (jax.nn.silu(gate) * up) @ w2


def centred_conv(x, w, rows):
    b, l, ch = x.shape
    seq = x if rows is None else x.reshape(b * rows, GRID_W, ch)
    pad = CONV_W // 2
    n = seq.shape[1]
    xp = jnp.pad(seq, ((0, 0), (pad, pad), (0, 0)))
    y = sum(xp[:, j:j + n] * w[j] for j in range(CONV_W))
    return y.reshape(b, l, ch)


def to_heads(t, n_heads):
    b, l, _ = t.shape
    return t.reshape(b, l, n_heads, -1).transpose(0, 2, 1, 3).astype(F32)


def flip_seq(t):
    return jnp.flip(t, axis=2)


def gated_delta_chunked(q, k, v, g, beta, s0):
    b, h, l, dk = q.shape
    dv = v.shape[-1]
    c = CHUNK_D
    n = l // c
    q = q.reshape(b, h, n, c, dk)
    k = k.reshape(b, h, n, c, dk)
    v = v.reshape(b, h, n, c, dv)
    beta = beta.reshape(b, h, n, c)
    G = jnp.cumsum(g.reshape(b, h, n, c), axis=-1)
    idx = jnp.arange(c)
    causal = idx[:, None] >= idx[None, :]
    strict = idx[:, None] > idx[None, :]
    gamma = jnp.exp(jnp.where(causal, G[..., :, None] - G[..., None, :], -jnp.inf))
    kb = k * beta[..., None]
    m = jnp.where(strict, jnp.einsum('bhnid,bhnjd->bhnij', kb, k) * gamma, 0.0)
    eye = jnp.eye(c, dtype=F32)
    rhs = jnp.concatenate([v * beta[..., None], kb * jnp.exp(G)[..., None]], axis=-1)
    sol = lax.linalg.triangular_solve(m + eye, rhs, left_side=True, lower=True, unit_diagonal=True)
    u, w = sol[..., :dv], sol[..., dv:]
    attn = jnp.einsum('bhnid,bhnjd->bhnij', q, k) * gamma
    q_dec = q * jnp.exp(G)[..., None]
    k_dec = k * jnp.exp(G[..., -1:] - G)[..., None]
    last = jnp.exp(G[..., -1])

    def step(s, inp):
        u_i, w_i, a_i, qd_i, kd_i, last_i = inp
        v_new = u_i - jnp.einsum('bhck,bhkv->bhcv', w_i, s)
        o_i = jnp.einsum('bhck,bhkv->bhcv', qd_i, s) + jnp.einsum('bhij,bhjv->bhiv', a_i, v_new)
        s = s * last_i[..., None, None] + jnp.einsum('bhck,bhcv->bhkv', kd_i, v_new)
        return s, o_i

    xs = tuple(jnp.moveaxis(t, 2, 0) for t in (u, w, attn, q_dec, k_dec, last))
    s_fin, o = lax.scan(step, s0, xs)
    return jnp.moveaxis(o, 0, 2).reshape(b, h, l, dv), s_fin


def gla_chunked(q, k, v, log_a, s0):
    b, h, l, dk = q.shape
    dv = v.shape[-1]
    c = CHUNK_G
    n = l // c
    q = q.reshape(b, h, n, c, dk)
    k = k.reshape(b, h, n, c, dk)
    v = v.reshape(b, h, n, c, dv)
    Bc = jnp.cumsum(log_a.reshape(b, h, n, c, dk), axis=3)
    idx = jnp.arange(c)
    causal = (idx[:, None] >= idx[None, :])[..., None]
    decay = jnp.exp(jnp.where(causal, Bc[..., :, None, :] - Bc[..., None, :, :], -jnp.inf))
    attn = jnp.einsum('bhnid,bhnjd,bhnijd->bhnij', q, k, decay)
    q_dec = q * jnp.exp(Bc)
    k_dec = k * jnp.exp(Bc[..., -1:, :] - Bc)
    last = jnp.exp(Bc[..., -1, :])

    def step(s, inp):
        qd_i, kd_i, a_i, v_i, last_i = inp
        o_i = jnp.einsum('bhck,bhkv->bhcv', qd_i, s) + jnp.einsum('bhij,bhjv->bhiv', a_i, v_i)
        s = s * last_i[..., :, None] + jnp.einsum('bhck,bhcv->bhkv', kd_i, v_i)
        return s, o_i

    xs = tuple(jnp.moveaxis(t, 2, 0) for t in (q_dec, k_dec, attn, v, last))
    s_fin, o = lax.scan(step, s0, xs)
    return jnp.moveaxis(o, 0, 2).reshape(b, h, l, dv), s_fin


def bidirectional(scan_fn, q, k, v, gates_f, gates_b, s0):
    o_f, s_f = scan_fn(q, k, v, *gates_f, s0[:, 0])
    o_b, s_b = scan_fn(flip_seq(q), flip_seq(k), flip_seq(v), *[flip_seq(t) for t in gates_b], s0[:, 1])
    return o_f + flip_seq(o_b), jnp.stack([s_f, s_b], axis=1)


def token_mixer(h, lp, rows, s_delta0, s_gla0):
    b, l, _ = h.shape
    (a_x, a_b, a_c, d_q, d_k, d_v, d_z, d_beta, d_a,
     g_q, g_k, g_v, g_r, g_lr, m_g) = jnp.split(h @ lp['w_in'], PROJ_SPLITS, axis=-1)

    y_a = a_b * centred_conv(a_c * a_x, lp['conv_a'], rows)
    br_a = y_a @ lp['w_br_a']

    qkv = jax.nn.silu(centred_conv(jnp.concatenate([d_q, d_k, d_v], axis=-1), lp['conv_qkv'], rows))
    q, k, v = jnp.split(qkv, (H_D * DK_D, 2 * H_D * DK_D), axis=-1)
    q = l2_normalize(to_heads(q, H_D)) * (DK_D ** -0.5)
    k = l2_normalize(to_heads(k, H_D))
    v = to_heads(v, H_D)
    beta = jax.nn.sigmoid(d_beta.astype(F32)).reshape(b, l, 2, H_D).transpose(2, 0, 3, 1)
    a_in = d_a.astype(F32).reshape(b, l, 2, H_D).transpose(2, 0, 3, 1)
    g_dec = -jnp.exp(lp['delta_a_log'].astype(F32))[:, None, :, None] * jax.nn.softplus(
        a_in + lp['delta_dt_bias'].astype(F32)[:, None, :, None])
    o_d, s_delta = bidirectional(gated_delta_chunked, q, k, v, (g_dec[0], beta[0]), (g_dec[1], beta[1]), s_delta0)
    o_d = rms_norm(o_d.transpose(0, 2, 1, 3), lp['delta_norm_g']) * jax.nn.silu(
        d_z.astype(F32).reshape(b, l, H_D, DV_D))
    br_d = o_d.reshape(b, l, H_D * DV_D).astype(h.dtype) @ lp['w_br_d']

    gq = to_heads(g_q, H_G) * (DK_G ** -0.5)
    gk = to_heads(g_k, H_G)
    gv = to_heads(g_v, H_G)
    lr = g_lr.astype(F32).reshape(b, l, 2, GLA_RANK)
    logits = jnp.einsum('blsr,srk->sblk', lr, lp['gla_w2'].astype(F32)) + lp['gla_b'].astype(F32)[:, None, None, :]
    log_a = (jax.nn.log_sigmoid(logits) / GLA_TAU).reshape(2, b, l, H_G, DK_G).transpose(0, 1, 3, 2, 4)
    o_g, s_gla = bidirectional(gla_chunked, gq, gk, gv, (log_a[0],), (log_a[1],), s_gla0)
    o_g = rms_norm(o_g.transpose(0, 2, 1, 3), lp['gla_norm_g']) * jax.nn.silu(
        g_r.astype(F32).reshape(b, l, H_G, DV_G))
    br_g = o_g.reshape(b, l, H_G * DV_G).astype(h.dtype) @ lp['w_br_g']

    gates = jax.nn.sigmoid(m_g).reshape(b, l, N_BRANCH, D_MODEL)
    merged = gates[:, :, 0] * br_a + gates[:, :, 1] * br_d + gates[:, :, 2] * br_g
    return merged @ lp['w_o'], s_delta, s_gla


def trunk_layer(x, cond, lp, rows, s_delta0, s_gla0):
    ada = (jax.nn.silu(cond) @ lp['w_ada'] + lp['b_ada']).reshape(cond.shape[0], 1, N_ADA, D_MODEL)

    def mod(t, j):
        return t * (1.0 + ada[:, :, 3 * j + 1]) + ada[:, :, 3 * j]

    x = layer_norm(ALPHA * x + 0.5 * ada[:, :, 2] * swiglu(mod(x, 0), lp['ffn_w1'][0], lp['ffn_w2'][0]),
                   lp['ln_g'][0], lp['ln_b'][0])
    y, s_delta, s_gla = token_mixer(mod(x, 1), lp, rows, s_delta0, s_gla0)
    x = layer_norm(ALPHA * x + ada[:, :, 5] * y, lp['ln_g'][1], lp['ln_b'][1])
    x = layer_norm(ALPHA * x + 0.5 * ada[:, :, 8] * swiglu(mod(x, 2), lp['ffn_w1'][1], lp['ffn_w2'][1]),
                   lp['ln_g'][2], lp['ln_b'][2])
    return x, s_delta, s_gla


def setup_inputs(seed: int = 0) -> dict:
    key = jax.random.key(seed)
    ks = jax.random.split(key, 32)

    def nrm(k, shape, s):
        return jax.random.normal(k, shape, F32) * s

    L = DEPTH
    dt = jnp.exp(jax.random.uniform(ks[13], (L, 2, H_D), F32, math.log(1e-3), math.log(1e-1)))
    return {
        'x_prompt': nrm(ks[0], (BATCH, SEQ, D_MODEL), 1.0),
        'x_sample': nrm(ks[1], (DEC_BATCH, DEC_SEQ, D_MODEL), 1.0),
        'state_delta': nrm(ks[2], (DEC_BATCH, DEPTH, 2, H_D, DK_D, DV_D), 0.1),
        'state_gla': nrm(ks[3], (DEC_BATCH, DEPTH, 2, H_G, DK_G, DV_G), 0.1),
        'c': nrm(ks[4], (DEC_BATCH, D_MODEL), 1.0),
        'c_ctx': nrm(ks[5], (D_MODEL,), 1.0),
        'w_ada': nrm(ks[6], (L, D_MODEL, N_ADA * D_MODEL), D_MODEL ** -0.5),
        'b_ada': nrm(ks[7], (L, N_ADA * D_MODEL), 0.02),
        'ln_g': 1.0 + nrm(ks[8], (L, 3, D_MODEL), 0.02),
        'ln_b': nrm(ks[9], (L, 3, D_MODEL), 0.02),
        'ffn_w1': nrm(ks[10], (L, 2, D_MODEL, 2 * D_FF), D_MODEL ** -0.5),
        'ffn_w2': nrm(ks[11], (L, 2, D_FF, D_MODEL), BETA_INIT * D_FF ** -0.5),
        'w_in': nrm(ks[12], (L, D_MODEL, D_PROJ), D_MODEL ** -0.5),
        'conv_a': nrm(ks[14], (L, CONV_W, W_A), CONV_W ** -0.5),
        'conv_qkv': nrm(ks[15], (L, CONV_W, 2 * H_D * DK_D + H_D * DV_D), CONV_W ** -0.5),
        'delta_a_log': jnp.log(jax.random.uniform(ks[16], (L, 2, H_D), F32, 1.0, 16.0)),
        'delta_dt_bias': dt + jnp.log(-jnp.expm1(-dt)),
        'delta_norm_g': 1.0 + nrm(ks[17], (L, DV_D), 0.02),
        'gla_w2': nrm(ks[18], (L, 2, GLA_RANK, H_G * DK_G), GLA_RANK ** -0.5),
        'gla_b': nrm(ks[19], (L, 2, H_G * DK_G), 0.02),
        'gla_norm_g': 1.0 + nrm(ks[20], (L, DV_G), 0.02),
        'w_br_a': nrm(ks[21], (L, W_A, D_MODEL), W_A ** -0.5),
        'w_br_d': nrm(ks[22], (L, H_D * DV_D, D_MODEL), (H_D * DV_D) ** -0.5),
        'w_br_g': nrm(ks[23], (L, H_G * DV_G, D_MODEL), (H_G * DV_G) ** -0.5),
        'w_o': nrm(ks[24], (L, D_MODEL, D_MODEL), BETA_INIT * D_MODEL ** -0.5),
    }


def reference(x_prompt, x_sample, state_delta, state_gla, c, c_ctx, w_ada, b_ada, ln_g, ln_b,
              ffn_w1, ffn_w2, w_in, conv_a, conv_qkv, delta_a_log, delta_dt_bias, delta_norm_g,
              gla_w2, gla_b, gla_norm_g, w_br_a, w_br_d, w_br_g, w_o):
    layers = [dict(w_ada=w_ada[i], b_ada=b_ada[i], ln_g=ln_g[i], ln_b=ln_b[i], ffn_w1=ffn_w1[i],
                   ffn_w2=ffn_w2[i], w_in=w_in[i], conv_a=conv_a[i], conv_qkv=conv_qkv[i],
                   delta_a_log=delta_a_log[i], delta_dt_bias=delta_dt_bias[i], delta_norm_g=delta_norm_g[i],
                   gla_w2=gla_w2[i], gla_b=gla_b[i], gla_norm_g=gla_norm_g[i], w_br_a=w_br_a[i],
                   w_br_d=w_br_d[i], w_br_g=w_br_g[i], w_o=w_o[i]) for i in range(DEPTH)]

    n_ctx = x_prompt.shape[0]
    zero_d = jnp.zeros((n_ctx, 2, H_D, DK_D, DV_D), F32)
    zero_g = jnp.zeros((n_ctx, 2, H_G, DK_G, DV_G), F32)
    h = x_prompt
    sds, sgs = [], []
    for i in range(DEPTH):
        h, sd, sg = trunk_layer(h, c_ctx[None, :], layers[i], None, zero_d, zero_g)
        sds.append(sd)
        sgs.append(sg)
    y_prompt = h
    new_state_delta = jnp.stack(sds, axis=1).astype(x_prompt.dtype)
    new_state_gla = jnp.stack(sgs, axis=1).astype(x_prompt.dtype)

    rows = x_sample.shape[1] // GRID_W
    h = x_sample
    for i in range(DEPTH):
        h, _, _ = trunk_layer(h, c, layers[i], rows, state_delta[:, i].astype(F32), state_gla[:, i].astype(F32))
    y_sample = h
    return (y_prompt, y_sample, new_state_delta, new_state_gla)
```

```python
import numpy as np
from contextlib import ExitStack
import concourse.bass as bass
import concourse.mybir as mybir
from concourse.bass_utils import run_bass_kernel_spmd

F32 = mybir.dt.float32
BF16 = mybir.dt.bfloat16
AF = mybir.ActivationFunctionType
ALU = mybir.AluOpType

PE, ACT, DVE, POOL, SP = "pe", "act", "dve", "pool", "sp"
COMPUTE = (PE, ACT, DVE, POOL)
DEBUG_SRC = False
_FW_NAMES = ("_record", "op", "dma", "matmul", "transpose", "act", "tt", "ts", "stt", "copy", "memset", "proj", "conv3", "<lambda>")


class T:
    def __init__(self, prog, h, name, psum=False):
        self.prog, self.h, self.name, self.psum = prog, h, name, psum
        self.last_write = None
        self.reads = []
        self.sem = None
        self.dma_n = 0

    def __getitem__(self, idx):
        return V(self, self.h[idx])

    @property
    def ap(self):
        return self.h[:] if not isinstance(self.h, bass.AP) else self.h


class V:
    def __init__(self, t, ap):
        self.t, self.ap = t, ap

    def __getitem__(self, idx):
        return V(self.t, self.ap[idx])


def _ap(x):
    return x.ap if isinstance(x, (V, T)) else x


def _tiles(xs):
    out = []
    for x in xs:
        if isinstance(x, V):
            out.append(x.t)
        elif isinstance(x, T):
            out.append(x)
    return out


class Ins:
    __slots__ = ("id", "eng", "fn", "deps", "is_dma", "dma_ev", "needed", "tick", "flushed", "src")


class Prog:
    def __init__(self, nc):
        self.nc = nc
        self.stack = ExitStack()
        self.ins = []
        self.pending = []
        self.engs = {PE: nc.tensor, ACT: nc.scalar, DVE: nc.vector, POOL: nc.gpsimd, SP: nc.sync}
        self.sem = {}
        for e in COMPUTE:
            self.sem[e] = self.stack.enter_context(nc.semaphore("s_" + e))
        self.ticks = {e: 0 for e in COMPUTE}
        self.waited = {e: {} for e in self.engs}
        self.nsem = 0
        self.out_events = []
        self.sp_events = {}
        self.last_needed = {e: None for e in COMPUTE}

    def sb(self, name, shape, dtype, stack=None):
        self.nalloc = getattr(self, "nalloc", 0) + 1
        name = "%s_%d" % (name, self.nalloc)
        h = (stack or self.stack).enter_context(self.nc.sbuf_tensor(name, list(shape), dtype))
        return T(self, h, name)

    def ps(self, name, shape, dtype=F32, stack=None):
        h = (stack or self.stack).enter_context(self.nc.psum_tensor(name, list(shape), dtype))
        return T(self, h, name, psum=True)

    def _dma_sem(self, t):
        if t.sem is None:
            t.sem = self.stack.enter_context(self.nc.semaphore("d%d_%s" % (self.nsem, t.name[:12])))
            self.nsem += 1
        return t.sem

    def _record(self, eng, fn, reads, writes, is_dma=False, dma_tile=None):
        i = Ins()
        i.id = len(self.ins)
        i.eng, i.fn, i.is_dma = eng, fn, is_dma
        i.deps = {}
        i.needed = False
        i.tick = None
        i.flushed = False
        i.dma_ev = None
        i.src = ""
        if DEBUG_SRC:
            import sys as _sys
            f = _sys._getframe(1)
            while f is not None and f.f_code.co_name in _FW_NAMES:
                f = f.f_back
            i.src = "L%d" % f.f_lineno if f is not None else ""
        rt, wt = _tiles(reads), _tiles(writes)
        if is_dma:
            sem = self._dma_sem(dma_tile)
            dma_tile.dma_n += 1
            i.dma_ev = ("d", sem, 16 * dma_tile.dma_n)
            ev = i.dma_ev
        else:
            ev = ("i", i.id)

        def add(e, kind):
            if e is None or e == ev:
                return
            if i.deps.get(e) != "raw":
                i.deps[e] = kind

        for t in rt:
            add(t.last_write, "raw")
            if t.psum:
                for r in t.reads:
                    add(r, "war")
        for t in wt:
            lw = t.last_write
            if not (is_dma and lw is not None and lw[0] == "d" and lw[1] is dma_tile.sem
                    and t is dma_tile and not t.reads):
                add(lw, "waw")
            for r in t.reads:
                add(r, "war")
        for t in rt:
            if t not in wt:
                t.reads.append(ev)
        for t in wt:
            t.last_write = ev
            t.reads = []
        self.ins.append(i)
        self.pending.append(i)
        return i

    def op(self, eng, fn, reads=(), writes=()):
        return self._record(eng, fn, reads, writes)

    def dma(self, eng, out, in_, **kw):
        on_chip = out if isinstance(out, (V, T)) else in_
        t = on_chip.t if isinstance(on_chip, V) else on_chip
        o, i_ = _ap(out), _ap(in_)
        sem = self._dma_sem(t)

        def fn(e):
            return e.dma_start(out=o, in_=i_, **kw)
        reads = [in_] if isinstance(in_, (V, T)) else []
        writes = [out] if isinstance(out, (V, T)) else []
        ins = self._record(eng, fn, reads, writes, is_dma=True, dma_tile=t)
        if not isinstance(out, (V, T)):
            self.out_events.append(ins.dma_ev)
        if eng == SP:
            self.sp_events[id(sem)] = (sem, ins.dma_ev[2])
        return ins

    def _resolve(self, e):
        if e[0] == "d":
            return e[1], e[2]
        p = self.ins[e[1]]
        return self.sem[p.eng], p

    def flush(self, final=False):
        pend = self.pending
        self.pending = []
        if not pend:
            return
        filt = {}
        for i in pend:
            keep = []
            for e, kind in i.deps.items():
                if e[0] == "i":
                    p = self.ins[e[1]]
                    if p.eng == i.eng and not i.is_dma:
                        if i.eng == PE:
                            continue
                    if not p.flushed:
                        p.needed = True
                keep.append(e)
            filt[i.id] = keep
        last = {}
        for i in pend:
            if not i.is_dma and i.eng in COMPUTE:
                last[i.eng] = i
        for i in last.values():
            i.needed = True
        for i in pend:
            if not i.is_dma and i.eng in COMPUTE and i.needed:
                self.ticks[i.eng] += 1
                i.tick = self.ticks[i.eng]
        nxt = {}
        for i in reversed(pend):
            if i.is_dma or i.eng not in COMPUTE:
                continue
            if i.tick is None:
                i.tick = nxt[i.eng]
            else:
                nxt[i.eng] = i.tick
        for i in pend:
            eh = self.engs[i.eng]
            need = {}
            for e in filt[i.id]:
                if e[0] == "d":
                    sem, val = e[1], e[2]
                else:
                    p = self.ins[e[1]]
                    sem, val = self.sem[p.eng], p.tick
                k = id(sem)
                if k not in need or need[k][1] < val:
                    need[k] = (sem, val)
            w = self.waited[i.eng]
            for k, (sem, val) in need.items():
                if w.get(k, 0) >= val:
                    continue
                eh.wait_ge(sem, val)
                w[k] = val
            b = i.fn(eh)
            if i.src:
                b.annotate(i.src)
            if i.is_dma:
                b.then_inc(i.dma_ev[1], 16)
            elif i.needed:
                b.then_inc(self.sem[i.eng], 1)
            i.flushed = True
            i.fn = None

    def finish(self):
        self.flush()
        need = {}
        for e in self.out_events:
            k = id(e[1])
            if k not in need or need[k][1] < e[2]:
                need[k] = (e[1], e[2])
        for sem, val in need.values():
            self.nc.sync.wait_ge(sem, val)

    def matmul(self, out, lhsT, rhs, start=True, stop=True, **kw):
        o, l, r = _ap(out), _ap(lhsT), _ap(rhs)
        return self.op(PE, lambda e: e.matmul(o, l, r, start=start, stop=stop, **kw),
                       reads=[lhsT, rhs], writes=[out])

    def transpose(self, out, in_, ident):
        o, i_, d = _ap(out), _ap(in_), _ap(ident)
        return self.op(PE, lambda e: e.transpose(o, i_, d), reads=[in_, ident], writes=[out])

    def act(self, out, in_, func, bias=None, scale=None, accum_out=None, eng=ACT):
        o, i_ = _ap(out), _ap(in_)
        kw = {}
        reads = [in_]
        writes = [out]
        if bias is not None:
            kw["bias"] = _ap(bias)
            reads.append(bias)
        if scale is not None:
            kw["scale"] = _ap(scale)
            reads.append(scale)
        if accum_out is not None:
            kw["accum_out"] = _ap(accum_out)
            writes.append(accum_out)
        return self.op(ACT, lambda e: e.activation(o, i_, func, **kw), reads=reads, writes=writes)

    def tt(self, eng, out, in0, in1, op):
        o, a, b = _ap(out), _ap(in0), _ap(in1)
        return self.op(eng, lambda e: e.tensor_tensor(o, a, b, op), reads=[in0, in1], writes=[out])

    def ts(self, eng, out, in0, s1, op0, s2=None, op1=None, accum_out=None):
        o, a = _ap(out), _ap(in0)
        reads = [in0]
        writes = [out]
        s1a, s2a = _ap(s1), _ap(s2)
        if isinstance(s1, (V, T)):
            reads.append(s1)
        if isinstance(s2, (V, T)):
            reads.append(s2)
        kw = {}
        if op1 is not None:
            kw["op1"] = op1
        if accum_out is not None:
            kw["accum_out"] = _ap(accum_out)
            writes.append(accum_out)
        return self.op(eng, lambda e: e.tensor_scalar(o, a, s1a, s2a, op0, **kw), reads=reads, writes=writes)

    def stt(self, eng, out, in0, scalar, in1, op0, op1):
        o, a, b = _ap(out), _ap(in0), _ap(in1)
        s = _ap(scalar)
        reads = [in0, in1]
        if isinstance(scalar, (V, T)):
            reads.append(scalar)
        return self.op(eng, lambda e: e.scalar_tensor_tensor(o, a, s, b, op0, op1), reads=reads, writes=[out])

    def copy(self, eng, out, in_):
        o, i_ = _ap(out), _ap(in_)
        if eng == ACT:
            return self.op(ACT, lambda e: e.copy(o, i_), reads=[in_], writes=[out])
        return self.op(eng, lambda e: e.tensor_copy(o, i_), reads=[in_], writes=[out])

    def memset(self, eng, out, val):
        o = _ap(out)
        return self.op(eng, lambda e: e.memset(o, val), reads=[], writes=[out])


    def barrier(self):
        self.flush()
        nc = self.nc
        for e in (PE, ACT, DVE, SP):
            eh = self.engs[e]
            w = self.waited[e]
            for e2 in COMPUTE:
                if e2 == e or self.ticks[e2] == 0:
                    continue
                k = id(self.sem[e2])
                if w.get(k, 0) >= self.ticks[e2]:
                    continue
                eh.wait_ge(self.sem[e2], self.ticks[e2])
                w[k] = self.ticks[e2]
            for k, (sem, val) in self.sp_events.items():
                if w.get(k, 0) >= val:
                    continue
                eh.wait_ge(sem, val)
                w[k] = val
D = 1024
DFF = 2816
NKC = 8
TG = 1024
NBLK = 8
DEPTH = 2
ALPHA = float((2 * DEPTH) ** 0.25)
LN_EPS = 1e-5
RMS_EPS = 1e-6
DPROJ = 8240
C_AX, C_AB, C_AC = 0, 512, 1024
C_DQ, C_DK, C_DV, C_DZ, C_DBETA, C_DA = 1536, 2048, 2560, 3072, 3584, 3592
C_GQ, C_GK, C_GV, C_GR, C_GLR, C_MG = 3600, 3856, 4112, 4624, 5136, 5168
NEG = -1.0e9
SLOT = 4096


def sub(v, offset_elems, dims):
    a = _ap(v)
    return bass.AP(a.tensor, a.offset + offset_elems, dims)


class _Mod2(list):
    def __getitem__(self, i):
        return list.__getitem__(self, i % 2)


class _Mod4(list):
    def __getitem__(self, i):
        return list.__getitem__(self, i % 4)


class Builder:
    def __init__(self, debug=None, skip=()):
        self.debug = debug or {}
        self.skip = skip
        nc = bass.Bass("TRN2", target_bir_lowering=False)
        self.nc = nc
        self.P = Prog(nc)
        self.dram = {}
        self.outs = {}
        self.declare()
        self.consts()
        self.vectors()
        self.ada_all()
        for g in range(2):
            self.group(g)
        self.P.finish()

    def din(self, name, shape):
        self.dram[name] = self.nc.dram_tensor(name, list(shape), F32, kind="ExternalInput").ap()

    def dout(self, name, shape):
        self.outs[name] = self.nc.dram_tensor(name, list(shape), F32, kind="ExternalOutput").ap()

    def declare(self):
        L = DEPTH
        self.din("xp", [TG, D]); self.din("xs", [TG, D])
        self.din("sd", [L, 2, 4, 128, 128]); self.din("sg", [L, 2, 4, 64, 128])
        self.din("cond", [2, D])
        self.din("w_ada", [L, D, 9 * D]); self.din("b_ada", [L, 9 * D])
        self.din("ln_g", [L, 3, D]); self.din("ln_b", [L, 3, D])
        self.din("ffn_w1", [L, 2, D, 2 * DFF]); self.din("ffn_w2", [L, 2, DFF, D])
        self.din("w_in", [L, D, DPROJ])
        self.din("conv_a", [L, 3, 512]); self.din("conv_qkv", [L, 3, 1536])
        self.din("delta_a_log", [L, 2, 4]); self.din("delta_dt_bias", [L, 2, 4])
        self.din("delta_norm_g", [L, 128])
        self.din("gla_w2", [L, 2, 16, 256]); self.din("gla_b", [L, 2, 256]); self.din("gla_norm_g", [L, 128])
        self.din("w_br_a", [L, 512, D]); self.din("w_br_d", [L, 512, D]); self.din("w_br_g", [L, 512, D])
        self.din("w_o", [L, D, D])
        self.dout("yp", [TG, D]); self.dout("ys", [TG, D])
        self.dout("nsd", [4, L, 2, 4, 128, 128]); self.dout("nsg", [4, L, 2, 4, 64, 128])
        for k, shp in self.debug.items():
            self.dout(k, shp)

    def bank(self):
        pool = getattr(self, "bank_pool", None)
        if pool:
            b = self.banks[pool[self.bank_i % len(pool)]]
        else:
            b = self.banks[self.bank_i % len(self.banks)]
        self.bank_i += 1
        return b

    def consts(self):
        P = self.P
        self.banks = [P.ps("bank%d" % i, [128, 512], F32) for i in range(8)]
        self.bank_i = 0
        self.slots = [P.sb("wslot%d" % i, [128, SLOT], BF16) for i in range(4)]
        self.slot_i = 0
        self.ident = P.sb("ident", [128, 128], F32)
        self.identb = P.sb("identb", [128, 128], BF16)
        self.ones = P.sb("ones", [128, 128], F32)
        P.memset(DVE, self.ones, 1.0)

        def sel(out_t, in_t, cmp, fill):
            o, i_ = _ap(out_t), _ap(in_t)
            cm, pat = 1, -1
            if cmp == ALU.is_le:
                cmp, cm, pat = ALU.is_ge, -1, 1
            elif cmp == ALU.is_lt:
                cmp, cm, pat = ALU.is_gt, -1, 1
            P.op(POOL, lambda e: e.affine_select(out=o, in_=i_, pattern=[[pat, 128]], compare_op=cmp,
                                                 fill=fill, base=0, channel_multiplier=cm),
                 reads=[in_t], writes=[out_t])
        sel(self.ident, self.ones, ALU.is_equal, 0.0)
        P.copy(DVE, self.identb, self.ident)
        mk = lambda n: P.sb(n, [128, 128], F32)
        self.NP = [P.sb("NP%d" % d, [128, 256], F32) for d in range(2)]
        self.negU = [self.NP[d][:, 0:128] for d in range(2)]
        self.posL = [self.NP[d][:, 128:256] for d in range(2)]
        self.m01 = [P.sb("m01_0", [128, 128], F32), P.sb("m01_1", [128, 128], F32)]
        self.triI = [mk("triI0"), mk("triI1")]
        self.gtriI = [mk("gtriI0"), mk("gtriI1")]
        self.gtriX = [mk("gtriX0"), mk("gtriX1")]
        self.bd16 = mk("bd16")
        self.mk = [P.sb("mk%d" % i, [128, 128], F32) for i in range(3)]
        cs_ = ExitStack()
        self.zeros = P.sb("zeros", [128, 128], F32, stack=cs_)
        negs = P.sb("negsixteenth", [128, 128], F32, stack=cs_)
        bd32 = P.sb("bd32", [128, 128], F32, stack=cs_)
        bd64 = P.sb("bd64", [128, 128], F32, stack=cs_)
        Et = {bsz: P.sb("E%d" % bsz, [128 // bsz, 128], F32, stack=cs_) for bsz in (16, 32, 64)}
        P.memset(DVE, self.zeros, 0.0)
        sel(self.negU[0], self.zeros, ALU.is_le, NEG)
        sel(self.negU[1], self.zeros, ALU.is_ge, NEG)
        sel(self.posL[0], self.zeros, ALU.is_gt, -NEG)
        sel(self.posL[1], self.zeros, ALU.is_lt, -NEG)
        sel(self.m01[0], self.ones, ALU.is_le, 0.0)
        sel(self.m01[1], self.ones, ALU.is_ge, 0.0)
        sel(self.triI[0], self.ones, ALU.is_le, 0.0)
        sel(self.triI[1], self.ones, ALU.is_ge, 0.0)
        for d in range(2):
            P.ts(DVE, self.gtriI[d], self.triI[d], -1.0 / 16.0, ALU.mult)
        P.memset(DVE, negs, -1.0 / 16.0)
        sel(self.gtriX[0], negs, ALU.is_gt, 0.0)
        sel(self.gtriX[1], negs, ALU.is_lt, 0.0)
        def blockdiag(bsz, t):
            nb_ = 128 // bsz
            E = Et[bsz]
            ea, oa = E.ap, self.ones[0:nb_, :].ap
            P.op(POOL, lambda e: e.affine_select(out=ea, in_=oa, pattern=[[1, 128]], compare_op=ALU.is_ge,
                                                 fill=0.0, base=0, channel_multiplier=-bsz), reads=[self.ones], writes=[E])
            P.op(POOL, lambda e: e.affine_select(out=ea, in_=ea, pattern=[[-1, 128]], compare_op=ALU.is_ge,
                                                 fill=0.0, base=bsz - 1, channel_multiplier=bsz), reads=[E], writes=[E])
            b = self.bank()
            P.matmul(b[:, 0:128], E, E)
            P.copy(DVE, t, b[:, 0:128])
            return t
        blockdiag(32, self.bd16)
        blockdiag(32, bd32)
        blockdiag(64, bd64)
        P.tt(DVE, self.mk[0], bd32, self.bd16, ALU.subtract)
        P.tt(DVE, self.mk[1], bd64, bd32, ALU.subtract)
        P.tt(DVE, self.mk[2], self.ones, bd64, ALU.subtract)
        P.barrier()
        cs_.close()

    def wtile(self, w2d, k0, nk, c0, cw=None):
        P = self.P
        ranges = c0 if isinstance(c0, list) else [(c0, cw)]
        tot = sum(r[1] for r in ranges)
        slot = self.slots[self.slot_i % len(self.slots)]
        self.slot_i += 1
        assert nk * tot <= SLOT
        dst = slot[:, 0:nk * tot].ap.rearrange("p (kc c) -> p kc c", kc=nk)
        off = 0
        for (a, w) in ranges:
            src = w2d[k0 * 128:(k0 + nk) * 128, a:a + w].rearrange("(kc p) c -> p kc c", p=128)
            P.dma(POOL, V(slot, dst[:, :, off:off + w]), src)
            off += w
        return V(slot, dst)

    def load_cols(self, dst_cols, rows_ap, nrows):
        P = self.P
        st = self.stg[self.stg_i % 2]
        self.stg_i += 1
        P.dma(SP, st[0:nrows, :], rows_ap)
        b = self.bank()
        P.transpose(b[:, 0:nrows], st[0:nrows, :], self.ident[0:nrows, 0:nrows])
        P.copy(DVE, dst_cols, b[:, 0:nrows])

    def vectors(self):
        P = self.P
        dr = self.dram
        self.stg = [P.sb("stg0", [128, 128], F32), P.sb("stg1", [128, 128], F32)]
        self.stg_i = 0
        self.vec = []
        for l in range(DEPTH):
            v = {}
            v["b_ada"] = P.sb("b_ada%d" % l, [128, 72], F32)
            self.load_cols(v["b_ada"][:, :], dr["b_ada"][l].rearrange("(r p) -> r p", p=128), 72)
            v["ln_g"] = P.sb("ln_g%d" % l, [128, 24], F32)
            self.load_cols(v["ln_g"][:, :], dr["ln_g"][l].rearrange("j (r p) -> (j r) p", p=128), 24)
            v["ln_b"] = P.sb("ln_b%d" % l, [128, 24], F32)
            self.load_cols(v["ln_b"][:, :], dr["ln_b"][l].rearrange("j (r p) -> (j r) p", p=128), 24)
            v["conv_a"] = P.sb("conv_a%d" % l, [128, 12], F32)
            self.load_cols(v["conv_a"][:, :], dr["conv_a"][l].rearrange("j (r p) -> (j r) p", p=128), 12)
            v["conv_qkv"] = P.sb("conv_qkv%d" % l, [128, 36], F32)
            self.load_cols(v["conv_qkv"][:, :], dr["conv_qkv"][l].rearrange("j (r p) -> (j r) p", p=128), 36)
            v["dng"] = P.sb("dng%d" % l, [128, 1], F32)
            self.load_cols(v["dng"][:, :], dr["delta_norm_g"][l:l + 1, :], 1)
            v["gng"] = P.sb("gng%d" % l, [128, 1], F32)
            self.load_cols(v["gng"][:, :], dr["gla_norm_g"][l:l + 1, :], 1)
            row = P.sb("dprow%d" % l, [1, 16], F32)
            P.dma(SP, row[0:1, 0:8], dr["delta_a_log"][l:l + 1].rearrange("o d h -> o (d h)"))
            P.dma(SP, row[0:1, 8:16], dr["delta_dt_bias"][l:l + 1].rearrange("o d h -> o (d h)"))
            b = self.bank()
            P.matmul(b[:, 0:16], self.ones[0:1, :], row[0:1, :])
            v["nega"] = P.sb("nega%d" % l, [128, 8], F32)
            v["dtb"] = P.sb("dtb%d" % l, [128, 8], F32)
            P.act(v["nega"], b[:, 0:8], AF.Exp)
            P.ts(DVE, v["nega"], v["nega"], -1.0, ALU.mult)
            P.copy(DVE, v["dtb"], b[:, 8:16])
            v["w2b"] = []
            for s_ in range(2):
                t = P.sb("w2b%d_%d" % (l, s_), [17, 256], BF16)
                P.dma(POOL, t[0:16, :], dr["gla_w2"][l, s_])
                P.dma(POOL, t[16:17, :], dr["gla_b"][l, s_:s_ + 1, :])
                v["w2b"].append(t)
            self.vec.append(v)
        cT = P.sb("condT", [128, 16], F32)
        self.load_cols(cT[:, :], dr["cond"].rearrange("i (r p) -> (i r) p", p=128), 16)
        self.condb = P.sb("condb", [128, 16], BF16)
        P.act(self.condb, cT, AF.Silu)
        P.flush()

    def ada_gen(self, l, bank_idx=None):
        P = self.P
        a = self.ada[l]
        w = self.dram["w_ada"][l]
        for cb in range(18):
            wt = self.wtile(w, 0, 8, cb * 512, 512)
            if cb % 4 == 0:
                bk = self.bank() if bank_idx is None else self.banks[bank_idx]
            for m in range(4):
                mi = cb * 4 + m
                o = bk[:, (mi % 16) * 2:(mi % 16) * 2 + 2]
                for kc in range(8):
                    P.matmul(o, wt[:, kc, m * 128:(m + 1) * 128], self.condb[:, kc:16:8],
                             start=(kc == 0), stop=(kc == 7))
            if cb % 4 == 3 or cb == 17:
                nm = 16 if cb % 4 == 3 else 8
                m0 = (cb // 4) * 16
                for i in range(2):
                    P.tt(DVE, a[:, i, m0:m0 + nm], bk[:, i:2 * nm:2], self.vec[l]["b_ada"][:, m0:m0 + nm], ALU.add)
            yield

    def ada_all(self):
        P = self.P
        self.ada = [P.sb("ada%d" % l, [128, 2, 72], F32) for l in range(DEPTH)]
        for _ in self.ada_gen(0):
            pass
        self.ada1_gen = self.ada_gen(1, bank_idx=7)
        P.flush()

    def group(self, g):
        P = self.P
        self.g = g
        x_dram = self.dram["xp" if g == 0 else "xs"]
        self.y_dram = self.outs["yp" if g == 0 else "ys"]
        self.segL = 256 if g == 0 else 64
        self.seqs = [(i * 2, 2) for i in range(4)] if g == 0 else [(0, 8)]
        with ExitStack() as gs:
            self.xa = [P.sb("xa%d" % n, [128, 8, 512], F32, stack=gs) for n in range(2)]
            self.h = [P.sb("h%d" % n, [128, 8, 512], BF16, stack=gs) for n in range(2)]
            self.dvec = P.sb("dvec", [128, 2, 3, 5, 8], F32, stack=gs)
            self.gs = gs
            self.derive_vectors(g, first=True)
            with ExitStack() as ps:
                xin = [P.sb("xin%d" % i, [128, D], F32, stack=ps) for i in range(2)]
                A0, B0 = self.first[:, 0, :], self.first[:, 1, :]
                for blk in range(8):
                    t = xin[blk % 2]
                    n, bs = blk // 4, slice((blk % 4) * 128, (blk % 4) * 128 + 128)
                    P.dma(SP, t, x_dram[blk * 128:(blk + 1) * 128, :])
                    for half in range(2):
                        b = self.bank()
                        for j in range(4):
                            c = half * 4 + j
                            P.transpose(b[:, j * 128:(j + 1) * 128], t[:, c * 128:(c + 1) * 128], self.ident)
                        for j in range(4):
                            c = half * 4 + j
                            P.ts(DVE, self.h[n][:, c, bs], b[:, j * 128:(j + 1) * 128], A0[:, c:c + 1], ALU.mult,
                                 B0[:, c:c + 1], ALU.add)
                        P.op(ACT, (lambda o_=self.xa[n][:, half * 4:half * 4 + 4, bs].ap,
                                   i_=b[:, 0:512].ap.rearrange("p (c t) -> p c t", c=4):
                                   lambda e: e.mul(o_, i_, ALPHA))(), reads=[b], writes=[self.xa[n]])
                P.barrier()
            for l in range(DEPTH):
                self.ffn(l, 0, 0)
                self.mixer(l)
                self.ffn(l, 1, 2)
            P.barrier()

    def derive_vectors(self, g, first):
        P = self.P
        dv = self.dvec
        if first:
            self.dvtmp = P.sb("dvtmp%d" % g, [128, 8], F32, stack=self.gs)
            self.first = P.sb("dvfirst%d" % g, [128, 2, 8], F32, stack=self.gs)
        tmp = self.dvtmp
        ada1_ready = (self.ada1_gen is None)
        for l in range(DEPTH):
            ad = self.ada[l]
            v = self.vec[l]
            for j in range(3):
                needs1 = (l == 1) or (l == 0 and j == 2)
                if first and needs1 and not ada1_ready:
                    if l == 0:
                        pass
                    else:
                        continue
                if (not first) and not needs1:
                    continue
                if (not first) and l == 0 and j == 2:
                    lg = v["ln_g"][:, j * 8:(j + 1) * 8]
                    lb = v["ln_b"][:, j * 8:(j + 1) * 8]
                    ad2 = self.ada[1]
                    sh = ad2[:, g, 0:8]
                    sc = ad2[:, g, 8:16]
                    P.ts(DVE, tmp, sc, 1.0, ALU.add)
                    P.tt(DVE, dv[:, l, j, 3, :], lg, tmp, ALU.mult)
                    P.tt(DVE, dv[:, l, j, 4, :], lb, tmp, ALU.mult)
                    P.tt(DVE, dv[:, l, j, 4, :], dv[:, l, j, 4, :], sh, ALU.add)
                    continue
                gt = ad[:, g, (3 * j + 2) * 8:(3 * j + 2) * 8 + 8]
                P.ts(DVE, dv[:, l, j, 2, :], gt, 0.5 if j != 1 else 1.0, ALU.mult)
                lg = v["ln_g"][:, j * 8:(j + 1) * 8]
                lb = v["ln_b"][:, j * 8:(j + 1) * 8]
                last = (l == DEPTH - 1 and j == 2)
                P.ts(DVE, dv[:, l, j, 0, :], lg, 1.0 if last else ALPHA, ALU.mult)
                P.ts(DVE, dv[:, l, j, 1, :], lb, 1.0 if last else ALPHA, ALU.mult)
                if last:
                    continue
                if first and l == 0 and j == 2 and not ada1_ready:
                    continue
                l2, j2 = (l, j + 1) if j < 2 else (l + 1, 0)
                ad2 = self.ada[l2]
                sh = ad2[:, g, (3 * j2) * 8:(3 * j2) * 8 + 8]
                sc = ad2[:, g, (3 * j2 + 1) * 8:(3 * j2 + 1) * 8 + 8]
                P.ts(DVE, tmp, sc, 1.0, ALU.add)
                P.tt(DVE, dv[:, l, j, 3, :], lg, tmp, ALU.mult)
                P.tt(DVE, dv[:, l, j, 4, :], lb, tmp, ALU.mult)
                P.tt(DVE, dv[:, l, j, 4, :], dv[:, l, j, 4, :], sh, ALU.add)
        if first:
            ad = self.ada[0]
            P.ts(DVE, self.first[:, 0, :], ad[:, g, 8:16], 1.0, ALU.add)
            P.copy(DVE, self.first[:, 1, :], ad[:, g, 0:8])
        P.flush()

    def ln_tiles(self, l, j):
        P = self.P
        dv = self.dvec
        last = (l == DEPTH - 1 and j == 2)
        with ExitStack() as ls:
            sq = [P.sb("lnsq%d" % i, [128, 512], F32, stack=ls) for i in range(2)]
            st = [P.sb("lnst%d" % i, [128, 512], F32, stack=ls) for i in range(4)]
            yt = [P.sb("lnyt%d" % i, [128, D], F32, stack=ls) for i in range(2)] if last else None
            for n in range(2):
                xa = self.xa[n]
                s1, s2 = self.bank(), self.bank()
                for c in range(8):
                    P.matmul(s1, self.ones, xa[:, c, :], start=(c == 0), stop=(c == 7))
                for c in range(8):
                    q = sq[c % 2]
                    P.act(q, xa[:, c, :], AF.Square)
                    P.matmul(s2, self.ones, q, start=(c == 0), stop=(c == 7))
                mean, m2, var, nmr = st
                P.op(ACT, (lambda o_=mean.ap, i_=s1.ap: lambda e: e.mul(o_, i_, 1.0 / D))(), reads=[s1], writes=[mean])
                P.tt(DVE, m2, mean, mean, ALU.mult)
                P.stt(DVE, var, s2, 1.0 / D, m2, ALU.mult, ALU.subtract)
                P.act(var, var, AF.Sqrt, bias=LN_EPS)
                P.op(DVE, (lambda a_=var.ap: lambda e: e.reciprocal(a_, a_))(), reads=[var], writes=[var])
                rstd = var
                P.stt(DVE, nmr, mean, -1.0, rstd, ALU.mult, ALU.mult)
                for c in range(8):
                    xc = xa[:, c, :]
                    P.tt(DVE, xc, xc, rstd, ALU.mult)
                    P.tt(DVE, xc, xc, nmr, ALU.add)
                    if not last:
                        P.act(self.h[n][:, c, :], xc, AF.Identity, scale=dv[:, l, j, 3, c:c + 1], bias=dv[:, l, j, 4, c:c + 1])
                    P.ts(DVE, xc, xc, dv[:, l, j, 0, c:c + 1], ALU.mult, dv[:, l, j, 1, c:c + 1], ALU.add)
                if last:
                    for tb in range(4):
                        y = yt[tb % 2]
                        for half in range(2):
                            b = self.bank()
                            for jj in range(4):
                                c = half * 4 + jj
                                P.transpose(b[:, jj * 128:(jj + 1) * 128], xa[:, c, tb * 128:(tb + 1) * 128], self.ident)
                            P.copy(ACT if half == 0 else DVE, y[:, half * 512:(half + 1) * 512], b)
                        r0 = n * 512 + tb * 128
                        P.dma(SP, self.y_dram[r0:r0 + 128, :], y)
            P.barrier()

    def ffn(self, l, jj, j):
        P = self.P
        w1 = self.dram["ffn_w1"][l, jj]
        w2 = self.dram["ffn_w2"][l, jj]
        gcol = self.dvec[:, l, j, 2, :]
        with ExitStack() as fs:
            hid = [P.sb("hid%d" % n, [128, 22, 512], BF16, stack=fs) for n in range(2)]
            sgt = [P.sb("sgt%d" % i, [128, 512], BF16, stack=fs) for i in range(2)]
            it = 0
            side = self.ada1_gen
            if side is not None:
                self.bank_pool = [0, 1, 2, 3, 4, 5, 6]

            def side_step(k):
                for _ in range(k):
                    if self.ada1_gen is None:
                        return
                    try:
                        next(self.ada1_gen)
                    except StopIteration:
                        self.ada1_gen = None
            for hb in range(6):
                side_step(2)
                nm = 4 if hb < 5 else 2
                wg = self.wtile(w1, 0, 8, hb * 512, nm * 128)
                wu = self.wtile(w1, 0, 8, DFF + hb * 512, nm * 128)
                for n in range(2):
                    for m in range(nm):
                        bg, bu = self.bank(), self.bank()
                        for kc in range(8):
                            P.matmul(bg, wg[:, kc, m * 128:(m + 1) * 128], self.h[n][:, kc, :], start=(kc == 0), stop=(kc == 7))
                        for kc in range(8):
                            P.matmul(bu, wu[:, kc, m * 128:(m + 1) * 128], self.h[n][:, kc, :], start=(kc == 0), stop=(kc == 7))
                        s = sgt[it % 2]
                        it += 1
                        P.act(s, bg, AF.Silu)
                        P.tt(DVE, hid[n][:, hb * 4 + m, :], bu, s, ALU.mult)
            for cb in range(4):
                side_step(2)
                wA = self.wtile(w2, 0, 11, cb * 256, 256)
                wB = self.wtile(w2, 11, 11, cb * 256, 256)
                for n in range(2):
                    for m in range(2):
                        c = cb * 2 + m
                        by = self.bank()
                        for kc in range(22):
                            w = wA if kc < 11 else wB
                            P.matmul(by, w[:, kc % 11, m * 128:(m + 1) * 128], hid[n][:, kc, :], start=(kc == 0), stop=(kc == 21))
                        P.stt(DVE, self.xa[n][:, c, :], by, gcol[:, c:c + 1], self.xa[n][:, c, :], ALU.mult, ALU.add)
            if side is not None:
                side_step(99)
                self.bank_pool = None
                self.derive_vectors(self.g, first=False)
            self.ln_tiles(l, j)

    def conv3(self, o, p, wcols, row, nmul):
        P = self.P
        SL = self.segL
        w0, w1, w2 = (wcols[:, j * nmul + row:j * nmul + row + 1] for j in range(3))
        P.act(o, p, AF.Identity, scale=w1)
        o3 = V(o.t, o.ap.rearrange("p (s l) -> p s l", l=SL))
        p3 = V(p.t, p.ap.rearrange("p (s l) -> p s l", l=SL))
        P.stt(DVE, o3[:, :, 1:SL], p3[:, :, 0:SL - 1], w0, o3[:, :, 1:SL], ALU.mult, ALU.add)
        P.stt(DVE, o3[:, :, 0:SL - 1], p3[:, :, 1:SL], w2, o3[:, :, 0:SL - 1], ALU.mult, ALU.add)

    def proj(self, b, wv, col0, n, M=128):
        for kc in range(8):
            self.P.matmul(b[0:M, :], wv[:, kc, col0:col0 + M], self.h[n][:, kc, :], start=(kc == 0), stop=(kc == 7))

    def merge_branch(self, l, bi, br_w, src):
        P = self.P
        w_in = self.dram["w_in"][l]
        wbr = self.wtile(br_w, 0, 4, 0, 1024)
        with ExitStack() as s_:
            sig = [P.sb("sig%d" % i, [128, 512], F32, stack=s_) for i in range(2)]
            tmp = [P.sb("mtmp%d" % i, [128, 512], F32, stack=s_) for i in range(2)]
            it = 0
            for cb in range(2):
                wg = self.wtile(w_in, 0, 8, C_MG + bi * 1024 + cb * 512, 512)
                for n in range(2):
                    for m in range(4):
                        c = cb * 4 + m
                        bb, bg = self.bank(), self.bank()
                        for kc in range(4):
                            P.matmul(bb, wbr[:, kc, c * 128:(c + 1) * 128], src(kc, n), start=(kc == 0), stop=(kc == 3))
                        self.proj(bg, wg, m * 128, n)
                        sg = sig[it % 2]
                        P.act(sg, bg, AF.Sigmoid)
                        if bi == 0:
                            P.tt(DVE, self.merged[n][:, c, :], bb, sg, ALU.mult)
                        else:
                            t = tmp[it % 2]
                            P.tt(DVE, t, bb, sg, ALU.mult)
                            P.tt(DVE, self.merged[n][:, c, :], self.merged[n][:, c, :], t, ALU.add)
                        it += 1
            P.barrier()

    def mixer(self, l):
        P = self.P
        with ExitStack() as ms:
            self.merged = [P.sb("merged%d" % n, [128, 8, 512], F32, stack=ms) for n in range(2)]
            skip = getattr(self, "skip", ())
            if "a" not in skip:
                self.branch_a(l)
            if "d" not in skip:
                self.branch_d(l)
            if "g" not in skip:
                self.branch_g(l)
            for n in range(2):
                for c in range(8):
                    P.copy(ACT if c % 2 else DVE, self.h[n][:, c, :], self.merged[n][:, c, :])
            w_o = self.dram["w_o"][l]
            gcol = self.dvec[:, l, 1, 2, :]
            for cb in range(2):
                wo = self.wtile(w_o, 0, 8, cb * 512, 512)
                for n in range(2):
                    for m in range(4):
                        c = cb * 4 + m
                        by = self.bank()
                        self.proj(by, wo, m * 128, n)
                        P.stt(DVE, self.xa[n][:, c, :], by, gcol[:, c:c + 1], self.xa[n][:, c, :], ALU.mult, ALU.add)
            self.ln_tiles(l, 1)

    def branch_a(self, l):
        P = self.P
        w_in = self.dram["w_in"][l]
        cw = self.vec[l]["conv_a"]
        with ExitStack() as s_:
            ya = [P.sb("ya%d" % n, [128, 4, 512], BF16, stack=s_) for n in range(2)]
            s2_ = ExitStack()
            axs = [P.sb("axs%d" % i, [128, 512], F32, stack=s2_) for i in range(2)]
            pp = [P.sb("app%d" % i, [128, 512], F32, stack=s2_) for i in range(2)]
            oo = [P.sb("aoo%d" % i, [128, 512], F32, stack=s2_) for i in range(2)]
            wx = self.wtile(w_in, 0, 8, C_AX, 512)
            wb = self.wtile(w_in, 0, 8, C_AB, 512)
            wc = self.wtile(w_in, 0, 8, C_AC, 512)
            it = 0
            for n in range(2):
                for m in range(4):
                    bx, bc, bb = self.bank(), self.bank(), self.bank()
                    self.proj(bx, wx, m * 128, n)
                    self.proj(bc, wc, m * 128, n)
                    self.proj(bb, wb, m * 128, n)
                    a_, p_, o_ = axs[it % 2], pp[it % 2], oo[it % 2]
                    it += 1
                    P.copy(ACT, a_, bx)
                    P.tt(DVE, p_, bc, a_, ALU.mult)
                    self.conv3(o_[:, :], p_[:, :], cw, m, 4)
                    P.tt(DVE, ya[n][:, m, :], bb, o_, ALU.mult)
            P.barrier()
            s2_.close()
            self.merge_branch(l, 0, self.dram["w_br_a"][l], lambda kc, n: ya[n][:, kc, :])

    def blkv(self, tiles, blk, lead=None):
        n, b0 = blk // 4, (blk % 4) * 128
        if lead is None:
            return tiles[n][:, b0:b0 + 128]
        return tiles[n][:, lead, b0:b0 + 128]

    def branch_d(self, l):
        P = self.P
        g = self.g
        w_in = self.dram["w_in"][l]
        v = self.vec[l]
        cw = v["conv_qkv"]
        NBAT = 4
        with ExitStack() as s_:
            od = [P.sb("od%d" % n, [128, 4, 512], BF16, stack=s_) for n in range(2)]
            s2_ = ExitStack()
            sb = lambda name, shp, dt=F32: P.sb(name, shp, dt, stack=s2_)
            betaT = sb("betaT", [128, 8, 8]); gT = sb("gT", [128, 8, 8]); Gc = sb("Gc", [128, 8, 8])
            eGL = sb("eGL", [128, 8, 8]); bexpG = sb("bexpG", [128, 8, 8]); eGrev = sb("eGrev", [128, 8, 8])
            t8 = sb("t8", [128, 8, 8])
            wsm = self.wtile(w_in, 0, 8, C_DBETA, 16)
            bsm = self.bank()
            for blk in range(8):
                for kc in range(8):
                    P.matmul(bsm[:, blk * 16:(blk + 1) * 16], self.blkv(self.h, blk, kc), wsm[:, kc, :],
                             start=(kc == 0), stop=(kc == 7))
            bview = V(bsm, bsm.ap[:, 0:128].rearrange("p (b c) -> p b c", c=16))
            P.act(betaT, bview[:, :, 0:8], AF.Sigmoid)
            bc8 = lambda t: V(t, bass.AP(t.ap.tensor, t.ap.offset, [list(t.ap.ap[0]), [0, 8], [1, 8]]))
            P.tt(DVE, t8, bview[:, :, 8:16], bc8(v["dtb"]), ALU.add)
            P.act(t8, t8, AF.Exp)
            P.act(t8, t8, AF.Ln, bias=1.0)
            P.tt(DVE, gT, t8, bc8(v["nega"]), ALU.mult)
            bG = self.bank()
            gTd = [sb("gTd%d" % d, [128, 32]) for d in range(2)]
            for d in range(2):
                P.copy(DVE, V(gTd[d], gTd[d].ap.rearrange("p (b c) -> p b c", c=4)), gT[:, :, d * 4:d * 4 + 4])
                P.matmul(bG[:, d * 32:(d + 1) * 32], self.triI[d], gTd[d])
            for d in range(2):
                P.copy(DVE, Gc[:, :, d * 4:d * 4 + 4], V(bG, bG.ap[:, d * 32:(d + 1) * 32].rearrange("p (b c) -> p b c", c=4)))
            bL = self.bank()
            bLv = V(bL, bL.ap[:, 0:64].rearrange("p (b c) -> p b c", c=8))
            P.matmul(bL[:, 0:64], self.ones, V(gT, gT.ap.rearrange("p b c -> p (b c)")))
            P.act(eGL, bLv, AF.Exp)
            P.tt(DVE, t8, bLv, Gc, ALU.subtract)
            P.act(eGrev, t8, AF.Exp)
            P.act(t8, Gc, AF.Exp)
            P.tt(DVE, bexpG, betaT, t8, ALU.mult)
            qT = [sb("qT%d" % n, [128, 512], BF16) for n in range(2)]
            kT = [sb("kT%d" % n, [128, 512], BF16) for n in range(2)]
            zs = [sb("zs%d" % n, [128, 512], BF16) for n in range(2)]
            wqs = {}
            ktok = sb("ktok", [128, 8, 128], BF16); vtok = sb("vtok", [128, 8, 128], BF16)
            oacc = [sb("oacc%d" % n, [128, 512]) for n in range(2)]
            pre = sb("dpre", [128, 512]); cvo = sb("dcvo", [128, 512]); rr = sb("drr", [128, 512])
            vT = [V(rr, rr.ap.bitcast(BF16)[:, n * 512:(n + 1) * 512]) for n in range(2)]
            gU = [sb("gU%d" % i, [128, 128], BF16) for i in range(NBAT)]
            AB = [[sb("AB%d_%d" % (i, k), [128, 384]) for k in range(2)] for i in range(NBAT)]
            MN = [sb("MN%d" % i, [128, 256]) for i in range(NBAT)]
            CC = [sb("CC%d" % i, [128, 256]) for i in range(NBAT)]
            CCb = [V(CC[i], CC[i].ap.bitcast(BF16)[:, 0:256]) for i in range(NBAT)]
            WWb = [V(CC[i], CC[i].ap.bitcast(BF16)[:, 256:512]) for i in range(NBAT)]
            YY = [[V(AB[i][k], AB[i][k].ap.bitcast(BF16)[:, 0:256]) for k in range(2)] for i in range(NBAT)]
            Tt = [sb("Tt%d" % i, [128, 128], BF16) for i in range(NBAT)]
            vbk = [sb("vbk%d" % i, [128, 256], BF16) for i in range(2)]
            mk2 = lambda nm, dt=BF16, w=128: [[sb("%s%d_%d" % (nm, d, i), [128, w], dt) for i in range(8)] for d in range(2)]
            qdec, kdec, atT = mk2("qdec"), mk2("kdec"), mk2("atT")
            uw = mk2("uw", BF16, 256)
            usb = [[uw[d][i][:, 0:128] for i in range(8)] for d in range(2)]
            wTs = [[uw[d][i][:, 128:256] for i in range(8)] for d in range(2)]
            vnw = [sb("vnw%d" % i, [128, 128], BF16) for i in range(2)]
            nseq = len(self.seqs)
            S = [[sb("S%d_%d" % (q, d), [128, 128]) for d in range(2)] for q in range(nseq)]
            Sb = [[sb("Sb%d_%d" % (q, d), [128, 128], BF16) for d in range(2)] for q in range(nseq)]
            vi = [0]

            def proj_gen(hh):
                wq = self.wtile(w_in, 0, 8, [(C_DQ + hh * 128, 128), (C_DK + hh * 128, 128),
                                             (C_DV + hh * 128, 128), (C_DZ + hh * 128, 128)])
                wqs[hh] = wq
                for wi, dstT in ((0, qT), (1, kT), (2, vT)):
                    for n in range(2):
                        b = self.bank()
                        self.proj(b, wq, wi * 128, n)
                        P.copy(ACT, pre, b)
                        self.conv3(cvo[:, :], pre[:, :], cw, wi * 4 + hh, 12)
                        if wi == 2:
                            P.act(dstT[n], cvo, AF.Silu)
                        else:
                            P.act(cvo, cvo, AF.Silu)
                            P.act(pre, cvo, AF.Square)
                            bs_ = self.bank()
                            P.matmul(bs_, self.ones, pre)
                            P.act(rr, bs_, AF.Sqrt, bias=RMS_EPS)
                            P.op(DVE, (lambda a_=rr.ap: lambda e: e.reciprocal(a_, a_))(), reads=[rr], writes=[rr])
                            if wi == 0:
                                P.stt(DVE, dstT[n], cvo, 128.0 ** -0.5, rr, ALU.mult, ALU.mult)
                            else:
                                P.tt(DVE, dstT[n], cvo, rr, ALU.mult)
                        yield
                for blk in range(8):
                    b = self.bank()
                    bb16 = V(b, b.ap.bitcast(BF16))
                    P.transpose(bb16[:, 0:128], self.blkv(kT, blk), self.identb)
                    P.transpose(bb16[:, 128:256], self.blkv(vT, blk), self.identb)
                    P.copy(ACT, ktok[:, blk, :], bb16[:, 0:128])
                    P.copy(DVE, vtok[:, blk, :], bb16[:, 128:256])
                    if blk % 2:
                        yield

            def bc2(vw):
                a = _ap(vw)
                return V(vw.t, bass.AP(a.tensor, a.offset, [list(a.ap[0]), [0, 2], list(a.ap[1])]))

            def seg2(vw):
                a = _ap(vw)
                return V(vw.t, bass.AP(a.tensor, a.offset, [list(a.ap[0]), [256, 2], [1, 128]]))

            def h2(vw):
                a = _ap(vw)
                return V(vw.t, a.rearrange("p (a b) -> p a b", a=2))

            def prescan_gen(hh, d, batches):
                idx = d * 4 + hh
                for blks in batches:
                    col = lambda t, blk: t[:, blk, idx:idx + 1]
                    st = lambda blk: blk % NBAT
                    pg = {blk: self.banks[blk % NBAT] for blk in blks}
                    for blk in blks:
                        r2 = AB[st(blk)][1][:, 0:128]
                        P.ts(DVE, r2, self.triI[d], col(gT, blk), ALU.mult)
                        P.matmul(pg[blk][:, 0:128], self.ones, r2)
                        P.matmul(pg[blk][:, 128:256], self.blkv(kT, blk), self.blkv(kT, blk))
                    yield
                    for blk in blks:
                        P.stt(DVE, h2(CC[st(blk)][:, 0:256]), bc2(pg[blk][:, 0:128]), col(Gc, blk), h2(self.NP[d][:, 0:256]),
                              ALU.subtract, ALU.add)
                        P.act(AB[st(blk)][0][:, 0:128], pg[blk][:, 0:128], AF.Exp)
                    yield
                    for blk in blks:
                        P.act(gU[st(blk)], CC[st(blk)][:, 0:128], AF.Exp)
                        P.act(CC[st(blk)][:, 128:256], CC[st(blk)][:, 128:256], AF.Exp, scale=-1.0)
                        P.tt(DVE, qdec[d][blk], self.blkv(qT, blk), AB[st(blk)][0][:, 0:128], ALU.mult)
                        P.stt(DVE, MN[st(blk)][:, 0:128], pg[blk][:, 128:256], col(betaT, blk), CC[st(blk)][:, 128:256],
                              ALU.mult, ALU.mult)
                    yield
                    pb = {blk: self.banks[blk % NBAT] for blk in blks}
                    for blk in blks:
                        P.transpose(pb[blk][:, 0:128], MN[st(blk)][:, 0:128], self.ident)
                    for blk in blks:
                        P.copy(DVE, MN[st(blk)][:, 128:256], pb[blk][:, 0:128])
                    yield
                    for blk in blks:
                        P.tt(DVE, h2(CC[st(blk)][:, 0:256]), h2(MN[st(blk)][:, 0:256]), bc2(self.bd16[:, :]), ALU.mult)
                    for blk in blks:
                        M0, N0 = CC[st(blk)][:, 0:128], CC[st(blk)][:, 128:256]
                        P.tt(DVE, AB[st(blk)][1][:, 128:256], self.ident, N0, ALU.subtract)
                        P.matmul(pb[blk][:, 0:128], M0, N0)
                        P.matmul(pb[blk][:, 256:384], N0, M0)
                    yield
                    for blk in blks:
                        P.copy(ACT, seg2(AB[st(blk)][1][:, 0:384]), seg2(pb[blk][:, 0:384]))
                    for k in (1, 2, 3):
                        cur, nxt = k % 2, (k + 1) % 2
                        for blk in blks:
                            A_ = AB[st(blk)][cur]
                            P.matmul(pb[blk][:, 0:256], A_[:, 256:384], A_[:, 0:256])
                            P.matmul(pb[blk][:, 256:384], A_[:, 0:128], A_[:, 256:384])
                        yield
                        for blk in blks:
                            P.copy(ACT, seg2(AB[st(blk)][nxt][:, 0:384]), seg2(pb[blk][:, 0:384]))
                            P.tt(DVE, AB[st(blk)][nxt][:, 128:256], AB[st(blk)][cur][:, 128:256], pb[blk][:, 128:256], ALU.add)
                    for blk in blks:
                        P.matmul(pb[blk][:, 0:128], AB[st(blk)][0][:, 256:384], AB[st(blk)][0][:, 128:256])
                    yield
                    for blk in blks:
                        pbb = V(pb[blk], pb[blk].ap.bitcast(BF16))
                        P.tt(DVE, YY[st(blk)][1][:, 128:256], AB[st(blk)][0][:, 128:256], pb[blk][:, 0:128], ALU.add)
                        P.transpose(pbb[:, 256:384], YY[st(blk)][1][:, 128:256], self.identb)
                    for blk in blks:
                        pbb = V(pb[blk], pb[blk].ap.bitcast(BF16))
                        P.copy(ACT, YY[st(blk)][1][:, 0:128], pbb[:, 256:384])
                    yield
                    for li in (1, 2):
                        mk = self.mk[li]
                        last = (li == 2)
                        for blk in blks:
                            P.tt(DVE, h2(CCb[st(blk)][:, 0:256]), h2(MN[st(blk)][:, 0:256]), bc2(mk[:, :]), ALU.mult)
                        for blk in blks:
                            cur = YY[st(blk)][li % 2]
                            Y, Yt = cur[:, 0:128], cur[:, 128:256]
                            C, Ct = CCb[st(blk)][:, 0:128], CCb[st(blk)][:, 128:256]
                            if not last:
                                P.matmul(pb[blk][:, 0:128], Ct, Y)
                            P.matmul(pb[blk][:, 128:256], C, Yt)
                        yield
                        for blk in blks:
                            if not last:
                                P.copy(ACT, WWb[st(blk)][:, 0:256], pb[blk][:, 0:256])
                            else:
                                P.copy(ACT, WWb[st(blk)][:, 128:256], pb[blk][:, 128:256])
                        for blk in blks:
                            cur = YY[st(blk)][li % 2]
                            Y, Yt = cur[:, 0:128], cur[:, 128:256]
                            if not last:
                                P.matmul(pb[blk][:, 256:384], Yt, WWb[st(blk)][:, 0:128])
                            P.matmul(pb[blk][:, 384:512], Y, WWb[st(blk)][:, 128:256])
                        yield
                        for blk in blks:
                            cur, nxt = YY[st(blk)][li % 2], YY[st(blk)][(li + 1) % 2]
                            if not last:
                                P.tt(DVE, nxt[:, 0:256], cur[:, 0:256], pb[blk][:, 256:512], ALU.subtract)
                            else:
                                P.tt(DVE, Tt[st(blk)], cur[:, 128:256], pb[blk][:, 384:512], ALU.subtract)
                    pu = {blk: self.banks[blk % NBAT] for blk in blks}
                    for blk in blks:
                        vk = vbk[blk % 2]
                        P.act(vk[:, 0:128], vtok[:, blk, :], AF.Copy, scale=col(betaT, blk))
                        P.act(vk[:, 128:256], ktok[:, blk, :], AF.Copy, scale=col(bexpG, blk))
                        P.act(kdec[d][blk], ktok[:, blk, :], AF.Copy, scale=col(eGrev, blk))
                        P.matmul(pu[blk][:, 0:128], Tt[st(blk)], vk[:, 0:128])
                        P.matmul(pu[blk][:, 128:256], vk[:, 128:256], Tt[st(blk)])
                        P.matmul(pu[blk][:, 256:384], self.blkv(kT, blk), self.blkv(qT, blk))
                    yield
                    for blk in blks:
                        P.copy(ACT, uw[d][blk], pu[blk][:, 0:256])
                        P.tt(DVE, atT[d][blk], pu[blk][:, 256:384], gU[st(blk)], ALU.mult)
                    yield

            def scan_gen(hh, d):
                idx = d * 4 + hh
                for q, (b0, nb) in enumerate(self.seqs):
                    if g == 0:
                        P.memset(DVE, S[q][d], 0.0)
                        P.memset(DVE, Sb[q][d], 0.0)
                    else:
                        P.dma(SP, S[q][d], self.dram["sd"][l, d, hh])
                        P.copy(ACT, Sb[q][d], S[q][d])
                nb = self.seqs[0][1]
                for step in range(nb):
                    for q, (b0, _) in enumerate(self.seqs):
                        blk = b0 + (step if d == 0 else nb - 1 - step)
                        St, Sbt = S[q][d], Sb[q][d]
                        p1 = self.banks[4 + vi[0] % 2]
                        P.matmul(p1[:, 0:128], wTs[d][blk], Sbt)
                        vn = vnw[vi[0] % 2]
                        vi[0] += 1
                        P.tt(DVE, vn, usb[d][blk], p1[:, 0:128], ALU.subtract)
                        yield
                        P.matmul(p1[:, 128:256], Sbt, qdec[d][blk], start=True, stop=False)
                        P.matmul(p1[:, 128:256], vn, atT[d][blk], start=False, stop=True)
                        P.matmul(p1[:, 256:384], kdec[d][blk], vn)
                        ov = self.blkv(oacc, blk)
                        if d == 0:
                            P.copy(ACT, ov, p1[:, 128:256])
                        else:
                            P.tt(DVE, ov, ov, p1[:, 128:256], ALU.add)
                        P.stt(DVE, St, St, eGL[:, blk, idx:idx + 1], p1[:, 256:384], ALU.mult, ALU.add)
                        P.copy(ACT, Sbt, St)
                        yield
                if g == 0:
                    for q in range(nseq):
                        P.dma(SP, self.outs["nsd"][q, l, d, hh], S[q][d])

            def norm(hh):
                for n in range(2):
                    b = self.bank()
                    self.proj(b, wqs[hh], 3 * 128, n)
                    P.act(zs[n], b, AF.Silu)
                    P.act(pre, oacc[n], AF.Square)
                    bs_ = self.bank()
                    P.matmul(bs_, self.ones, pre)
                    P.act(rr, bs_, AF.Sqrt, bias=RMS_EPS, scale=1.0 / 128.0)
                    P.op(DVE, (lambda a_=rr.ap: lambda e: e.reciprocal(a_, a_))(), reads=[rr], writes=[rr])
                    P.tt(DVE, rr, oacc[n], rr, ALU.mult)
                    P.stt(DVE, od[n][:, hh, :], rr, v["dng"][:, 0:1], zs[n], ALU.mult, ALU.mult)

            def prescan2(hh, d, offset=1):
                ga = prescan_gen(hh, d, [[0, 1], [4, 5]])
                gb = prescan_gen(hh, d, [[2, 3], [6, 7]])
                rnd = 0
                while ga is not None or gb is not None:
                    if ga is not None:
                        try:
                            next(ga)
                        except StopIteration:
                            ga = None
                    if gb is not None and rnd >= offset:
                        try:
                            next(gb)
                        except StopIteration:
                            gb = None
                    rnd += 1
                    yield

            def run(main, side=None, ratio=2):
                cnt = 0
                for _ in main:
                    cnt += 1
                    if side is not None and cnt % ratio == 0:
                        try:
                            next(side)
                        except StopIteration:
                            side = None
                if side is not None:
                    for _ in side:
                        pass

            self.bank_pool = [6, 7]
            run(proj_gen(0))
            for hh in range(4):
                run(prescan2(hh, 0))
                run(prescan2(hh, 1), scan_gen(hh, 0), ratio=1)
                if hh < 3:
                    run(scan_gen(hh, 1), proj_gen(hh + 1), ratio=1)
                else:
                    run(scan_gen(hh, 1))
                norm(hh)
            self.bank_pool = None
            P.barrier()
            s2_.close()
            self.merge_branch(l, 1, self.dram["w_br_d"][l], lambda kc, n: od[n][:, kc, :])

    def branch_g(self, l):
        P = self.P
        g = self.g
        w_in = self.dram["w_in"][l]
        v = self.vec[l]
        NB4 = 4
        with ExitStack() as s_:
            og = [P.sb("og%d" % n, [128, 4, 512], BF16, stack=s_) for n in range(2)]
            s2_ = ExitStack()
            sb = lambda name, shp, dt=F32: P.sb(name, shp, dt, stack=s2_)
            gqT = [[sb("gqT%d_%d" % (hp, n), [128, 512], BF16) for n in range(2)] for hp in range(2)]
            gkT = [[sb("gkT%d_%d" % (hp, n), [128, 512], BF16) for n in range(2)] for hp in range(2)]
            gktok = sb("gktok", [128, 8, 256], BF16)
            gvtok = sb("gvtok", [128, 8, 512], BF16)
            lra = [sb("lra%d" % s, [17, 1024], BF16) for s in range(2)]
            ogacc = [[sb("ogacc%d_%d" % (a, n), [128, 512]) for n in range(2)] for a in range(2)]
            sp = sb("gsp", [128, 8, 128])
            kd = [sb("gkd%d" % i, [128, 8, 128], BF16) for i in range(2)]
            ek = [sb("gek_%d" % i, [128, 128], BF16) for i in range(NB4)]
            ebuf = sb("gebuf", [128, NB4, 128])
            eB = [ebuf[:, i, :] for i in range(NB4)]
            enB = [sb("genB_%d" % i, [128, 128], BF16) for i in range(NB4)]
            eBl = [sb("geBl%d" % i, [128, 8]) for i in range(2)]
            qe = [[[sb("gqe%d_%d_%d" % (pp, a, i), [128, 128], BF16) for i in range(8)] for a in range(2)] for pp in range(2)]
            qef = [sb("gqef%d" % i, [128, 128], BF16) for i in range(NB4)]
            rm = sb("grm", [128, 2])
            P.memset(DVE, rm, 0.0)
            P.memset(DVE, rm[0:64, 0:1], 1.0)
            P.memset(DVE, rm[64:128, 1:2], 1.0)
            ke = [sb("gke%d" % i, [128, 128], BF16) for i in range(NB4)]
            atT = [[[sb("gat%d_%d_%d" % (pp, a, i), [128, 128], BF16) for i in range(8)] for a in range(2)] for pp in range(2)]
            nseq = len(self.seqs)
            S = [[sb("gS%d_%d" % (q, pp), [128, 128]) for pp in range(2)] for q in range(nseq)]
            Sb = [[sb("gSb%d_%d" % (q, pp), [128, 128], BF16) for pp in range(2)] for q in range(nseq)]
            ebf = V(ebuf, ebuf.ap.rearrange("p b c -> p (b c)").bitcast(BF16))
            grs = [ebf[:, n * 512:(n + 1) * 512] for n in range(2)]
            spf = V(sp, sp.ap.rearrange("p b c -> p (b c)"))
            sq, rr = spf[:, 0:512], spf[:, 512:1024]

            wqk = self.wtile(w_in, 0, 8, C_GQ, 512)
            for hp in range(2):
                for n in range(2):
                    b = self.bank()
                    self.proj(b, wqk, hp * 128, n)
                    P.ts(DVE, gqT[hp][n], b, 0.125, ALU.mult)
                    b2 = self.bank()
                    self.proj(b2, wqk, 256 + hp * 128, n)
                    P.copy(ACT, gkT[hp][n], b2)
            for blk in range(8):
                b = self.bank()
                for kc in range(8):
                    P.matmul(b[:, 0:256], self.blkv(self.h, blk, kc), wqk[:, kc, 256:512], start=(kc == 0), stop=(kc == 7))
                P.copy(ACT, gktok[:, blk, :], b[:, 0:256])
            wv = self.wtile(w_in, 0, 8, C_GV, 512)
            for blk in range(8):
                b = self.bank()
                for kc in range(8):
                    P.matmul(b, self.blkv(self.h, blk, kc), wv[:, kc, :], start=(kc == 0), stop=(kc == 7))
                P.copy(DVE if blk % 2 else ACT, gvtok[:, blk, :], b)
            wl = self.wtile(w_in, 0, 8, C_GLR, 32)
            for s in range(2):
                P.memset(DVE, lra[s], 1.0)
                for n in range(2):
                    b = self.bank()
                    self.proj(b, wl, s * 16, n, M=16)
                    P.copy(ACT, lra[s][0:16, n * 512:(n + 1) * 512], b[0:16, :])
            wr = self.wtile(w_in, 0, 8, C_GR, 512)

            def prescan_half(hp, s, pp, batches):
                w2b = v["w2b"][s]
                for blks in batches:
                    st = lambda blk: blk % NB4
                    bl = {blk: self.banks[blk % NB4] for blk in blks}
                    for blk in blks:
                        P.matmul(bl[blk][:, 128:256], sp[:, blk, :], self.gtriI[s])
                        P.matmul(bl[blk][:, 256:384], self.gtriX[s], sp[:, blk, :])
                    yield
                    lc = 127 if s == 0 else 0
                    for blk in blks:
                        P.act(eB[st(blk)], bl[blk][:, 128:256], AF.Exp)
                        P.act(enB[st(blk)], bl[blk][:, 128:256], AF.Exp, scale=-1.0)
                        P.act(ek[st(blk)], bl[blk][:, 256:384], AF.Exp)
                    yield
                    for blk in blks:
                        eb, enb = eB[st(blk)], enB[st(blk)]
                        P.copy(DVE, eBl[pp][:, blk:blk + 1], eb[:, lc:lc + 1])
                        P.tt(DVE, qef[st(blk)], self.blkv(gqT[hp], blk), eb, ALU.mult)
                        for a in range(2):
                            P.ts(DVE, qe[pp][a][blk], qef[st(blk)], rm[:, a:a + 1], ALU.mult)
                        P.tt(DVE, ke[st(blk)], self.blkv(gkT[hp], blk), enb, ALU.mult)
                        P.tt(DVE, kd[pp][:, blk, :], gktok[:, blk, hp * 128:(hp + 1) * 128], ek[st(blk)], ALU.mult)
                    yield
                    pa = {blk: self.banks[blk % NB4] for blk in blks}
                    for blk in blks:
                        for a in range(2):
                            P.matmul(pa[blk][:, a * 128:(a + 1) * 128], ke[st(blk)], qe[pp][a][blk])
                    yield
                    for blk in blks:
                        for a in range(2):
                            P.tt(DVE, atT[pp][a][blk], pa[blk][:, a * 128:(a + 1) * 128], self.m01[s], ALU.mult)
                    yield

            sci = [0]

            def prescan_gen(hp, s, pp, offset=1):
                w2b_ = v["w2b"][s]
                for blk in range(8):
                    P.matmul(self.banks[blk // 4][:, (blk % 4) * 128:(blk % 4) * 128 + 128],
                             lra[s][0:17, blk * 128:(blk + 1) * 128], w2b_[0:17, hp * 128:(hp + 1) * 128])
                sph = [V(sp, sp.ap[:, 4 * hf:4 * hf + 4, :].rearrange("p b c -> p (b c)")) for hf in range(2)]
                for hf in range(2):
                    P.act(sph[hf], self.banks[hf][:, 0:512], AF.Exp, scale=-1.0)
                for hf in range(2):
                    P.act(sph[hf], sph[hf], AF.Ln, bias=1.0)
                yield
                ga = prescan_half(hp, s, pp, [[0, 1], [4, 5]])
                gb = prescan_half(hp, s, pp, [[2, 3], [6, 7]])
                rnd = 0
                while ga is not None or gb is not None:
                    if ga is not None:
                        try:
                            next(ga)
                        except StopIteration:
                            ga = None
                    if gb is not None and rnd >= offset:
                        try:
                            next(gb)
                        except StopIteration:
                            gb = None
                    rnd += 1
                    yield

            def scan_gen(hp, s, pp):
                for q in range(nseq):
                    if g == 0:
                        P.memset(DVE, S[q][pp], 0.0)
                        P.memset(DVE, Sb[q][pp], 0.0)
                    else:
                        P.dma(SP, S[q][pp], self.dram["sg"][l, s, hp * 2:hp * 2 + 2].rearrange("h k v -> (h k) v"))
                        P.copy(ACT, Sb[q][pp], S[q][pp])
                nb = self.seqs[0][1]
                for step in range(nb):
                    for q, (b0, _) in enumerate(self.seqs):
                        blk = b0 + (step if s == 0 else nb - 1 - step)
                        St, Sbt = S[q][pp], Sb[q][pp]
                        po = self.banks[4 + sci[0] % 2]
                        sci[0] += 1
                        for a in range(2):
                            head = hp * 2 + a
                            P.matmul(po[:, a * 128:(a + 1) * 128], Sbt, qe[pp][a][blk], start=True, stop=False)
                            P.matmul(po[:, a * 128:(a + 1) * 128], gvtok[:, blk, head * 128:(head + 1) * 128], atT[pp][a][blk],
                                     start=False, stop=True)
                        for a in range(2):
                            head = hp * 2 + a
                            P.matmul(po[:, 256 + a * 128:384 + a * 128], kd[pp][:, blk, :],
                                     gvtok[:, blk, head * 128:(head + 1) * 128])
                        yield
                        for a in range(2):
                            ov = self.blkv(ogacc[a], blk)
                            if s == 0:
                                P.copy(ACT, ov, po[:, a * 128:(a + 1) * 128])
                            else:
                                P.tt(DVE, ov, ov, po[:, a * 128:(a + 1) * 128], ALU.add)
                        for a in range(2):
                            rws = slice(a * 64, a * 64 + 64)
                            P.stt(DVE, St[rws, :], St[rws, :], eBl[pp][rws, blk:blk + 1],
                                  po[rws, 256 + a * 128:384 + a * 128], ALU.mult, ALU.add)
                        P.copy(ACT, Sbt, St)
                        yield
                if g == 0:
                    for q in range(nseq):
                        P.dma(SP, self.outs["nsg"][q, l, s, hp * 2:hp * 2 + 2].rearrange("h k v -> (h k) v"), S[q][pp])

            def norm(hp):
                for a in range(2):
                    head = hp * 2 + a
                    for n in range(2):
                        b = self.bank()
                        self.proj(b, wr, head * 128, n)
                        P.act(grs[n], b, AF.Silu)
                        P.act(sq, ogacc[a][n], AF.Square)
                        bs_ = self.bank()
                        P.matmul(bs_, self.ones, sq)
                        P.act(rr, bs_, AF.Sqrt, bias=RMS_EPS, scale=1.0 / 128.0)
                        P.op(DVE, (lambda a_=rr.ap: lambda e: e.reciprocal(a_, a_))(), reads=[sp], writes=[sp])
                        P.tt(DVE, rr, ogacc[a][n], rr, ALU.mult)
                        P.stt(DVE, og[n][:, head, :], rr, v["gng"][:, 0:1], grs[n], ALU.mult, ALU.mult)

            def run(main, side=None, ratio=2):
                cnt = 0
                for _ in main:
                    cnt += 1
                    if side is not None and cnt % ratio == 0:
                        try:
                            next(side)
                        except StopIteration:
                            side = None
                if side is not None:
                    for _ in side:
                        pass

            self.bank_pool = [6, 7]
            run(prescan_gen(0, 0, 0))
            run(prescan_gen(0, 1, 1), scan_gen(0, 0, 0), ratio=1)
            run(prescan_gen(1, 0, 0), scan_gen(0, 1, 1), ratio=1)
            norm(0)
            run(prescan_gen(1, 1, 1), scan_gen(1, 0, 0), ratio=1)
            run(scan_gen(1, 1, 1))
            norm(1)
            self.bank_pool = None
            P.barrier()
            s2_.close()
            self.merge_branch(l, 2, self.dram["w_br_g"][l], lambda kc, n: og[n][:, kc, :])


WEIGHT_KEYS = ["w_ada", "b_ada", "ln_g", "ln_b", "ffn_w1", "ffn_w2", "w_in", "conv_a", "conv_qkv",
               "delta_a_log", "delta_dt_bias", "delta_norm_g", "gla_w2", "gla_b", "gla_norm_g",
               "w_br_a", "w_br_d", "w_br_g", "w_o"]


def make_in_maps(inputs, n_cores=8):
    f = lambda a: np.ascontiguousarray(np.asarray(a, dtype=np.float32))
    shared = {k: f(inputs[k]) for k in WEIGHT_KEYS}
    maps = []
    for c in range(n_cores):
        m = dict(shared)
        m["xp"] = f(inputs["x_prompt"][4 * c:4 * c + 4]).reshape(TG, D)
        m["xs"] = f(inputs["x_sample"][c]).reshape(TG, D)
        m["sd"] = f(inputs["state_delta"][c])
        m["sg"] = f(inputs["state_gla"][c])
        m["cond"] = f(np.stack([np.asarray(inputs["c_ctx"]), np.asarray(inputs["c"])[c]], 0))
        maps.append(m)
    return maps


_NC_CACHE = {}


def kernel(**inputs):
    if "nc" not in _NC_CACHE:
        _NC_CACHE["nc"] = Builder().nc
    nc = _NC_CACHE["nc"]
    maps = make_in_maps(inputs)
    res = run_bass_kernel_spmd(nc, maps, core_ids=list(range(8)))
    r = res.results
    y_prompt = np.concatenate([r[c]["yp"].reshape(4, 256, D) for c in range(8)], 0).astype(np.float32)
    y_sample = np.stack([r[c]["ys"].reshape(1024, D) for c in range(8)], 0).astype(np.float32)
    nsd = np.concatenate([r[c]["nsd"] for c in range(8)], 0).astype(np.float32)
    nsg = np.concatenate([r[c]["nsg"] for c in range(8)], 0).astype(np.float32)
    return (y_prompt, y_sample, nsd, nsg)
```

```python
import numpy as np
from contextlib import ExitStack
import concourse.bass as bass
import concourse.mybir as mybir
from concourse.bass_utils import run_bass_kernel_spmd

F32 = mybir.dt.float32
BF16 = mybir.dt.bfloat16
AF = mybir.ActivationFunctionType
ALU = mybir.AluOpType

PE, ACT, DVE, POOL, SP = "pe", "act", "dve", "pool", "sp"
COMPUTE = (PE, ACT, DVE, POOL)
DEBUG_SRC = False
_FW_NAMES = ("_record", "op", "dma", "matmul", "transpose", "act", "tt", "ts", "stt", "copy", "memset", "proj", "conv3", "<lambda>")


class T:
    def __init__(self, prog, h, name, psum=False):
        self.prog, self.h, self.name, self.psum = prog, h, name, psum
        self.last_write = None
        self.reads = []
        self.sem = None
        self.dma_n = 0

    def __getitem__(self, idx):
        return V(self, self.h[idx])

    @property
    def ap(self):
        return self.h[:] if not isinstance(self.h, bass.AP) else self.h


class V:
    def __init__(self, t, ap):
        self.t, self.ap = t, ap

    def __getitem__(self, idx):
        return V(self.t, self.ap[idx])


def _ap(x):
    return x.ap if isinstance(x, (V, T)) else x


def _tiles(xs):
    out = []
    for x in xs:
        if isinstance(x, V):
            out.append(x.t)
        elif isinstance(x, T):
            out.append(x)
    return out


class Ins:
    __slots__ = ("id", "eng", "fn", "deps", "is_dma", "dma_ev", "needed", "tick", "flushed", "src")


class Prog:
    def __init__(self, nc):
        self.nc = nc
        self.stack = ExitStack()
        self.ins = []
        self.pending = []
        self.engs = {PE: nc.tensor, ACT: nc.scalar, DVE: nc.vector, POOL: nc.gpsimd, SP: nc.sync}
        self.sem = {}
        for e in COMPUTE:
            self.sem[e] = self.stack.enter_context(nc.semaphore("s_" + e))
        self.ticks = {e: 0 for e in COMPUTE}
        self.waited = {e: {} for e in self.engs}
        self.nsem = 0
        self.out_events = []
        self.sp_events = {}
        self.last_needed = {e: None for e in COMPUTE}

    def sb(self, name, shape, dtype, stack=None):
        self.nalloc = getattr(self, "nalloc", 0) + 1
        name = "%s_%d" % (name, self.nalloc)
        h = (stack or self.stack).enter_context(self.nc.sbuf_tensor(name, list(shape), dtype))
        return T(self, h, name)

    def ps(self, name, shape, dtype=F32, stack=None):
        h = (stack or self.stack).enter_context(self.nc.psum_tensor(name, list(shape), dtype))
        return T(self, h, name, psum=True)

    def _dma_sem(self, t):
        if t.sem is None:
            t.sem = self.stack.enter_context(self.nc.semaphore("d%d_%s" % (self.nsem, t.name[:12])))
            self.nsem += 1
        return t.sem

    def _record(self, eng, fn, reads, writes, is_dma=False, dma_tile=None):
        i = Ins()
        i.id = len(self.ins)
        i.eng, i.fn, i.is_dma = eng, fn, is_dma
        i.deps = {}
        i.needed = False
        i.tick = None
        i.flushed = False
        i.dma_ev = None
        i.src = ""
        if DEBUG_SRC:
            import sys as _sys
            f = _sys._getframe(1)
            while f is not None and f.f_code.co_name in _FW_NAMES:
                f = f.f_back
            i.src = "L%d" % f.f_lineno if f is not None else ""
        rt, wt = _tiles(reads), _tiles(writes)
        if is_dma:
            sem = self._dma_sem(dma_tile)
            dma_tile.dma_n += 1
            i.dma_ev = ("d", sem, 16 * dma_tile.dma_n)
            ev = i.dma_ev
        else:
            ev = ("i", i.id)

        def add(e, kind):
            if e is None or e == ev:
                return
            if i.deps.get(e) != "raw":
                i.deps[e] = kind

        for t in rt:
            add(t.last_write, "raw")
            if t.psum:
                for r in t.reads:
                    add(r, "war")
        for t in wt:
            lw = t.last_write
            if not (is_dma and lw is not None and lw[0] == "d" and lw[1] is dma_tile.sem
                    and t is dma_tile and not t.reads):
                add(lw, "waw")
            for r in t.reads:
                add(r, "war")
        for t in rt:
            if t not in wt:
                t.reads.append(ev)
        for t in wt:
            t.last_write = ev
            t.reads = []
        self.ins.append(i)
        self.pending.append(i)
        return i

    def op(self, eng, fn, reads=(), writes=()):
        return self._record(eng, fn, reads, writes)

    def dma(self, eng, out, in_, **kw):
        on_chip = out if isinstance(out, (V, T)) else in_
        t = on_chip.t if isinstance(on_chip, V) else on_chip
        o, i_ = _ap(out), _ap(in_)
        sem = self._dma_sem(t)

        def fn(e):
            return e.dma_start(out=o, in_=i_, **kw)
        reads = [in_] if isinstance(in_, (V, T)) else []
        writes = [out] if isinstance(out, (V, T)) else []
        ins = self._record(eng, fn, reads, writes, is_dma=True, dma_tile=t)
        if not isinstance(out, (V, T)):
            self.out_events.append(ins.dma_ev)
        if eng == SP:
            self.sp_events[id(sem)] = (sem, ins.dma_ev[2])
        return ins

    def _resolve(self, e):
        if e[0] == "d":
            return e[1], e[2]
        p = self.ins[e[1]]
        return self.sem[p.eng], p

    def flush(self, final=False):
        pend = self.pending
        self.pending = []
        if not pend:
            return
        filt = {}
        for i in pend:
            keep = []
            for e, kind in i.deps.items():
                if e[0] == "i":
                    p = self.ins[e[1]]
                    if p.eng == i.eng and not i.is_dma:
                        if i.eng == PE:
                            continue
                    if not p.flushed:
                        p.needed = True
                keep.append(e)
            filt[i.id] = keep
        last = {}
        for i in pend:
            if not i.is_dma and i.eng in COMPUTE:
                last[i.eng] = i
        for i in last.values():
            i.needed = True
        for i in pend:
            if not i.is_dma and i.eng in COMPUTE and i.needed:
                self.ticks[i.eng] += 1
                i.tick = self.ticks[i.eng]
        nxt = {}
        for i in reversed(pend):
            if i.is_dma or i.eng not in COMPUTE:
                continue
            if i.tick is None:
                i.tick = nxt[i.eng]
            else:
                nxt[i.eng] = i.tick
        for i in pend:
            eh = self.engs[i.eng]
            need = {}
            for e in filt[i.id]:
                if e[0] == "d":
                    sem, val = e[1], e[2]
                else:
                    p = self.ins[e[1]]
                    sem, val = self.sem[p.eng], p.tick
                k = id(sem)
                if k not in need or need[k][1] < val:
                    need[k] = (sem, val)
            w = self.waited[i.eng]
            for k, (sem, val) in need.items():
                if w.get(k, 0) >= val:
                    continue
                eh.wait_ge(sem, val)
                w[k] = val
            b = i.fn(eh)
            if i.src:
                b.annotate(i.src)
            if i.is_dma:
                b.then_inc(i.dma_ev[1], 16)
            elif i.needed:
                b.then_inc(self.sem[i.eng], 1)
            i.flushed = True
            i.fn = None

    def finish(self):
        self.flush()
        need = {}
        for e in self.out_events:
            k = id(e[1])
            if k not in need or need[k][1] < e[2]:
                need[k] = (e[1], e[2])
        for sem, val in need.values():
            self.nc.sync.wait_ge(sem, val)

    def matmul(self, out, lhsT, rhs, start=True, stop=True, **kw):
        o, l, r = _ap(out), _ap(lhsT), _ap(rhs)
        return self.op(PE, lambda e: e.matmul(o, l, r, start=start, stop=stop, **kw),
                       reads=[lhsT, rhs], writes=[out])

    def transpose(self, out, in_, ident):
        o, i_, d = _ap(out), _ap(in_), _ap(ident)
        return self.op(PE, lambda e: e.transpose(o, i_, d), reads=[in_, ident], writes=[out])

    def act(self, out, in_, func, bias=None, scale=None, accum_out=None, eng=ACT):
        o, i_ = _ap(out), _ap(in_)
        kw = {}
        reads = [in_]
        writes = [out]
        if bias is not None:
            kw["bias"] = _ap(bias)
            reads.append(bias)
        if scale is not None:
            kw["scale"] = _ap(scale)
            reads.append(scale)
        if accum_out is not None:
            kw["accum_out"] = _ap(accum_out)
            writes.append(accum_out)
        return self.op(ACT, lambda e: e.activation(o, i_, func, **kw), reads=reads, writes=writes)

    def tt(self, eng, out, in0, in1, op):
        o, a, b = _ap(out), _ap(in0), _ap(in1)
        return self.op(eng, lambda e: e.tensor_tensor(o, a, b, op), reads=[in0, in1], writes=[out])

    def ts(self, eng, out, in0, s1, op0, s2=None, op1=None, accum_out=None):
        o, a = _ap(out), _ap(in0)
        reads = [in0]
        writes = [out]
        s1a, s2a = _ap(s1), _ap(s2)
        if isinstance(s1, (V, T)):
            reads.append(s1)
        if isinstance(s2, (V, T)):
            reads.append(s2)
        kw = {}
        if op1 is not None:
            kw["op1"] = op1
        if accum_out is not None:
            kw["accum_out"] = _ap(accum_out)
            writes.append(accum_out)
        return self.op(eng, lambda e: e.tensor_scalar(o, a, s1a, s2a, op0, **kw), reads=reads, writes=writes)

    def stt(self, eng, out, in0, scalar, in1, op0, op1):
        o, a, b = _ap(out), _ap(in0), _ap(in1)
        s = _ap(scalar)
        reads = [in0, in1]
        if isinstance(scalar, (V, T)):
            reads.append(scalar)
        return self.op(eng, lambda e: e.scalar_tensor_tensor(o, a, s, b, op0, op1), reads=reads, writes=[out])

    def copy(self, eng, out, in_):
        o, i_ = _ap(out), _ap(in_)
        if eng == ACT:
            return self.op(ACT, lambda e: e.copy(o, i_), reads=[in_], writes=[out])
        return self.op(eng, lambda e: e.tensor_copy(o, i_), reads=[in_], writes=[out])

    def memset(self, eng, out, val):
        o = _ap(out)
        return self.op(eng, lambda e: e.memset(o, val), reads=[], writes=[out])


    def barrier(self):
        self.flush()
        nc = self.nc
        for e in (PE, ACT, DVE, SP):
            eh = self.engs[e]
            w = self.waited[e]
            for e2 in COMPUTE:
                if e2 == e or self.ticks[e2] == 0:
                    continue
                k = id(self.sem[e2])
                if w.get(k, 0) >= self.ticks[e2]:
                    continue
                eh.wait_ge(self.sem[e2], self.ticks[e2])
                w[k] = self.ticks[e2]
            for k, (sem, val) in self.sp_events.items():
                if w.get(k, 0) >= val:
                    continue
                eh.wait_ge(sem, val)
                w[k] = val
D = 1024
DFF = 2816
NKC = 8
TG = 1024
NBLK = 8
DEPTH = 2
ALPHA = float((2 * DEPTH) ** 0.25)
LN_EPS = 1e-5
RMS_EPS = 1e-6
DPROJ = 8240
C_AX, C_AB, C_AC = 0, 512, 1024
C_DQ, C_DK, C_DV, C_DZ, C_DBETA, C_DA = 1536, 2048, 2560, 3072, 3584, 3592
C_GQ, C_GK, C_GV, C_GR, C_GLR, C_MG = 3600, 3856, 4112, 4624, 5136, 5168
NEG = -1.0e9
SLOT = 4096


def sub(v, offset_elems, dims):
    a = _ap(v)
    return bass.AP(a.tensor, a.offset + offset_elems, dims)


class _Mod2(list):
    def __getitem__(self, i):
        return list.__getitem__(self, i % 2)


class _Mod4(list):
    def __getitem__(self, i):
        return list.__getitem__(self, i % 4)


class Builder:
    def __init__(self, debug=None, skip=()):
        self.debug = debug or {}
        self.skip = skip
        nc = bass.Bass("TRN2", target_bir_lowering=False)
        self.nc = nc
        self.P = Prog(nc)
        self.dram = {}
        self.outs = {}
        self.declare()
        self.consts()
        self.vectors()
        self.ada_all()
        for g in range(2):
            self.group(g)
        self.P.finish()

    def din(self, name, shape):
        self.dram[name] = self.nc.dram_tensor(name, list(shape), F32, kind="ExternalInput").ap()

    def dout(self, name, shape):
        self.outs[name] = self.nc.dram_tensor(name, list(shape), F32, kind="ExternalOutput").ap()

    def declare(self):
        L = DEPTH
        self.din("xp", [TG, D]); self.din("xs", [TG, D])
        self.din("sd", [L, 2, 4, 128, 128]); self.din("sg", [L, 2, 4, 64, 128])
        self.din("cond", [2, D])
        self.din("w_ada", [L, D, 9 * D]); self.din("b_ada", [L, 9 * D])
        self.din("ln_g", [L, 3, D]); self.din("ln_b", [L, 3, D])
        self.din("ffn_w1", [L, 2, D, 2 * DFF]); self.din("ffn_w2", [L, 2, DFF, D])
        self.din("w_in", [L, D, DPROJ])
        self.din("conv_a", [L, 3, 512]); self.din("conv_qkv", [L, 3, 1536])
        self.din("delta_a_log", [L, 2, 4]); self.din("delta_dt_bias", [L, 2, 4])
        self.din("delta_norm_g", [L, 128])
        self.din("gla_w2", [L, 2, 16, 256]); self.din("gla_b", [L, 2, 256]); self.din("gla_norm_g", [L, 128])
        self.din("w_br_a", [L, 512, D]); self.din("w_br_d", [L, 512, D]); self.din("w_br_g", [L, 512, D])
        self.din("w_o", [L, D, D])
        self.dout("yp", [TG, D]); self.dout("ys", [TG, D])
        self.dout("nsd", [4, L, 2, 4, 128, 128]); self.dout("nsg", [4, L, 2, 4, 64, 128])
        for k, shp in self.debug.items():
            self.dout(k, shp)

    def bank(self):
        pool = getattr(self, "bank_pool", None)
        if pool:
            b = self.banks[pool[self.bank_i % len(pool)]]
        else:
            b = self.banks[self.bank_i % len(self.banks)]
        self.bank_i += 1
        return b

    def consts(self):
        P = self.P
        self.banks = [P.ps("bank%d" % i, [128, 512], F32) for i in range(8)]
        self.bank_i = 0
        self.slots = [P.sb("wslot%d" % i, [128, SLOT], BF16) for i in range(4)]
        self.slot_i = 0
        self.ident = P.sb("ident", [128, 128], F32)
        self.identb = P.sb("identb", [128, 128], BF16)
        self.ones = P.sb("ones", [128, 128], F32)
        P.memset(DVE, self.ones, 1.0)

        def sel(out_t, in_t, cmp, fill):
            o, i_ = _ap(out_t), _ap(in_t)
            cm, pat = 1, -1
            if cmp == ALU.is_le:
                cmp, cm, pat = ALU.is_ge, -1, 1
            elif cmp == ALU.is_lt:
                cmp, cm, pat = ALU.is_gt, -1, 1
            P.op(POOL, lambda e: e.affine_select(out=o, in_=i_, pattern=[[pat, 128]], compare_op=cmp,
                                                 fill=fill, base=0, channel_multiplier=cm),
                 reads=[in_t], writes=[out_t])
        sel(self.ident, self.ones, ALU.is_equal, 0.0)
        P.copy(DVE, self.identb, self.ident)
        mk = lambda n: P.sb(n, [128, 128], F32)
        self.NP = [P.sb("NP%d" % d, [128, 256], F32) for d in range(2)]
        self.negU = [self.NP[d][:, 0:128] for d in range(2)]
        self.posL = [self.NP[d][:, 128:256] for d in range(2)]
        self.m01 = [P.sb("m01_0", [128, 128], F32), P.sb("m01_1", [128, 128], F32)]
        self.triI = [mk("triI0"), mk("triI1")]
        self.gtriI = [mk("gtriI0"), mk("gtriI1")]
        self.gtriX = [mk("gtriX0"), mk("gtriX1")]
        self.bd16 = mk("bd16")
        self.mk = [P.sb("mk%d" % i, [128, 128], F32) for i in range(3)]
        cs_ = ExitStack()
        self.zeros = P.sb("zeros", [128, 128], F32, stack=cs_)
        negs = P.sb("negsixteenth", [128, 128], F32, stack=cs_)
        bd32 = P.sb("bd32", [128, 128], F32, stack=cs_)
        bd64 = P.sb("bd64", [128, 128], F32, stack=cs_)
        Et = {bsz: P.sb("E%d" % bsz, [128 // bsz, 128], F32, stack=cs_) for bsz in (16, 32, 64)}
        P.memset(DVE, self.zeros, 0.0)
        sel(self.negU[0], self.zeros, ALU.is_le, NEG)
        sel(self.negU[1], self.zeros, ALU.is_ge, NEG)
        sel(self.posL[0], self.zeros, ALU.is_gt, -NEG)
        sel(self.posL[1], self.zeros, ALU.is_lt, -NEG)
        sel(self.m01[0], self.ones, ALU.is_le, 0.0)
        sel(self.m01[1], self.ones, ALU.is_ge, 0.0)
        sel(self.triI[0], self.ones, ALU.is_le, 0.0)
        sel(self.triI[1], self.ones, ALU.is_ge, 0.0)
        for d in range(2):
            P.ts(DVE, self.gtriI[d], self.triI[d], -1.0 / 16.0, ALU.mult)
        P.memset(DVE, negs, -1.0 / 16.0)
        sel(self.gtriX[0], negs, ALU.is_gt, 0.0)
        sel(self.gtriX[1], negs, ALU.is_lt, 0.0)
        def blockdiag(bsz, t):
            nb_ = 128 // bsz
            E = Et[bsz]
            ea, oa = E.ap, self.ones[0:nb_, :].ap
            P.op(POOL, lambda e: e.affine_select(out=ea, in_=oa, pattern=[[1, 128]], compare_op=ALU.is_ge,
                                                 fill=0.0, base=0, channel_multiplier=-bsz), reads=[self.ones], writes=[E])
            P.op(POOL, lambda e: e.affine_select(out=ea, in_=ea, pattern=[[-1, 128]], compare_op=ALU.is_ge,
                                                 fill=0.0, base=bsz - 1, channel_multiplier=bsz), reads=[E], writes=[E])
            b = self.bank()
            P.matmul(b[:, 0:128], E, E)
            P.copy(DVE, t, b[:, 0:128])
            return t
        blockdiag(16, self.bd16)
        blockdiag(32, bd32)
        blockdiag(64, bd64)
        P.tt(DVE, self.mk[0], bd32, self.bd16, ALU.subtract)
        P.tt(DVE, self.mk[1], bd64, bd32, ALU.subtract)
        P.tt(DVE, self.mk[2], self.ones, bd64, ALU.subtract)
        P.barrier()
        cs_.close()

    def wtile(self, w2d, k0, nk, c0, cw=None):
        P = self.P
        ranges = c0 if isinstance(c0, list) else [(c0, cw)]
        tot = sum(r[1] for r in ranges)
        slot = self.slots[self.slot_i % len(self.slots)]
        self.slot_i += 1
        assert nk * tot <= SLOT
        dst = slot[:, 0:nk * tot].ap.rearrange("p (kc c) -> p kc c", kc=nk)
        off = 0
        for (a, w) in ranges:
            src = w2d[k0 * 128:(k0 + nk) * 128, a:a + w].rearrange("(kc p) c -> p kc c", p=128)
            P.dma(POOL, V(slot, dst[:, :, off:off + w]), src)
            off += w
        return V(slot, dst)

    def load_cols(self, dst_cols, rows_ap, nrows):
        P = self.P
        st = self.stg[self.stg_i % 2]
        self.stg_i += 1
        P.dma(SP, st[0:nrows, :], rows_ap)
        b = self.bank()
        P.transpose(b[:, 0:nrows], st[0:nrows, :], self.ident[0:nrows, 0:nrows])
        P.copy(DVE, dst_cols, b[:, 0:nrows])

    def vectors(self):
        P = self.P
        dr = self.dram
        self.stg = [P.sb("stg0", [128, 128], F32), P.sb("stg1", [128, 128], F32)]
        self.stg_i = 0
        self.vec = []
        for l in range(DEPTH):
            v = {}
            v["b_ada"] = P.sb("b_ada%d" % l, [128, 72], F32)
            self.load_cols(v["b_ada"][:, :], dr["b_ada"][l].rearrange("(r p) -> r p", p=128), 72)
            v["ln_g"] = P.sb("ln_g%d" % l, [128, 24], F32)
            self.load_cols(v["ln_g"][:, :], dr["ln_g"][l].rearrange("j (r p) -> (j r) p", p=128), 24)
            v["ln_b"] = P.sb("ln_b%d" % l, [128, 24], F32)
            self.load_cols(v["ln_b"][:, :], dr["ln_b"][l].rearrange("j (r p) -> (j r) p", p=128), 24)
            v["conv_a"] = P.sb("conv_a%d" % l, [128, 12], F32)
            self.load_cols(v["conv_a"][:, :], dr["conv_a"][l].rearrange("j (r p) -> (j r) p", p=128), 12)
            v["conv_qkv"] = P.sb("conv_qkv%d" % l, [128, 36], F32)
            self.load_cols(v["conv_qkv"][:, :], dr["conv_qkv"][l].rearrange("j (r p) -> (j r) p", p=128), 36)
            v["dng"] = P.sb("dng%d" % l, [128, 1], F32)
            self.load_cols(v["dng"][:, :], dr["delta_norm_g"][l:l + 1, :], 1)
            v["gng"] = P.sb("gng%d" % l, [128, 1], F32)
            self.load_cols(v["gng"][:, :], dr["gla_norm_g"][l:l + 1, :], 1)
            row = P.sb("dprow%d" % l, [1, 16], F32)
            P.dma(SP, row[0:1, 0:8], dr["delta_a_log"][l:l + 1].rearrange("o d h -> o (d h)"))
            P.dma(SP, row[0:1, 8:16], dr["delta_dt_bias"][l:l + 1].rearrange("o d h -> o (d h)"))
            b = self.bank()
            P.matmul(b[:, 0:16], self.ones[0:1, :], row[0:1, :])
            v["nega"] = P.sb("nega%d" % l, [128, 8], F32)
            v["dtb"] = P.sb("dtb%d" % l, [128, 8], F32)
            P.act(v["nega"], b[:, 0:8], AF.Exp)
            P.ts(DVE, v["nega"], v["nega"], -1.0, ALU.mult)
            P.copy(DVE, v["dtb"], b[:, 8:16])
            v["w2b"] = []
            for s_ in range(2):
                t = P.sb("w2b%d_%d" % (l, s_), [17, 256], BF16)
                P.dma(POOL, t[0:16, :], dr["gla_w2"][l, s_])
                P.dma(POOL, t[16:17, :], dr["gla_b"][l, s_:s_ + 1, :])
                v["w2b"].append(t)
            self.vec.append(v)
        cT = P.sb("condT", [128, 16], F32)
        self.load_cols(cT[:, :], dr["cond"].rearrange("i (r p) -> (i r) p", p=128), 16)
        self.condb = P.sb("condb", [128, 16], BF16)
        P.act(self.condb, cT, AF.Silu)
        P.flush()

    def ada_gen(self, l, bank_idx=None):
        P = self.P
        a = self.ada[l]
        w = self.dram["w_ada"][l]
        for cb in range(18):
            wt = self.wtile(w, 0, 8, cb * 512, 512)
            if cb % 4 == 0:
                bk = self.bank() if bank_idx is None else self.banks[bank_idx]
            for m in range(4):
                mi = cb * 4 + m
                o = bk[:, (mi % 16) * 2:(mi % 16) * 2 + 2]
                for kc in range(8):
                    P.matmul(o, wt[:, kc, m * 128:(m + 1) * 128], self.condb[:, kc:16:8],
                             start=(kc == 0), stop=(kc == 7))
            if cb % 4 == 3 or cb == 17:
                nm = 16 if cb % 4 == 3 else 8
                m0 = (cb // 4) * 16
                for i in range(2):
                    P.tt(DVE, a[:, i, m0:m0 + nm], bk[:, i:2 * nm:2], self.vec[l]["b_ada"][:, m0:m0 + nm], ALU.add)
            yield

    def ada_all(self):
        P = self.P
        self.ada = [P.sb("ada%d" % l, [128, 2, 72], F32) for l in range(DEPTH)]
        for _ in self.ada_gen(0):
            pass
        self.ada1_gen = self.ada_gen(1, bank_idx=7)
        P.flush()

    def group(self, g):
        P = self.P
        self.g = g
        x_dram = self.dram["xp" if g == 0 else "xs"]
        self.y_dram = self.outs["yp" if g == 0 else "ys"]
        self.segL = 256 if g == 0 else 64
        self.seqs = [(i * 2, 2) for i in range(4)] if g == 0 else [(0, 8)]
        with ExitStack() as gs:
            self.xa = [P.sb("xa%d" % n, [128, 8, 512], F32, stack=gs) for n in range(2)]
            self.h = [P.sb("h%d" % n, [128, 8, 512], BF16, stack=gs) for n in range(2)]
            self.dvec = P.sb("dvec", [128, 2, 3, 5, 8], F32, stack=gs)
            self.gs = gs
            self.derive_vectors(g, first=True)
            with ExitStack() as ps:
                xin = [P.sb("xin%d" % i, [128, D], F32, stack=ps) for i in range(2)]
                A0, B0 = self.first[:, 0, :], self.first[:, 1, :]
                for blk in range(8):
                    t = xin[blk % 2]
                    n, bs = blk // 4, slice((blk % 4) * 128, (blk % 4) * 128 + 128)
                    P.dma(SP, t, x_dram[blk * 128:(blk + 1) * 128, :])
                    for half in range(2):
                        b = self.bank()
                        for j in range(4):
                            c = half * 4 + j
                            P.transpose(b[:, j * 128:(j + 1) * 128], t[:, c * 128:(c + 1) * 128], self.ident)
                        for j in range(4):
                            c = half * 4 + j
                            P.ts(DVE, self.h[n][:, c, bs], b[:, j * 128:(j + 1) * 128], A0[:, c:c + 1], ALU.mult,
                                 B0[:, c:c + 1], ALU.add)
                        P.op(ACT, (lambda o_=self.xa[n][:, half * 4:half * 4 + 4, bs].ap,
                                   i_=b[:, 0:512].ap.rearrange("p (c t) -> p c t", c=4):
                                   lambda e: e.mul(o_, i_, ALPHA))(), reads=[b], writes=[self.xa[n]])
                P.barrier()
            for l in range(DEPTH):
                self.ffn(l, 0, 0)
                self.mixer(l)
                self.ffn(l, 1, 2)
            P.barrier()

    def derive_vectors(self, g, first):
        P = self.P
        dv = self.dvec
        if first:
            self.dvtmp = P.sb("dvtmp%d" % g, [128, 8], F32, stack=self.gs)
            self.first = P.sb("dvfirst%d" % g, [128, 2, 8], F32, stack=self.gs)
        tmp = self.dvtmp
        ada1_ready = (self.ada1_gen is None)
        for l in range(DEPTH):
            ad = self.ada[l]
            v = self.vec[l]
            for j in range(3):
                needs1 = (l == 1) or (l == 0 and j == 2)
                if first and needs1 and not ada1_ready:
                    if l == 0:
                        pass
                    else:
                        continue
                if (not first) and not needs1:
                    continue
                if (not first) and l == 0 and j == 2:
                    lg = v["ln_g"][:, j * 8:(j + 1) * 8]
                    lb = v["ln_b"][:, j * 8:(j + 1) * 8]
                    ad2 = self.ada[1]
                    sh = ad2[:, g, 0:8]
                    sc = ad2[:, g, 8:16]
                    P.ts(DVE, tmp, sc, 1.0, ALU.add)
                    P.tt(DVE, dv[:, l, j, 3, :], lg, tmp, ALU.mult)
                    P.tt(DVE, dv[:, l, j, 4, :], lb, tmp, ALU.mult)
                    P.tt(DVE, dv[:, l, j, 4, :], dv[:, l, j, 4, :], sh, ALU.add)
                    continue
                gt = ad[:, g, (3 * j + 2) * 8:(3 * j + 2) * 8 + 8]
                P.ts(DVE, dv[:, l, j, 2, :], gt, 0.5 if j != 1 else 1.0, ALU.mult)
                lg = v["ln_g"][:, j * 8:(j + 1) * 8]
                lb = v["ln_b"][:, j * 8:(j + 1) * 8]
                last = (l == DEPTH - 1 and j == 2)
                P.ts(DVE, dv[:, l, j, 0, :], lg, 1.0 if last else ALPHA, ALU.mult)
                P.ts(DVE, dv[:, l, j, 1, :], lb, 1.0 if last else ALPHA, ALU.mult)
                if last:
                    continue
                if first and l == 0 and j == 2 and not ada1_ready:
                    continue
                l2, j2 = (l, j + 1) if j < 2 else (l + 1, 0)
                ad2 = self.ada[l2]
                sh = ad2[:, g, (3 * j2) * 8:(3 * j2) * 8 + 8]
                sc = ad2[:, g, (3 * j2 + 1) * 8:(3 * j2 + 1) * 8 + 8]
                P.ts(DVE, tmp, sc, 1.0, ALU.add)
                P.tt(DVE, dv[:, l, j, 3, :], lg, tmp, ALU.mult)
                P.tt(DVE, dv[:, l, j, 4, :], lb, tmp, ALU.mult)
                P.tt(DVE, dv[:, l, j, 4, :], dv[:, l, j, 4, :], sh, ALU.add)
        if first:
            ad = self.ada[0]
            P.ts(DVE, self.first[:, 0, :], ad[:, g, 8:16], 1.0, ALU.add)
            P.copy(DVE, self.first[:, 1, :], ad[:, g, 0:8])
        P.flush()

    def ln_tiles(self, l, j):
        P = self.P
        dv = self.dvec
        last = (l == DEPTH - 1 and j == 2)
        with ExitStack() as ls:
            sq = [P.sb("lnsq%d" % i, [128, 512], F32, stack=ls) for i in range(2)]
            st = [P.sb("lnst%d" % i, [128, 512], F32, stack=ls) for i in range(4)]
            yt = [P.sb("lnyt%d" % i, [128, D], F32, stack=ls) for i in range(2)] if last else None
            for n in range(2):
                xa = self.xa[n]
                s1, s2 = self.bank(), self.bank()
                for c in range(8):
                    P.matmul(s1, self.ones, xa[:, c, :], start=(c == 0), stop=(c == 7))
                for c in range(8):
                    q = sq[c % 2]
                    P.act(q, xa[:, c, :], AF.Square)
                    P.matmul(s2, self.ones, q, start=(c == 0), stop=(c == 7))
                mean, m2, var, nmr = st
                P.op(ACT, (lambda o_=mean.ap, i_=s1.ap: lambda e: e.mul(o_, i_, 1.0 / D))(), reads=[s1], writes=[mean])
                P.tt(DVE, m2, mean, mean, ALU.mult)
                P.stt(DVE, var, s2, 1.0 / D, m2, ALU.mult, ALU.subtract)
                P.act(var, var, AF.Sqrt, bias=LN_EPS)
                P.op(DVE, (lambda a_=var.ap: lambda e: e.reciprocal(a_, a_))(), reads=[var], writes=[var])
                rstd = var
                P.stt(DVE, nmr, mean, -1.0, rstd, ALU.mult, ALU.mult)
                for c in range(8):
                    xc = xa[:, c, :]
                    P.tt(DVE, xc, xc, rstd, ALU.mult)
                    P.tt(DVE, xc, xc, nmr, ALU.add)
                    if not last:
                        P.act(self.h[n][:, c, :], xc, AF.Identity, scale=dv[:, l, j, 3, c:c + 1], bias=dv[:, l, j, 4, c:c + 1])
                    if c % 2:
                        P.act(xc, xc, AF.Identity, scale=dv[:, l, j, 0, c:c + 1], bias=dv[:, l, j, 1, c:c + 1])
                    else:
                        P.ts(DVE, xc, xc, dv[:, l, j, 0, c:c + 1], ALU.mult, dv[:, l, j, 1, c:c + 1], ALU.add)
                if last:
                    for tb in range(4):
                        y = yt[tb % 2]
                        for half in range(2):
                            b = self.bank()
                            for jj in range(4):
                                c = half * 4 + jj
                                P.transpose(b[:, jj * 128:(jj + 1) * 128], xa[:, c, tb * 128:(tb + 1) * 128], self.ident)
                            P.copy(ACT if half == 0 else DVE, y[:, half * 512:(half + 1) * 512], b)
                        r0 = n * 512 + tb * 128
                        P.dma(SP, self.y_dram[r0:r0 + 128, :], y)
            P.barrier()

    def ffn(self, l, jj, j):
        P = self.P
        w1 = self.dram["ffn_w1"][l, jj]
        w2 = self.dram["ffn_w2"][l, jj]
        gcol = self.dvec[:, l, j, 2, :]
        with ExitStack() as fs:
            hid = [P.sb("hid%d" % n, [128, 22, 512], BF16, stack=fs) for n in range(2)]
            sgt = [P.sb("sgt%d" % i, [128, 512], BF16, stack=fs) for i in range(2)]
            it = 0
            side = self.ada1_gen
            if side is not None:
                self.bank_pool = [0, 1, 2, 3, 4, 5, 6]

            def side_step(k):
                for _ in range(k):
                    if self.ada1_gen is None:
                        return
                    try:
                        next(self.ada1_gen)
                    except StopIteration:
                        self.ada1_gen = None
            for hb in range(6):
                side_step(2)
                nm = 4 if hb < 5 else 2
                wg = self.wtile(w1, 0, 8, hb * 512, nm * 128)
                wu = self.wtile(w1, 0, 8, DFF + hb * 512, nm * 128)
                for n in range(2):
                    for m in range(nm):
                        bg, bu = self.bank(), self.bank()
                        for kc in range(8):
                            P.matmul(bg, wg[:, kc, m * 128:(m + 1) * 128], self.h[n][:, kc, :], start=(kc == 0), stop=(kc == 7))
                        for kc in range(8):
                            P.matmul(bu, wu[:, kc, m * 128:(m + 1) * 128], self.h[n][:, kc, :], start=(kc == 0), stop=(kc == 7))
                        s = sgt[it % 2]
                        it += 1
                        P.act(s, bg, AF.Silu)
                        P.tt(DVE, hid[n][:, hb * 4 + m, :], bu, s, ALU.mult)
            for cb in range(4):
                side_step(2)
                wA = self.wtile(w2, 0, 11, cb * 256, 256)
                wB = self.wtile(w2, 11, 11, cb * 256, 256)
                for n in range(2):
                    for m in range(2):
                        c = cb * 2 + m
                        by = self.bank()
                        for kc in range(22):
                            w = wA if kc < 11 else wB
                            P.matmul(by, w[:, kc % 11, m * 128:(m + 1) * 128], hid[n][:, kc, :], start=(kc == 0), stop=(kc == 21))
                        P.stt(DVE, self.xa[n][:, c, :], by, gcol[:, c:c + 1], self.xa[n][:, c, :], ALU.mult, ALU.add)
            if side is not None:
                side_step(99)
                self.bank_pool = None
                self.derive_vectors(self.g, first=False)
            self.ln_tiles(l, j)

    def conv3(self, o, p, wcols, row, nmul):
        P = self.P
        SL = self.segL
        w0, w1, w2 = (wcols[:, j * nmul + row:j * nmul + row + 1] for j in range(3))
        P.act(o, p, AF.Identity, scale=w1)
        o3 = V(o.t, o.ap.rearrange("p (s l) -> p s l", l=SL))
        p3 = V(p.t, p.ap.rearrange("p (s l) -> p s l", l=SL))
        P.stt(DVE, o3[:, :, 1:SL], p3[:, :, 0:SL - 1], w0, o3[:, :, 1:SL], ALU.mult, ALU.add)
        P.stt(DVE, o3[:, :, 0:SL - 1], p3[:, :, 1:SL], w2, o3[:, :, 0:SL - 1], ALU.mult, ALU.add)

    def proj(self, b, wv, col0, n, M=128):
        for kc in range(8):
            self.P.matmul(b[0:M, :], wv[:, kc, col0:col0 + M], self.h[n][:, kc, :], start=(kc == 0), stop=(kc == 7))

    def merge_branch(self, l, bi, br_w, src):
        P = self.P
        w_in = self.dram["w_in"][l]
        wbr = self.wtile(br_w, 0, 4, 0, 1024)
        with ExitStack() as s_:
            sig = [P.sb("sig%d" % i, [128, 512], F32, stack=s_) for i in range(2)]
            tmp = [P.sb("mtmp%d" % i, [128, 512], F32, stack=s_) for i in range(2)]
            it = 0
            for cb in range(2):
                wg = self.wtile(w_in, 0, 8, C_MG + bi * 1024 + cb * 512, 512)
                for n in range(2):
                    for m in range(4):
                        c = cb * 4 + m
                        bb, bg = self.bank(), self.bank()
                        for kc in range(4):
                            P.matmul(bb, wbr[:, kc, c * 128:(c + 1) * 128], src(kc, n), start=(kc == 0), stop=(kc == 3))
                        self.proj(bg, wg, m * 128, n)
                        sg = sig[it % 2]
                        P.act(sg, bg, AF.Sigmoid)
                        if bi == 0:
                            P.tt(DVE, self.merged[n][:, c, :], bb, sg, ALU.mult)
                        else:
                            t = tmp[it % 2]
                            P.tt(DVE, t, bb, sg, ALU.mult)
                            P.tt(DVE, self.merged[n][:, c, :], self.merged[n][:, c, :], t, ALU.add)
                        it += 1
            P.barrier()

    def mixer(self, l):
        P = self.P
        with ExitStack() as ms:
            self.merged = [P.sb("merged%d" % n, [128, 8, 512], F32, stack=ms) for n in range(2)]
            skip = getattr(self, "skip", ())
            if "a" not in skip:
                self.branch_a(l)
            if "d" not in skip:
                self.branch_d(l)
            if "g" not in skip:
                self.branch_g(l)
            for n in range(2):
                for c in range(8):
                    P.copy(ACT if c % 2 else DVE, self.h[n][:, c, :], self.merged[n][:, c, :])
            w_o = self.dram["w_o"][l]
            gcol = self.dvec[:, l, 1, 2, :]
            for cb in range(2):
                wo = self.wtile(w_o, 0, 8, cb * 512, 512)
                for n in range(2):
                    for m in range(4):
                        c = cb * 4 + m
                        by = self.bank()
                        self.proj(by, wo, m * 128, n)
                        P.stt(DVE, self.xa[n][:, c, :], by, gcol[:, c:c + 1], self.xa[n][:, c, :], ALU.mult, ALU.add)
            self.ln_tiles(l, 1)

    def branch_a(self, l):
        P = self.P
        w_in = self.dram["w_in"][l]
        cw = self.vec[l]["conv_a"]
        with ExitStack() as s_:
            ya = [P.sb("ya%d" % n, [128, 4, 512], BF16, stack=s_) for n in range(2)]
            s2_ = ExitStack()
            axs = [P.sb("axs%d" % i, [128, 512], F32, stack=s2_) for i in range(2)]
            pp = [P.sb("app%d" % i, [128, 512], F32, stack=s2_) for i in range(2)]
            oo = [P.sb("aoo%d" % i, [128, 512], F32, stack=s2_) for i in range(2)]
            wx = self.wtile(w_in, 0, 8, C_AX, 512)
            wb = self.wtile(w_in, 0, 8, C_AB, 512)
            wc = self.wtile(w_in, 0, 8, C_AC, 512)
            it = 0
            for n in range(2):
                for m in range(4):
                    bx, bc, bb = self.bank(), self.bank(), self.bank()
                    self.proj(bx, wx, m * 128, n)
                    self.proj(bc, wc, m * 128, n)
                    self.proj(bb, wb, m * 128, n)
                    a_, p_, o_ = axs[it % 2], pp[it % 2], oo[it % 2]
                    it += 1
                    P.copy(ACT, a_, bx)
                    P.tt(DVE, p_, bc, a_, ALU.mult)
                    self.conv3(o_[:, :], p_[:, :], cw, m, 4)
                    P.tt(DVE, ya[n][:, m, :], bb, o_, ALU.mult)
            P.barrier()
            s2_.close()
            self.merge_branch(l, 0, self.dram["w_br_a"][l], lambda kc, n: ya[n][:, kc, :])

    def blkv(self, tiles, blk, lead=None):
        n, b0 = blk // 4, (blk % 4) * 128
        if lead is None:
            return tiles[n][:, b0:b0 + 128]
        return tiles[n][:, lead, b0:b0 + 128]

    def branch_d(self, l):
        P = self.P
        g = self.g
        w_in = self.dram["w_in"][l]
        v = self.vec[l]
        cw = v["conv_qkv"]
        NBAT = 4
        with ExitStack() as s_:
            od = [P.sb("od%d" % n, [128, 4, 512], BF16, stack=s_) for n in range(2)]
            s2_ = ExitStack()
            sb = lambda name, shp, dt=F32: P.sb(name, shp, dt, stack=s2_)
            betaT = sb("betaT", [128, 8, 8]); gT = sb("gT", [128, 8, 8]); Gc = sb("Gc", [128, 8, 8])
            eGL = sb("eGL", [128, 8, 8]); bexpG = sb("bexpG", [128, 8, 8]); eGrev = sb("eGrev", [128, 8, 8])
            t8 = sb("t8", [128, 8, 8])
            wsm = self.wtile(w_in, 0, 8, C_DBETA, 16)
            bsm = self.bank()
            for blk in range(8):
                for kc in range(8):
                    P.matmul(bsm[:, blk * 16:(blk + 1) * 16], self.blkv(self.h, blk, kc), wsm[:, kc, :],
                             start=(kc == 0), stop=(kc == 7))
            bview = V(bsm, bsm.ap[:, 0:128].rearrange("p (b c) -> p b c", c=16))
            P.act(betaT, bview[:, :, 0:8], AF.Sigmoid)
            bc8 = lambda t: V(t, bass.AP(t.ap.tensor, t.ap.offset, [list(t.ap.ap[0]), [0, 8], [1, 8]]))
            P.tt(DVE, t8, bview[:, :, 8:16], bc8(v["dtb"]), ALU.add)
            P.act(t8, t8, AF.Exp)
            P.act(t8, t8, AF.Ln, bias=1.0)
            P.tt(DVE, gT, t8, bc8(v["nega"]), ALU.mult)
            bG = self.bank()
            gTd = [sb("gTd%d" % d, [128, 32]) for d in range(2)]
            for d in range(2):
                P.copy(DVE, V(gTd[d], gTd[d].ap.rearrange("p (b c) -> p b c", c=4)), gT[:, :, d * 4:d * 4 + 4])
                P.matmul(bG[:, d * 32:(d + 1) * 32], self.triI[d], gTd[d])
            for d in range(2):
                P.copy(DVE, Gc[:, :, d * 4:d * 4 + 4], V(bG, bG.ap[:, d * 32:(d + 1) * 32].rearrange("p (b c) -> p b c", c=4)))
            bL = self.bank()
            bLv = V(bL, bL.ap[:, 0:64].rearrange("p (b c) -> p b c", c=8))
            P.matmul(bL[:, 0:64], self.ones, V(gT, gT.ap.rearrange("p b c -> p (b c)")))
            P.act(eGL, bLv, AF.Exp)
            P.tt(DVE, t8, bLv, Gc, ALU.subtract)
            P.act(eGrev, t8, AF.Exp)
            P.act(t8, Gc, AF.Exp)
            P.tt(DVE, bexpG, betaT, t8, ALU.mult)
            qT = [sb("qT%d" % n, [128, 512], BF16) for n in range(2)]
            kT = [sb("kT%d" % n, [128, 512], BF16) for n in range(2)]
            zs = [sb("zs%d" % n, [128, 512], BF16) for n in range(2)]
            wqs = {}
            ktok = sb("ktok", [128, 8, 128], BF16); vtok = sb("vtok", [128, 8, 128], BF16)
            oacc = [sb("oacc%d" % n, [128, 512]) for n in range(2)]
            pre = sb("dpre", [128, 512]); cvo = sb("dcvo", [128, 512]); rr = sb("drr", [128, 512])
            vT = [V(rr, rr.ap.bitcast(BF16)[:, n * 512:(n + 1) * 512]) for n in range(2)]
            gU = [sb("gU%d" % i, [128, 128], BF16) for i in range(NBAT)]
            AB = [[sb("AB%d_%d" % (i, k), [128, 384]) for k in range(2)] for i in range(NBAT)]
            MN = [sb("MN%d" % i, [128, 256]) for i in range(NBAT)]
            CC = [sb("CC%d" % i, [128, 256]) for i in range(NBAT)]
            CCb = [V(CC[i], CC[i].ap.bitcast(BF16)[:, 0:256]) for i in range(NBAT)]
            WWb = [V(CC[i], CC[i].ap.bitcast(BF16)[:, 256:512]) for i in range(NBAT)]
            YY = [[V(AB[i][k], AB[i][k].ap.bitcast(BF16)[:, 0:256]) for k in range(2)] for i in range(NBAT)]
            Tt = [sb("Tt%d" % i, [128, 128], BF16) for i in range(NBAT)]
            vbk = [sb("vbk%d" % i, [128, 256], BF16) for i in range(2)]
            mk2 = lambda nm, dt=BF16, w=128: [[sb("%s%d_%d" % (nm, d, i), [128, w], dt) for i in range(8)] for d in range(2)]
            qdec, kdec, atT = mk2("qdec"), mk2("kdec"), mk2("atT")
            uw = mk2("uw", BF16, 256)
            usb = [[uw[d][i][:, 0:128] for i in range(8)] for d in range(2)]
            wTs = [[uw[d][i][:, 128:256] for i in range(8)] for d in range(2)]
            vnw = [sb("vnw%d" % i, [128, 128], BF16) for i in range(2)]
            nseq = len(self.seqs)
            S = [[sb("S%d_%d" % (q, d), [128, 128]) for d in range(2)] for q in range(nseq)]
            Sb = [[sb("Sb%d_%d" % (q, d), [128, 128], BF16) for d in range(2)] for q in range(nseq)]
            vi = [0]

            def proj_gen(hh):
                wq = self.wtile(w_in, 0, 8, [(C_DQ + hh * 128, 128), (C_DK + hh * 128, 128),
                                             (C_DV + hh * 128, 128), (C_DZ + hh * 128, 128)])
                wqs[hh] = wq
                for wi, dstT in ((0, qT), (1, kT), (2, vT)):
                    for n in range(2):
                        b = self.bank()
                        self.proj(b, wq, wi * 128, n)
                        P.copy(ACT, pre, b)
                        self.conv3(cvo[:, :], pre[:, :], cw, wi * 4 + hh, 12)
                        if wi == 2:
                            P.act(dstT[n], cvo, AF.Silu)
                        else:
                            P.act(cvo, cvo, AF.Silu)
                            P.act(pre, cvo, AF.Square)
                            bs_ = self.bank()
                            P.matmul(bs_, self.ones, pre)
                            P.act(rr, bs_, AF.Sqrt, bias=RMS_EPS)
                            P.op(DVE, (lambda a_=rr.ap: lambda e: e.reciprocal(a_, a_))(), reads=[rr], writes=[rr])
                            if wi == 0:
                                P.stt(DVE, dstT[n], cvo, 128.0 ** -0.5, rr, ALU.mult, ALU.mult)
                            else:
                                P.tt(DVE, dstT[n], cvo, rr, ALU.mult)
                        yield
                for blk in range(8):
                    b = self.bank()
                    bb16 = V(b, b.ap.bitcast(BF16))
                    P.transpose(bb16[:, 0:128], self.blkv(kT, blk), self.identb)
                    P.transpose(bb16[:, 128:256], self.blkv(vT, blk), self.identb)
                    P.copy(ACT, ktok[:, blk, :], bb16[:, 0:128])
                    P.copy(DVE, vtok[:, blk, :], bb16[:, 128:256])
                    if blk % 2:
                        yield

            def bc2(vw):
                a = _ap(vw)
                return V(vw.t, bass.AP(a.tensor, a.offset, [list(a.ap[0]), [0, 2], list(a.ap[1])]))

            def seg2(vw):
                a = _ap(vw)
                return V(vw.t, bass.AP(a.tensor, a.offset, [list(a.ap[0]), [256, 2], [1, 128]]))

            def h2(vw):
                a = _ap(vw)
                return V(vw.t, a.rearrange("p (a b) -> p a b", a=2))

            def prescan_gen(hh, d, batches):
                idx = d * 4 + hh
                for blks in batches:
                    col = lambda t, blk: t[:, blk, idx:idx + 1]
                    st = lambda blk: blk % NBAT
                    pg = {blk: self.banks[blk % NBAT] for blk in blks}
                    for blk in blks:
                        r2 = AB[st(blk)][1][:, 0:128]
                        P.ts(DVE, r2, self.triI[d], col(gT, blk), ALU.mult)
                        P.matmul(pg[blk][:, 0:128], self.ones, r2)
                        P.matmul(pg[blk][:, 128:256], self.blkv(kT, blk), self.blkv(kT, blk))
                    yield
                    for blk in blks:
                        P.stt(DVE, h2(CC[st(blk)][:, 0:256]), bc2(pg[blk][:, 0:128]), col(Gc, blk), h2(self.NP[d][:, 0:256]),
                              ALU.subtract, ALU.add)
                        P.act(AB[st(blk)][0][:, 0:128], pg[blk][:, 0:128], AF.Exp)
                    yield
                    for blk in blks:
                        P.act(gU[st(blk)], CC[st(blk)][:, 0:128], AF.Exp)
                        P.act(CC[st(blk)][:, 128:256], CC[st(blk)][:, 128:256], AF.Exp, scale=-1.0)
                        P.tt(DVE, qdec[d][blk], self.blkv(qT, blk), AB[st(blk)][0][:, 0:128], ALU.mult)
                        P.stt(DVE, MN[st(blk)][:, 0:128], pg[blk][:, 128:256], col(betaT, blk), CC[st(blk)][:, 128:256],
                              ALU.mult, ALU.mult)
                    yield
                    pb = {blk: self.banks[blk % NBAT] for blk in blks}
                    for blk in blks:
                        P.transpose(pb[blk][:, 0:128], MN[st(blk)][:, 0:128], self.ident)
                    for blk in blks:
                        P.copy(DVE, MN[st(blk)][:, 128:256], pb[blk][:, 0:128])
                    yield
                    for blk in blks:
                        P.tt(DVE, h2(CC[st(blk)][:, 0:256]), h2(MN[st(blk)][:, 0:256]), bc2(self.bd16[:, :]), ALU.mult)
                    for blk in blks:
                        M0, N0 = CC[st(blk)][:, 0:128], CC[st(blk)][:, 128:256]
                        P.tt(DVE, AB[st(blk)][1][:, 128:256], self.ident, N0, ALU.subtract)
                        P.matmul(pb[blk][:, 0:128], M0, N0)
                        P.matmul(pb[blk][:, 256:384], N0, M0)
                    yield
                    for blk in blks:
                        P.copy(ACT, seg2(AB[st(blk)][1][:, 0:384]), seg2(pb[blk][:, 0:384]))
                    for k in (1, 2):
                        cur, nxt = k % 2, (k + 1) % 2
                        for blk in blks:
                            A_ = AB[st(blk)][cur]
                            P.matmul(pb[blk][:, 0:256], A_[:, 256:384], A_[:, 0:256])
                            P.matmul(pb[blk][:, 256:384], A_[:, 0:128], A_[:, 256:384])
                        yield
                        for blk in blks:
                            P.copy(ACT, seg2(AB[st(blk)][nxt][:, 0:384]), seg2(pb[blk][:, 0:384]))
                            P.tt(DVE, AB[st(blk)][nxt][:, 128:256], AB[st(blk)][cur][:, 128:256], pb[blk][:, 128:256], ALU.add)
                    for blk in blks:
                        P.matmul(pb[blk][:, 0:128], AB[st(blk)][1][:, 256:384], AB[st(blk)][1][:, 128:256])
                    yield
                    for blk in blks:
                        pbb = V(pb[blk], pb[blk].ap.bitcast(BF16))
                        P.tt(DVE, YY[st(blk)][0][:, 128:256], AB[st(blk)][1][:, 128:256], pb[blk][:, 0:128], ALU.add)
                        P.transpose(pbb[:, 256:384], YY[st(blk)][0][:, 128:256], self.identb)
                    for blk in blks:
                        pbb = V(pb[blk], pb[blk].ap.bitcast(BF16))
                        P.copy(ACT, YY[st(blk)][0][:, 0:128], pbb[:, 256:384])
                    yield
                    for li in range(3):
                        mk = self.mk[li]
                        last = (li == 2)
                        for blk in blks:
                            P.tt(DVE, h2(CCb[st(blk)][:, 0:256]), h2(MN[st(blk)][:, 0:256]), bc2(mk[:, :]), ALU.mult)
                        for blk in blks:
                            cur = YY[st(blk)][li % 2]
                            Y, Yt = cur[:, 0:128], cur[:, 128:256]
                            C, Ct = CCb[st(blk)][:, 0:128], CCb[st(blk)][:, 128:256]
                            if not last:
                                P.matmul(pb[blk][:, 0:128], Ct, Y)
                            P.matmul(pb[blk][:, 128:256], C, Yt)
                        yield
                        for blk in blks:
                            if not last:
                                P.copy(ACT, WWb[st(blk)][:, 0:256], pb[blk][:, 0:256])
                            else:
                                P.copy(ACT, WWb[st(blk)][:, 128:256], pb[blk][:, 128:256])
                        for blk in blks:
                            cur = YY[st(blk)][li % 2]
                            Y, Yt = cur[:, 0:128], cur[:, 128:256]
                            if not last:
                                P.matmul(pb[blk][:, 256:384], Yt, WWb[st(blk)][:, 0:128])
                            P.matmul(pb[blk][:, 384:512], Y, WWb[st(blk)][:, 128:256])
                        yield
                        for blk in blks:
                            cur, nxt = YY[st(blk)][li % 2], YY[st(blk)][(li + 1) % 2]
                            if not last:
                                P.tt(DVE, nxt[:, 0:256], cur[:, 0:256], pb[blk][:, 256:512], ALU.subtract)
                            else:
                                P.tt(DVE, Tt[st(blk)], cur[:, 128:256], pb[blk][:, 384:512], ALU.subtract)
                    pu = {blk: self.banks[blk % NBAT] for blk in blks}
                    for blk in blks:
                        vk = vbk[blk % 2]
                        P.act(vk[:, 0:128], vtok[:, blk, :], AF.Copy, scale=col(betaT, blk))
                        P.act(vk[:, 128:256], ktok[:, blk, :], AF.Copy, scale=col(bexpG, blk))
                        P.act(kdec[d][blk], ktok[:, blk, :], AF.Copy, scale=col(eGrev, blk))
                        P.matmul(pu[blk][:, 0:128], Tt[st(blk)], vk[:, 0:128])
                        P.matmul(pu[blk][:, 128:256], vk[:, 128:256], Tt[st(blk)])
                        P.matmul(pu[blk][:, 256:384], self.blkv(kT, blk), self.blkv(qT, blk))
                    yield
                    for blk in blks:
                        P.copy(ACT, uw[d][blk], pu[blk][:, 0:256])
                        P.tt(DVE, atT[d][blk], pu[blk][:, 256:384], gU[st(blk)], ALU.mult)
                    yield

            def scan_gen(hh, d):
                idx = d * 4 + hh
                for q, (b0, nb) in enumerate(self.seqs):
                    if g == 0:
                        P.memset(DVE, S[q][d], 0.0)
                        P.memset(DVE, Sb[q][d], 0.0)
                    else:
                        P.dma(SP, S[q][d], self.dram["sd"][l, d, hh])
                        P.copy(ACT, Sb[q][d], S[q][d])
                nb = self.seqs[0][1]
                for step in range(nb):
                    for q, (b0, _) in enumerate(self.seqs):
                        blk = b0 + (step if d == 0 else nb - 1 - step)
                        St, Sbt = S[q][d], Sb[q][d]
                        p1 = self.banks[4 + vi[0] % 2]
                        P.matmul(p1[:, 0:128], wTs[d][blk], Sbt)
                        vn = vnw[vi[0] % 2]
                        vi[0] += 1
                        P.tt(DVE, vn, usb[d][blk], p1[:, 0:128], ALU.subtract)
                        yield
                        P.matmul(p1[:, 128:256], Sbt, qdec[d][blk], start=True, stop=False)
                        P.matmul(p1[:, 128:256], vn, atT[d][blk], start=False, stop=True)
                        P.matmul(p1[:, 256:384], kdec[d][blk], vn)
                        ov = self.blkv(oacc, blk)
                        if d == 0:
                            P.copy(ACT, ov, p1[:, 128:256])
                        else:
                            P.tt(DVE, ov, ov, p1[:, 128:256], ALU.add)
                        P.stt(DVE, St, St, eGL[:, blk, idx:idx + 1], p1[:, 256:384], ALU.mult, ALU.add)
                        P.copy(ACT, Sbt, St)
                        yield
                if g == 0:
                    for q in range(nseq):
                        P.dma(SP, self.outs["nsd"][q, l, d, hh], S[q][d])

            def norm(hh):
                for n in range(2):
                    b = self.bank()
                    self.proj(b, wqs[hh], 3 * 128, n)
                    P.act(zs[n], b, AF.Silu)
                for n in range(2):
                    P.act(pre, oacc[n], AF.Square)
                    bs_ = self.bank()
                    P.matmul(bs_, self.ones, pre)
                    P.act(rr, bs_, AF.Sqrt, bias=RMS_EPS, scale=1.0 / 128.0)
                    P.op(DVE, (lambda a_=rr.ap: lambda e: e.reciprocal(a_, a_))(), reads=[rr], writes=[rr])
                    P.tt(DVE, rr, oacc[n], rr, ALU.mult)
                    P.stt(DVE, od[n][:, hh, :], rr, v["dng"][:, 0:1], zs[n], ALU.mult, ALU.mult)

            def prescan2(hh, d, offset=1):
                ga = prescan_gen(hh, d, [[0, 1], [4, 5]])
                gb = prescan_gen(hh, d, [[2, 3], [6, 7]])
                rnd = 0
                while ga is not None or gb is not None:
                    if ga is not None:
                        try:
                            next(ga)
                        except StopIteration:
                            ga = None
                    if gb is not None and rnd >= offset:
                        try:
                            next(gb)
                        except StopIteration:
                            gb = None
                    rnd += 1
                    yield

            def run(main, side=None, ratio=2):
                cnt = 0
                for _ in main:
                    cnt += 1
                    if side is not None and cnt % ratio == 0:
                        try:
                            next(side)
                        except StopIteration:
                            side = None
                if side is not None:
                    for _ in side:
                        pass

            self.bank_pool = [6, 7]
            run(proj_gen(0))
            for hh in range(4):
                run(prescan2(hh, 0))
                run(prescan2(hh, 1), scan_gen(hh, 0), ratio=1)
                if hh < 3:
                    run(scan_gen(hh, 1), proj_gen(hh + 1), ratio=1)
                else:
                    run(scan_gen(hh, 1))
                norm(hh)
            self.bank_pool = None
            P.barrier()
            s2_.close()
            self.merge_branch(l, 1, self.dram["w_br_d"][l], lambda kc, n: od[n][:, kc, :])

    def branch_g(self, l):
        P = self.P
        g = self.g
        w_in = self.dram["w_in"][l]
        v = self.vec[l]
        NB4 = 4
        with ExitStack() as s_:
            og = [P.sb("og%d" % n, [128, 4, 512], BF16, stack=s_) for n in range(2)]
            s2_ = ExitStack()
            sb = lambda name, shp, dt=F32: P.sb(name, shp, dt, stack=s2_)
            gqT = [[sb("gqT%d_%d" % (hp, n), [128, 512], BF16) for n in range(2)] for hp in range(2)]
            gkT = [[sb("gkT%d_%d" % (hp, n), [128, 512], BF16) for n in range(2)] for hp in range(2)]
            gktok = sb("gktok", [128, 8, 256], BF16)
            gvtok = sb("gvtok", [128, 8, 512], BF16)
            lra = [sb("lra%d" % s, [17, 1024], BF16) for s in range(2)]
            ogacc = [[sb("ogacc%d_%d" % (a, n), [128, 512]) for n in range(2)] for a in range(2)]
            sp = sb("gsp", [128, 8, 128])
            kd = [sb("gkd%d" % i, [128, 8, 128], BF16) for i in range(2)]
            ek = [sb("gek_%d" % i, [128, 128], BF16) for i in range(NB4)]
            ebuf = sb("gebuf", [128, NB4, 128])
            eB = [ebuf[:, i, :] for i in range(NB4)]
            enB = [sb("genB_%d" % i, [128, 128], BF16) for i in range(NB4)]
            eBl = [sb("geBl%d" % i, [128, 8]) for i in range(2)]
            qe = [[[sb("gqe%d_%d_%d" % (pp, a, i), [128, 128], BF16) for i in range(8)] for a in range(2)] for pp in range(2)]
            qef = [sb("gqef%d" % i, [128, 128], BF16) for i in range(NB4)]
            rm = sb("grm", [128, 2])
            P.memset(DVE, rm, 0.0)
            P.memset(DVE, rm[0:64, 0:1], 1.0)
            P.memset(DVE, rm[64:128, 1:2], 1.0)
            ke = [sb("gke%d" % i, [128, 128], BF16) for i in range(NB4)]
            atT = [[[sb("gat%d_%d_%d" % (pp, a, i), [128, 128], BF16) for i in range(8)] for a in range(2)] for pp in range(2)]
            nseq = len(self.seqs)
            S = [[sb("gS%d_%d" % (q, pp), [128, 128]) for pp in range(2)] for q in range(nseq)]
            Sb = [[sb("gSb%d_%d" % (q, pp), [128, 128], BF16) for pp in range(2)] for q in range(nseq)]
            ebf = V(ebuf, ebuf.ap.rearrange("p b c -> p (b c)").bitcast(BF16))
            grs = [ebf[:, n * 512:(n + 1) * 512] for n in range(2)]
            spf = V(sp, sp.ap.rearrange("p b c -> p (b c)"))
            sq, rr = spf[:, 0:512], spf[:, 512:1024]

            wqk = self.wtile(w_in, 0, 8, C_GQ, 512)
            for hp in range(2):
                for n in range(2):
                    b = self.bank()
                    self.proj(b, wqk, hp * 128, n)
                    P.ts(DVE, gqT[hp][n], b, 0.125, ALU.mult)
                    b2 = self.bank()
                    self.proj(b2, wqk, 256 + hp * 128, n)
                    P.copy(ACT, gkT[hp][n], b2)
            for blk in range(8):
                b = self.bank()
                for kc in range(8):
                    P.matmul(b[:, 0:256], self.blkv(self.h, blk, kc), wqk[:, kc, 256:512], start=(kc == 0), stop=(kc == 7))
                P.copy(ACT, gktok[:, blk, :], b[:, 0:256])
            wv = self.wtile(w_in, 0, 8, C_GV, 512)
            for blk in range(8):
                b = self.bank()
                for kc in range(8):
                    P.matmul(b, self.blkv(self.h, blk, kc), wv[:, kc, :], start=(kc == 0), stop=(kc == 7))
                P.copy(DVE if blk % 2 else ACT, gvtok[:, blk, :], b)
            wl = self.wtile(w_in, 0, 8, C_GLR, 32)
            for s in range(2):
                P.memset(DVE, lra[s], 1.0)
                for n in range(2):
                    b = self.bank()
                    self.proj(b, wl, s * 16, n, M=16)
                    P.copy(ACT, lra[s][0:16, n * 512:(n + 1) * 512], b[0:16, :])
            wr = self.wtile(w_in, 0, 8, C_GR, 512)

            def prescan_half(hp, s, pp, batches):
                w2b = v["w2b"][s]
                for blks in batches:
                    st = lambda blk: blk % NB4
                    bl = {blk: self.banks[blk % NB4] for blk in blks}
                    for blk in blks:
                        P.matmul(bl[blk][:, 128:256], sp[:, blk, :], self.gtriI[s])
                        P.matmul(bl[blk][:, 256:384], self.gtriX[s], sp[:, blk, :])
                    yield
                    lc = 127 if s == 0 else 0
                    for blk in blks:
                        P.act(eB[st(blk)], bl[blk][:, 128:256], AF.Exp)
                        P.act(enB[st(blk)], bl[blk][:, 128:256], AF.Exp, scale=-1.0)
                        P.act(ek[st(blk)], bl[blk][:, 256:384], AF.Exp)
                    yield
                    for blk in blks:
                        eb, enb = eB[st(blk)], enB[st(blk)]
                        P.copy(DVE, eBl[pp][:, blk:blk + 1], eb[:, lc:lc + 1])
                        P.tt(DVE, qef[st(blk)], self.blkv(gqT[hp], blk), eb, ALU.mult)
                        for a in range(2):
                            P.ts(DVE, qe[pp][a][blk], qef[st(blk)], rm[:, a:a + 1], ALU.mult)
                        P.tt(DVE, ke[st(blk)], self.blkv(gkT[hp], blk), enb, ALU.mult)
                        P.tt(DVE, kd[pp][:, blk, :], gktok[:, blk, hp * 128:(hp + 1) * 128], ek[st(blk)], ALU.mult)
                    yield
                    pa = {blk: self.banks[blk % NB4] for blk in blks}
                    for blk in blks:
                        for a in range(2):
                            P.matmul(pa[blk][:, a * 128:(a + 1) * 128], ke[st(blk)], qe[pp][a][blk])
                    yield
                    for blk in blks:
                        for a in range(2):
                            P.tt(DVE, atT[pp][a][blk], pa[blk][:, a * 128:(a + 1) * 128], self.m01[s], ALU.mult)
                    yield

            sci = [0]

            def prescan_gen(hp, s, pp, offset=1):
                w2b_ = v["w2b"][s]
                for blk in range(8):
                    P.matmul(self.banks[blk // 4][:, (blk % 4) * 128:(blk % 4) * 128 + 128],
                             lra[s][0:17, blk * 128:(blk + 1) * 128], w2b_[0:17, hp * 128:(hp + 1) * 128])
                sph = [V(sp, sp.ap[:, 4 * hf:4 * hf + 4, :].rearrange("p b c -> p (b c)")) for hf in range(2)]
                for hf in range(2):
                    P.act(sph[hf], self.banks[hf][:, 0:512], AF.Exp, scale=-1.0)
                for hf in range(2):
                    P.act(sph[hf], sph[hf], AF.Ln, bias=1.0)
                yield
                ga = prescan_half(hp, s, pp, [[0, 1], [4, 5]])
                gb = prescan_half(hp, s, pp, [[2, 3], [6, 7]])
                rnd = 0
                while ga is not None or gb is not None:
                    if ga is not None:
                        try:
                            next(ga)
                        except StopIteration:
                            ga = None
                    if gb is not None and rnd >= offset:
                        try:
                            next(gb)
                        except StopIteration:
                            gb = None
                    rnd += 1
                    yield

            def scan_gen(hp, s, pp):
                for q in range(nseq):
                    if g == 0:
                        P.memset(DVE, S[q][pp], 0.0)
                        P.memset(DVE, Sb[q][pp], 0.0)
                    else:
                        P.dma(SP, S[q][pp], self.dram["sg"][l, s, hp * 2:hp * 2 + 2].rearrange("h k v -> (h k) v"))
                        P.copy(ACT, Sb[q][pp], S[q][pp])
                nb = self.seqs[0][1]
                for step in range(nb):
                    for q, (b0, _) in enumerate(self.seqs):
                        blk = b0 + (step if s == 0 else nb - 1 - step)
                        St, Sbt = S[q][pp], Sb[q][pp]
                        po = self.banks[4 + sci[0] % 2]
                        sci[0] += 1
                        for a in range(2):
                            head = hp * 2 + a
                            P.matmul(po[:, a * 128:(a + 1) * 128], Sbt, qe[pp][a][blk], start=True, stop=False)
                            P.matmul(po[:, a * 128:(a + 1) * 128], gvtok[:, blk, head * 128:(head + 1) * 128], atT[pp][a][blk],
                                     start=False, stop=True)
                        for a in range(2):
                            head = hp * 2 + a
                            P.matmul(po[:, 256 + a * 128:384 + a * 128], kd[pp][:, blk, :],
                                     gvtok[:, blk, head * 128:(head + 1) * 128])
                        yield
                        for a in range(2):
                            ov = self.blkv(ogacc[a], blk)
                            if s == 0:
                                P.copy(ACT, ov, po[:, a * 128:(a + 1) * 128])
                            else:
                                P.tt(DVE, ov, ov, po[:, a * 128:(a + 1) * 128], ALU.add)
                        for a in range(2):
                            rws = slice(a * 64, a * 64 + 64)
                            P.stt(DVE, St[rws, :], St[rws, :], eBl[pp][rws, blk:blk + 1],
                                  po[rws, 256 + a * 128:384 + a * 128], ALU.mult, ALU.add)
                        P.copy(ACT, Sbt, St)
                        yield
                if g == 0:
                    for q in range(nseq):
                        P.dma(SP, self.outs["nsg"][q, l, s, hp * 2:hp * 2 + 2].rearrange("h k v -> (h k) v"), S[q][pp])

            def norm(hp):
                for a in range(2):
                    head = hp * 2 + a
                    for n in range(2):
                        b = self.bank()
                        self.proj(b, wr, head * 128, n)
                        P.act(grs[n], b, AF.Silu)
                    for n in range(2):
                        P.act(sq, ogacc[a][n], AF.Square)
                        bs_ = self.bank()
                        P.matmul(bs_, self.ones, sq)
                        P.act(rr, bs_, AF.Sqrt, bias=RMS_EPS, scale=1.0 / 128.0)
                        P.op(DVE, (lambda a_=rr.ap: lambda e: e.reciprocal(a_, a_))(), reads=[sp], writes=[sp])
                        P.tt(DVE, rr, ogacc[a][n], rr, ALU.mult)
                        P.stt(DVE, og[n][:, head, :], rr, v["gng"][:, 0:1], grs[n], ALU.mult, ALU.mult)

            def run(main, side=None, ratio=2):
                cnt = 0
                for _ in main:
                    cnt += 1
                    if side is not None and cnt % ratio == 0:
                        try:
                            next(side)
                        except StopIteration:
                            side = None
                if side is not None:
                    for _ in side:
                        pass

            self.bank_pool = [6, 7]
            run(prescan_gen(0, 0, 0))
            run(prescan_gen(0, 1, 1), scan_gen(0, 0, 0), ratio=1)
            run(prescan_gen(1, 0, 0), scan_gen(0, 1, 1), ratio=1)
            norm(0)
            run(prescan_gen(1, 1, 1), scan_gen(1, 0, 0), ratio=1)
            run(scan_gen(1, 1, 1))
            norm(1)
            self.bank_pool = None
            P.barrier()
            s2_.close()
            self.merge_branch(l, 2, self.dram["w_br_g"][l], lambda kc, n: og[n][:, kc, :])


WEIGHT_KEYS = ["w_ada", "b_ada", "ln_g", "ln_b", "ffn_w1", "ffn_w2", "w_in", "conv_a", "conv_qkv",
               "delta_a_log", "delta_dt_bias", "delta_norm_g", "gla_w2", "gla_b", "gla_norm_g",
               "w_br_a", "w_br_d", "w_br_g", "w_o"]


def make_in_maps(inputs, n_cores=8):
    f = lambda a: np.ascontiguousarray(np.asarray(a, dtype=np.float32))
    shared = {k: f(inputs[k]) for k in WEIGHT_KEYS}
    maps = []
    for c in range(n_cores):
        m = dict(shared)
        m["xp"] = f(inputs["x_prompt"][4 * c:4 * c + 4]).reshape(TG, D)
        m["xs"] = f(inputs["x_sample"][c]).reshape(TG, D)
        m["sd"] = f(inputs["state_delta"][c])
        m["sg"] = f(inputs["state_gla"][c])
        m["cond"] = f(np.stack([np.asarray(inputs["c_ctx"]), np.asarray(inputs["c"])[c]], 0))
        maps.append(m)
    return maps


_NC_CACHE = {}


def kernel(**inputs):
    if "nc" not in _NC_CACHE:
        _NC_CACHE["nc"] = Builder().nc
    nc = _NC_CACHE["nc"]
    maps = make_in_maps(inputs)
    res = run_bass_kernel_spmd(nc, maps, core_ids=list(range(8)))
    r = res.results
    y_prompt = np.concatenate([r[c]["yp"].reshape(4, 256, D) for c in range(8)], 0).astype(np.float32)
    y_sample = np.stack([r[c]["ys"].reshape(1024, D) for c in range(8)], 0).astype(np.float32)
    nsd = np.concatenate([r[c]["nsd"] for c in range(8)], 0).astype(np.float32)
    nsg = np.concatenate([r[c]["nsg"] for c in range(8)], 0).astype(np.float32)
    return (y_prompt, y_sample, nsd, nsg)
```

```python
import numpy as np
from contextlib import ExitStack
import concourse.bass as bass
import concourse.mybir as mybir
from concourse.bass_utils import run_bass_kernel_spmd

F32 = mybir.dt.float32
BF16 = mybir.dt.bfloat16
AF = mybir.ActivationFunctionType
ALU = mybir.AluOpType

PE, ACT, DVE, POOL, SP = "pe", "act", "dve", "pool", "sp"
COMPUTE = (PE, ACT, DVE, POOL)
DEBUG_SRC = False
_FW_NAMES = ("_record", "op", "dma", "matmul", "transpose", "act", "tt", "ts", "stt", "copy", "memset", "proj", "conv3", "<lambda>")


class T:
    def __init__(self, prog, h, name, psum=False):
        self.prog, self.h, self.name, self.psum = prog, h, name, psum
        self.last_write = None
        self.reads = []
        self.sem = None
        self.dma_n = 0

    def __getitem__(self, idx):
        return V(self, self.h[idx])

    @property
    def ap(self):
        return self.h[:] if not isinstance(self.h, bass.AP) else self.h


class V:
    def __init__(self, t, ap):
        self.t, self.ap = t, ap

    def __getitem__(self, idx):
        return V(self.t, self.ap[idx])


def _ap(x):
    return x.ap if isinstance(x, (V, T)) else x


def _tiles(xs):
    out = []
    for x in xs:
        if isinstance(x, V):
            out.append(x.t)
        elif isinstance(x, T):
            out.append(x)
    return out


class Ins:
    __slots__ = ("id", "eng", "fn", "deps", "is_dma", "dma_ev", "needed", "tick", "flushed", "src")


class Prog:
    def __init__(self, nc):
        self.nc = nc
        self.stack = ExitStack()
        self.ins = []
        self.pending = []
        self.engs = {PE: nc.tensor, ACT: nc.scalar, DVE: nc.vector, POOL: nc.gpsimd, SP: nc.sync}
        self.sem = {}
        for e in COMPUTE:
            self.sem[e] = self.stack.enter_context(nc.semaphore("s_" + e))
        self.ticks = {e: 0 for e in COMPUTE}
        self.waited = {e: {} for e in self.engs}
        self.nsem = 0
        self.out_events = []
        self.sp_events = {}
        self.last_needed = {e: None for e in COMPUTE}

    def sb(self, name, shape, dtype, stack=None):
        self.nalloc = getattr(self, "nalloc", 0) + 1
        name = "%s_%d" % (name, self.nalloc)
        h = (stack or self.stack).enter_context(self.nc.sbuf_tensor(name, list(shape), dtype))
        return T(self, h, name)

    def ps(self, name, shape, dtype=F32, stack=None):
        h = (stack or self.stack).enter_context(self.nc.psum_tensor(name, list(shape), dtype))
        return T(self, h, name, psum=True)

    def _dma_sem(self, t):
        if t.sem is None:
            t.sem = self.stack.enter_context(self.nc.semaphore("d%d_%s" % (self.nsem, t.name[:12])))
            self.nsem += 1
        return t.sem

    def _record(self, eng, fn, reads, writes, is_dma=False, dma_tile=None):
        i = Ins()
        i.id = len(self.ins)
        i.eng, i.fn, i.is_dma = eng, fn, is_dma
        i.deps = {}
        i.needed = False
        i.tick = None
        i.flushed = False
        i.dma_ev = None
        i.src = ""
        if DEBUG_SRC:
            import sys as _sys
            f = _sys._getframe(1)
            while f is not None and f.f_code.co_name in _FW_NAMES:
                f = f.f_back
            i.src = "L%d" % f.f_lineno if f is not None else ""
        rt, wt = _tiles(reads), _tiles(writes)
        if is_dma:
            sem = self._dma_sem(dma_tile)
            dma_tile.dma_n += 1
            i.dma_ev = ("d", sem, 16 * dma_tile.dma_n)
            ev = i.dma_ev
        else:
            ev = ("i", i.id)

        def add(e, kind):
            if e is None or e == ev:
                return
            if i.deps.get(e) != "raw":
                i.deps[e] = kind

        for t in rt:
            add(t.last_write, "raw")
            if t.psum:
                for r in t.reads:
                    add(r, "war")
        for t in wt:
            lw = t.last_write
            if not (is_dma and lw is not None and lw[0] == "d" and lw[1] is dma_tile.sem
                    and t is dma_tile and not t.reads):
                add(lw, "waw")
            for r in t.reads:
                add(r, "war")
        for t in rt:
            if t not in wt:
                t.reads.append(ev)
        for t in wt:
            t.last_write = ev
            t.reads = []
        self.ins.append(i)
        self.pending.append(i)
        return i

    def op(self, eng, fn, reads=(), writes=()):
        return self._record(eng, fn, reads, writes)

    def dma(self, eng, out, in_, **kw):
        on_chip = out if isinstance(out, (V, T)) else in_
        t = on_chip.t if isinstance(on_chip, V) else on_chip
        o, i_ = _ap(out), _ap(in_)
        sem = self._dma_sem(t)

        def fn(e):
            return e.dma_start(out=o, in_=i_, **kw)
        reads = [in_] if isinstance(in_, (V, T)) else []
        writes = [out] if isinstance(out, (V, T)) else []
        ins = self._record(eng, fn, reads, writes, is_dma=True, dma_tile=t)
        if not isinstance(out, (V, T)):
            self.out_events.append(ins.dma_ev)
        if eng == SP:
            self.sp_events[id(sem)] = (sem, ins.dma_ev[2])
        return ins

    def _resolve(self, e):
        if e[0] == "d":
            return e[1], e[2]
        p = self.ins[e[1]]
        return self.sem[p.eng], p

    def flush(self, final=False):
        pend = self.pending
        self.pending = []
        if not pend:
            return
        filt = {}
        for i in pend:
            keep = []
            for e, kind in i.deps.items():
                if e[0] == "i":
                    p = self.ins[e[1]]
                    if p.eng == i.eng and not i.is_dma:
                        if i.eng == PE:
                            continue
                    if not p.flushed:
                        p.needed = True
                keep.append(e)
            filt[i.id] = keep
        last = {}
        for i in pend:
            if not i.is_dma and i.eng in COMPUTE:
                last[i.eng] = i
        for i in last.values():
            i.needed = True
        for i in pend:
            if not i.is_dma and i.eng in COMPUTE and i.needed:
                self.ticks[i.eng] += 1
                i.tick = self.ticks[i.eng]
        nxt = {}
        for i in reversed(pend):
            if i.is_dma or i.eng not in COMPUTE:
                continue
            if i.tick is None:
                i.tick = nxt[i.eng]
            else:
                nxt[i.eng] = i.tick
        for i in pend:
            eh = self.engs[i.eng]
            need = {}
            for e in filt[i.id]:
                if e[0] == "d":
                    sem, val = e[1], e[2]
                else:
                    p = self.ins[e[1]]
                    sem, val = self.sem[p.eng], p.tick
                k = id(sem)
                if k not in need or need[k][1] < val:
                    need[k] = (sem, val)
            w = self.waited[i.eng]
            for k, (sem, val) in need.items():
                if w.get(k, 0) >= val:
                    continue
                eh.wait_ge(sem, val)
                w[k] = val
            b = i.fn(eh)
            if i.src:
                b.annotate(i.src)
            if i.is_dma:
                b.then_inc(i.dma_ev[1], 16)
            elif i.needed:
                b.then_inc(self.sem[i.eng], 1)
            i.flushed = True
            i.fn = None

    def finish(self):
        self.flush()
        need = {}
        for e in self.out_events:
            k = id(e[1])
            if k not in need or need[k][1] < e[2]:
                need[k] = (e[1], e[2])
        for sem, val in need.values():
            self.nc.sync.wait_ge(sem, val)

    def matmul(self, out, lhsT, rhs, start=True, stop=True, **kw):
        o, l, r = _ap(out), _ap(lhsT), _ap(rhs)
        return self.op(PE, lambda e: e.matmul(o, l, r, start=start, stop=stop, **kw),
                       reads=[lhsT, rhs], writes=[out])

    def transpose(self, out, in_, ident):
        o, i_, d = _ap(out), _ap(in_), _ap(ident)
        return self.op(PE, lambda e: e.transpose(o, i_, d), reads=[in_, ident], writes=[out])

    def act(self, out, in_, func, bias=None, scale=None, accum_out=None, eng=ACT):
        o, i_ = _ap(out), _ap(in_)
        kw = {}
        reads = [in_]
        writes = [out]
        if bias is not None:
            kw["bias"] = _ap(bias)
            reads.append(bias)
        if scale is not None:
            kw["scale"] = _ap(scale)
            reads.append(scale)
        if accum_out is not None:
            kw["accum_out"] = _ap(accum_out)
            writes.append(accum_out)
        return self.op(ACT, lambda e: e.activation(o, i_, func, **kw), reads=reads, writes=writes)

    def tt(self, eng, out, in0, in1, op):
        o, a, b = _ap(out), _ap(in0), _ap(in1)
        return self.op(eng, lambda e: e.tensor_tensor(o, a, b, op), reads=[in0, in1], writes=[out])

    def ts(self, eng, out, in0, s1, op0, s2=None, op1=None, accum_out=None):
        o, a = _ap(out), _ap(in0)
        reads = [in0]
        writes = [out]
        s1a, s2a = _ap(s1), _ap(s2)
        if isinstance(s1, (V, T)):
            reads.append(s1)
        if isinstance(s2, (V, T)):
            reads.append(s2)
        kw = {}
        if op1 is not None:
            kw["op1"] = op1
        if accum_out is not None:
            kw["accum_out"] = _ap(accum_out)
            writes.append(accum_out)
        return self.op(eng, lambda e: e.tensor_scalar(o, a, s1a, s2a, op0, **kw), reads=reads, writes=writes)

    def stt(self, eng, out, in0, scalar, in1, op0, op1):
        o, a, b = _ap(out), _ap(in0), _ap(in1)
        s = _ap(scalar)
        reads = [in0, in1]
        if isinstance(scalar, (V, T)):
            reads.append(scalar)
        return self.op(eng, lambda e: e.scalar_tensor_tensor(o, a, s, b, op0, op1), reads=reads, writes=[out])

    def copy(self, eng, out, in_):
        o, i_ = _ap(out), _ap(in_)
        if eng == ACT:
            return self.op(ACT, lambda e: e.copy(o, i_), reads=[in_], writes=[out])
        return self.op(eng, lambda e: e.tensor_copy(o, i_), reads=[in_], writes=[out])

    def memset(self, eng, out, val):
        o = _ap(out)
        return self.op(eng, lambda e: e.memset(o, val), reads=[], writes=[out])


    def barrier(self):
        self.flush()
        nc = self.nc
        for e in (PE, ACT, DVE, SP):
            eh = self.engs[e]
            w = self.waited[e]
            for e2 in COMPUTE:
                if e2 == e or self.ticks[e2] == 0:
                    continue
                k = id(self.sem[e2])
                if w.get(k, 0) >= self.ticks[e2]:
                    continue
                eh.wait_ge(self.sem[e2], self.ticks[e2])
                w[k] = self.ticks[e2]
            for k, (sem, val) in self.sp_events.items():
                if w.get(k, 0) >= val:
                    continue
                eh.wait_ge(sem, val)
                w[k] = val
D = 1024
DFF = 2816
NKC = 8
TG = 1024
NBLK = 8
DEPTH = 2
ALPHA = float((2 * DEPTH) ** 0.25)
LN_EPS = 1e-5
RMS_EPS = 1e-6
DPROJ = 8240
C_AX, C_AB, C_AC = 0, 512, 1024
C_DQ, C_DK, C_DV, C_DZ, C_DBETA, C_DA = 1536, 2048, 2560, 3072, 3584, 3592
C_GQ, C_GK, C_GV, C_GR, C_GLR, C_MG = 3600, 3856, 4112, 4624, 5136, 5168
NEG = -1.0e9
SLOT = 4096


def sub(v, offset_elems, dims):
    a = _ap(v)
    return bass.AP(a.tensor, a.offset + offset_elems, dims)


class _Mod2(list):
    def __getitem__(self, i):
        return list.__getitem__(self, i % 2)


class _Mod4(list):
    def __getitem__(self, i):
        return list.__getitem__(self, i % 4)


class Builder:
    def __init__(self, debug=None, skip=()):
        self.debug = debug or {}
        self.skip = skip
        nc = bass.Bass("TRN2", target_bir_lowering=False)
        self.nc = nc
        self.P = Prog(nc)
        self.dram = {}
        self.outs = {}
        self.declare()
        self.consts()
        self.vectors()
        self.ada_all()
        for g in range(2):
            self.group(g)
        self.P.finish()

    def din(self, name, shape):
        self.dram[name] = self.nc.dram_tensor(name, list(shape), F32, kind="ExternalInput").ap()

    def dout(self, name, shape):
        self.outs[name] = self.nc.dram_tensor(name, list(shape), F32, kind="ExternalOutput").ap()

    def declare(self):
        L = DEPTH
        self.din("xp", [TG, D]); self.din("xs", [TG, D])
        self.din("sd", [L, 2, 4, 128, 128]); self.din("sg", [L, 2, 4, 64, 128])
        self.din("cond", [2, D])
        self.din("w_ada", [L, D, 9 * D]); self.din("b_ada", [L, 9 * D])
        self.din("ln_g", [L, 3, D]); self.din("ln_b", [L, 3, D])
        self.din("ffn_w1", [L, 2, D, 2 * DFF]); self.din("ffn_w2", [L, 2, DFF, D])
        self.din("w_in", [L, D, DPROJ])
        self.din("conv_a", [L, 3, 512]); self.din("conv_qkv", [L, 3, 1536])
        self.din("delta_a_log", [L, 2, 4]); self.din("delta_dt_bias", [L, 2, 4])
        self.din("delta_norm_g", [L, 128])
        self.din("gla_w2", [L, 2, 16, 256]); self.din("gla_b", [L, 2, 256]); self.din("gla_norm_g", [L, 128])
        self.din("w_br_a", [L, 512, D]); self.din("w_br_d", [L, 512, D]); self.din("w_br_g", [L, 512, D])
        self.din("w_o", [L, D, D])
        self.dout("yp", [TG, D]); self.dout("ys", [TG, D])
        self.dout("nsd", [4, L, 2, 4, 128, 128]); self.dout("nsg", [4, L, 2, 4, 64, 128])
        for k, shp in self.debug.items():
            self.dout(k, shp)

    def bank(self):
        pool = getattr(self, "bank_pool", None)
        if pool:
            b = self.banks[pool[self.bank_i % len(pool)]]
        else:
            b = self.banks[self.bank_i % len(self.banks)]
        self.bank_i += 1
        return b

    def consts(self):
        P = self.P
        self.banks = [P.ps("bank%d" % i, [128, 512], F32) for i in range(8)]
        self.bank_i = 0
        self.slots = [P.sb("wslot%d" % i, [128, SLOT], BF16) for i in range(4)]
        self.slot_i = 0
        self.ident = P.sb("ident", [128, 128], F32)
        self.identb = P.sb("identb", [128, 128], BF16)
        self.ones = P.sb("ones", [128, 128], F32)
        P.memset(DVE, self.ones, 1.0)

        def sel(out_t, in_t, cmp, fill):
            o, i_ = _ap(out_t), _ap(in_t)
            cm, pat = 1, -1
            if cmp == ALU.is_le:
                cmp, cm, pat = ALU.is_ge, -1, 1
            elif cmp == ALU.is_lt:
                cmp, cm, pat = ALU.is_gt, -1, 1
            P.op(POOL, lambda e: e.affine_select(out=o, in_=i_, pattern=[[pat, 128]], compare_op=cmp,
                                                 fill=fill, base=0, channel_multiplier=cm),
                 reads=[in_t], writes=[out_t])
        sel(self.ident, self.ones, ALU.is_equal, 0.0)
        P.copy(DVE, self.identb, self.ident)
        mk = lambda n: P.sb(n, [128, 128], F32)
        self.NP = [P.sb("NP%d" % d, [128, 256], F32) for d in range(2)]
        self.negU = [self.NP[d][:, 0:128] for d in range(2)]
        self.posL = [self.NP[d][:, 128:256] for d in range(2)]
        self.m01 = [P.sb("m01_0", [128, 128], F32), P.sb("m01_1", [128, 128], F32)]
        self.triI = [mk("triI0"), mk("triI1")]
        self.gtriI = [mk("gtriI0"), mk("gtriI1")]
        self.gtriX = [mk("gtriX0"), mk("gtriX1")]
        self.bd16 = mk("bd16")
        self.mk = [P.sb("mk%d" % i, [128, 128], F32) for i in range(3)]
        cs_ = ExitStack()
        self.zeros = P.sb("zeros", [128, 128], F32, stack=cs_)
        negs = P.sb("negsixteenth", [128, 128], F32, stack=cs_)
        bd32 = P.sb("bd32", [128, 128], F32, stack=cs_)
        bd64 = P.sb("bd64", [128, 128], F32, stack=cs_)
        Et = {bsz: P.sb("E%d" % bsz, [128 // bsz, 128], F32, stack=cs_) for bsz in (16, 32, 64)}
        P.memset(DVE, self.zeros, 0.0)
        sel(self.negU[0], self.zeros, ALU.is_le, NEG)
        sel(self.negU[1], self.zeros, ALU.is_ge, NEG)
        sel(self.posL[0], self.zeros, ALU.is_gt, -NEG)
        sel(self.posL[1], self.zeros, ALU.is_lt, -NEG)
        sel(self.m01[0], self.ones, ALU.is_le, 0.0)
        sel(self.m01[1], self.ones, ALU.is_ge, 0.0)
        sel(self.triI[0], self.ones, ALU.is_le, 0.0)
        sel(self.triI[1], self.ones, ALU.is_ge, 0.0)
        for d in range(2):
            P.ts(DVE, self.gtriI[d], self.triI[d], -1.0 / 16.0, ALU.mult)
        P.memset(DVE, negs, -1.0 / 16.0)
        sel(self.gtriX[0], negs, ALU.is_gt, 0.0)
        sel(self.gtriX[1], negs, ALU.is_lt, 0.0)
        def blockdiag(bsz, t):
            nb_ = 128 // bsz
            E = Et[bsz]
            ea, oa = E.ap, self.ones[0:nb_, :].ap
            P.op(POOL, lambda e: e.affine_select(out=ea, in_=oa, pattern=[[1, 128]], compare_op=ALU.is_ge,
                                                 fill=0.0, base=0, channel_multiplier=-bsz), reads=[self.ones], writes=[E])
            P.op(POOL, lambda e: e.affine_select(out=ea, in_=ea, pattern=[[-1, 128]], compare_op=ALU.is_ge,
                                                 fill=0.0, base=bsz - 1, channel_multiplier=bsz), reads=[E], writes=[E])
            b = self.bank()
            P.matmul(b[:, 0:128], E, E)
            P.copy(DVE, t, b[:, 0:128])
            return t
        blockdiag(16, self.bd16)
        blockdiag(32, bd32)
        blockdiag(64, bd64)
        P.tt(DVE, self.mk[0], bd32, self.bd16, ALU.subtract)
        P.tt(DVE, self.mk[1], bd64, bd32, ALU.subtract)
        P.tt(DVE, self.mk[2], self.ones, bd64, ALU.subtract)
        P.barrier()
        cs_.close()

    def wtile(self, w2d, k0, nk, c0, cw=None):
        P = self.P
        ranges = c0 if isinstance(c0, list) else [(c0, cw)]
        tot = sum(r[1] for r in ranges)
        slot = self.slots[self.slot_i % len(self.slots)]
        self.slot_i += 1
        assert nk * tot <= SLOT
        dst = slot[:, 0:nk * tot].ap.rearrange("p (kc c) -> p kc c", kc=nk)
        off = 0
        for (a, w) in ranges:
            src = w2d[k0 * 128:(k0 + nk) * 128, a:a + w].rearrange("(kc p) c -> p kc c", p=128)
            P.dma(POOL, V(slot, dst[:, :, off:off + w]), src)
            off += w
        return V(slot, dst)

    def load_cols(self, dst_cols, rows_ap, nrows):
        P = self.P
        st = self.stg[self.stg_i % 2]
        self.stg_i += 1
        P.dma(SP, st[0:nrows, :], rows_ap)
        b = self.bank()
        P.transpose(b[:, 0:nrows], st[0:nrows, :], self.ident[0:nrows, 0:nrows])
        P.copy(DVE, dst_cols, b[:, 0:nrows])

    def vectors(self):
        P = self.P
        dr = self.dram
        self.stg = [P.sb("stg0", [128, 128], F32), P.sb("stg1", [128, 128], F32)]
        self.stg_i = 0
        self.vec = []
        for l in range(DEPTH):
            v = {}
            v["b_ada"] = P.sb("b_ada%d" % l, [128, 72], F32)
            self.load_cols(v["b_ada"][:, :], dr["b_ada"][l].rearrange("(r p) -> r p", p=128), 72)
            v["ln_g"] = P.sb("ln_g%d" % l, [128, 24], F32)
            self.load_cols(v["ln_g"][:, :], dr["ln_g"][l].rearrange("j (r p) -> (j r) p", p=128), 24)
            v["ln_b"] = P.sb("ln_b%d" % l, [128, 24], F32)
            self.load_cols(v["ln_b"][:, :], dr["ln_b"][l].rearrange("j (r p) -> (j r) p", p=128), 24)
            v["conv_a"] = P.sb("conv_a%d" % l, [128, 12], F32)
            self.load_cols(v["conv_a"][:, :], dr["conv_a"][l].rearrange("j (r p) -> (j r) p", p=128), 12)
            v["conv_qkv"] = P.sb("conv_qkv%d" % l, [128, 36], F32)
            self.load_cols(v["conv_qkv"][:, :], dr["conv_qkv"][l].rearrange("j (r p) -> (j r) p", p=128), 36)
            v["dng"] = P.sb("dng%d" % l, [128, 1], F32)
            self.load_cols(v["dng"][:, :], dr["delta_norm_g"][l:l + 1, :], 1)
            v["gng"] = P.sb("gng%d" % l, [128, 1], F32)
            self.load_cols(v["gng"][:, :], dr["gla_norm_g"][l:l + 1, :], 1)
            row = P.sb("dprow%d" % l, [1, 16], F32)
            P.dma(SP, row[0:1, 0:8], dr["delta_a_log"][l:l + 1].rearrange("o d h -> o (d h)"))
            P.dma(SP, row[0:1, 8:16], dr["delta_dt_bias"][l:l + 1].rearrange("o d h -> o (d h)"))
            b = self.bank()
            P.matmul(b[:, 0:16], self.ones[0:1, :], row[0:1, :])
            v["nega"] = P.sb("nega%d" % l, [128, 8], F32)
            v["dtb"] = P.sb("dtb%d" % l, [128, 8], F32)
            P.act(v["nega"], b[:, 0:8], AF.Exp)
            P.ts(DVE, v["nega"], v["nega"], -1.0, ALU.mult)
            P.copy(DVE, v["dtb"], b[:, 8:16])
            v["w2b"] = []
            for s_ in range(2):
                t = P.sb("w2b%d_%d" % (l, s_), [17, 256], BF16)
                P.dma(POOL, t[0:16, :], dr["gla_w2"][l, s_])
                P.dma(POOL, t[16:17, :], dr["gla_b"][l, s_:s_ + 1, :])
                v["w2b"].append(t)
            self.vec.append(v)
        cT = P.sb("condT", [128, 16], F32)
        self.load_cols(cT[:, :], dr["cond"].rearrange("i (r p) -> (i r) p", p=128), 16)
        self.condb = P.sb("condb", [128, 16], BF16)
        P.act(self.condb, cT, AF.Silu)
        P.flush()

    def ada_gen(self, l, bank_idx=None):
        P = self.P
        a = self.ada[l]
        w = self.dram["w_ada"][l]
        for cb in range(18):
            wt = self.wtile(w, 0, 8, cb * 512, 512)
            if cb % 4 == 0:
                bk = self.bank() if bank_idx is None else self.banks[bank_idx]
            for m in range(4):
                mi = cb * 4 + m
                o = bk[:, (mi % 16) * 2:(mi % 16) * 2 + 2]
                for kc in range(8):
                    P.matmul(o, wt[:, kc, m * 128:(m + 1) * 128], self.condb[:, kc:16:8],
                             start=(kc == 0), stop=(kc == 7))
            if cb % 4 == 3 or cb == 17:
                nm = 16 if cb % 4 == 3 else 8
                m0 = (cb // 4) * 16
                for i in range(2):
                    P.tt(DVE, a[:, i, m0:m0 + nm], bk[:, i:2 * nm:2], self.vec[l]["b_ada"][:, m0:m0 + nm], ALU.add)
            yield

    def ada_all(self):
        P = self.P
        self.ada = [P.sb("ada%d" % l, [128, 2, 72], F32) for l in range(DEPTH)]
        for _ in self.ada_gen(0):
            pass
        self.ada1_gen = self.ada_gen(1, bank_idx=7)
        P.flush()

    def group(self, g):
        P = self.P
        self.g = g
        x_dram = self.dram["xp" if g == 0 else "xs"]
        self.y_dram = self.outs["yp" if g == 0 else "ys"]
        self.segL = 256 if g == 0 else 64
        self.seqs = [(i * 2, 2) for i in range(4)] if g == 0 else [(0, 8)]
        with ExitStack() as gs:
            self.xa = [P.sb("xa%d" % n, [128, 8, 512], F32, stack=gs) for n in range(2)]
            self.h = [P.sb("h%d" % n, [128, 8, 512], BF16, stack=gs) for n in range(2)]
            self.dvec = P.sb("dvec", [128, 2, 3, 5, 8], F32, stack=gs)
            self.gs = gs
            self.derive_vectors(g, first=True)
            with ExitStack() as ps:
                xin = [P.sb("xin%d" % i, [128, D], F32, stack=ps) for i in range(2)]
                A0, B0 = self.first[:, 0, :], self.first[:, 1, :]
                for blk in range(8):
                    t = xin[blk % 2]
                    n, bs = blk // 4, slice((blk % 4) * 128, (blk % 4) * 128 + 128)
                    P.dma(SP, t, x_dram[blk * 128:(blk + 1) * 128, :])
                    for half in range(2):
                        b = self.bank()
                        for j in range(4):
                            c = half * 4 + j
                            P.transpose(b[:, j * 128:(j + 1) * 128], t[:, c * 128:(c + 1) * 128], self.ident)
                        for j in range(4):
                            c = half * 4 + j
                            P.ts(DVE, self.h[n][:, c, bs], b[:, j * 128:(j + 1) * 128], A0[:, c:c + 1], ALU.mult,
                                 B0[:, c:c + 1], ALU.add)
                        P.op(ACT, (lambda o_=self.xa[n][:, half * 4:half * 4 + 4, bs].ap,
                                   i_=b[:, 0:512].ap.rearrange("p (c t) -> p c t", c=4):
                                   lambda e: e.mul(o_, i_, ALPHA))(), reads=[b], writes=[self.xa[n]])
                P.barrier()
            for l in range(DEPTH):
                self.ffn(l, 0, 0)
                self.mixer(l)
                self.ffn(l, 1, 2)
            P.barrier()

    def derive_vectors(self, g, first):
        P = self.P
        dv = self.dvec
        if first:
            self.dvtmp = P.sb("dvtmp%d" % g, [128, 8], F32, stack=self.gs)
            self.first = P.sb("dvfirst%d" % g, [128, 2, 8], F32, stack=self.gs)
        tmp = self.dvtmp
        ada1_ready = (self.ada1_gen is None)
        for l in range(DEPTH):
            ad = self.ada[l]
            v = self.vec[l]
            for j in range(3):
                needs1 = (l == 1) or (l == 0 and j == 2)
                if first and needs1 and not ada1_ready:
                    if l == 0:
                        pass
                    else:
                        continue
                if (not first) and not needs1:
                    continue
                if (not first) and l == 0 and j == 2:
                    lg = v["ln_g"][:, j * 8:(j + 1) * 8]
                    lb = v["ln_b"][:, j * 8:(j + 1) * 8]
                    ad2 = self.ada[1]
                    sh = ad2[:, g, 0:8]
                    sc = ad2[:, g, 8:16]
                    P.ts(DVE, tmp, sc, 1.0, ALU.add)
                    P.tt(DVE, dv[:, l, j, 3, :], lg, tmp, ALU.mult)
                    P.tt(DVE, dv[:, l, j, 4, :], lb, tmp, ALU.mult)
                    P.tt(DVE, dv[:, l, j, 4, :], dv[:, l, j, 4, :], sh, ALU.add)
                    continue
                gt = ad[:, g, (3 * j + 2) * 8:(3 * j + 2) * 8 + 8]
                P.ts(DVE, dv[:, l, j, 2, :], gt, 0.5 if j != 1 else 1.0, ALU.mult)
                lg = v["ln_g"][:, j * 8:(j + 1) * 8]
                lb = v["ln_b"][:, j * 8:(j + 1) * 8]
                last = (l == DEPTH - 1 and j == 2)
                P.ts(DVE, dv[:, l, j, 0, :], lg, 1.0 if last else ALPHA, ALU.mult)
                P.ts(DVE, dv[:, l, j, 1, :], lb, 1.0 if last else ALPHA, ALU.mult)
                if last:
                    continue
                if first and l == 0 and j == 2 and not ada1_ready:
                    continue
                l2, j2 = (l, j + 1) if j < 2 else (l + 1, 0)
                ad2 = self.ada[l2]
                sh = ad2[:, g, (3 * j2) * 8:(3 * j2) * 8 + 8]
                sc = ad2[:, g, (3 * j2 + 1) * 8:(3 * j2 + 1) * 8 + 8]
                P.ts(DVE, tmp, sc, 1.0, ALU.add)
                P.tt(DVE, dv[:, l, j, 3, :], lg, tmp, ALU.mult)
                P.tt(DVE, dv[:, l, j, 4, :], lb, tmp, ALU.mult)
                P.tt(DVE, dv[:, l, j, 4, :], dv[:, l, j, 4, :], sh, ALU.add)
        if first:
            ad = self.ada[0]
            P.ts(DVE, self.first[:, 0, :], ad[:, g, 8:16], 1.0, ALU.add)
            P.copy(DVE, self.first[:, 1, :], ad[:, g, 0:8])
        P.flush()

    def ln_tiles(self, l, j):
        P = self.P
        dv = self.dvec
        last = (l == DEPTH - 1 and j == 2)
        with ExitStack() as ls:
            sq = [P.sb("lnsq%d" % i, [128, 512], BF16, stack=ls) for i in range(2)]
            onesb = P.sb("lnonesb", [128, 128], BF16, stack=ls)
            P.memset(DVE, onesb, 1.0)
            st = [P.sb("lnst%d" % i, [128, 512], F32, stack=ls) for i in range(4)]
            yt = [P.sb("lnyt%d" % i, [128, D], F32, stack=ls) for i in range(2)] if last else None
            for n in range(2):
                xa = self.xa[n]
                s1, s2 = self.bank(), self.bank()
                for c in range(8):
                    P.matmul(s1, self.ones, xa[:, c, :], start=(c == 0), stop=(c == 7))
                for c in range(8):
                    q = sq[c % 2]
                    P.act(q, xa[:, c, :], AF.Square)
                    P.matmul(s2, onesb, q, start=(c == 0), stop=(c == 7))
                mean, m2, var, nmr = st
                P.op(ACT, (lambda o_=mean.ap, i_=s1.ap: lambda e: e.mul(o_, i_, 1.0 / D))(), reads=[s1], writes=[mean])
                P.tt(DVE, m2, mean, mean, ALU.mult)
                P.stt(DVE, var, s2, 1.0 / D, m2, ALU.mult, ALU.subtract)
                P.act(var, var, AF.Sqrt, bias=LN_EPS)
                P.op(DVE, (lambda a_=var.ap: lambda e: e.reciprocal(a_, a_))(), reads=[var], writes=[var])
                rstd = var
                P.stt(DVE, nmr, mean, -1.0, rstd, ALU.mult, ALU.mult)
                for c in range(8):
                    xc = xa[:, c, :]
                    P.tt(DVE, xc, xc, rstd, ALU.mult)
                    P.tt(DVE, xc, xc, nmr, ALU.add)
                    if not last:
                        P.act(self.h[n][:, c, :], xc, AF.Identity, scale=dv[:, l, j, 3, c:c + 1], bias=dv[:, l, j, 4, c:c + 1])
                    if c % 2:
                        P.act(xc, xc, AF.Identity, scale=dv[:, l, j, 0, c:c + 1], bias=dv[:, l, j, 1, c:c + 1])
                    else:
                        P.ts(DVE, xc, xc, dv[:, l, j, 0, c:c + 1], ALU.mult, dv[:, l, j, 1, c:c + 1], ALU.add)
                if last:
                    for tb in range(4):
                        y = yt[tb % 2]
                        for half in range(2):
                            b = self.bank()
                            for jj in range(4):
                                c = half * 4 + jj
                                P.transpose(b[:, jj * 128:(jj + 1) * 128], xa[:, c, tb * 128:(tb + 1) * 128], self.ident)
                            P.copy(ACT if half == 0 else DVE, y[:, half * 512:(half + 1) * 512], b)
                        r0 = n * 512 + tb * 128
                        P.dma(SP, self.y_dram[r0:r0 + 128, :], y)
            P.barrier()

    def ffn(self, l, jj, j):
        P = self.P
        w1 = self.dram["ffn_w1"][l, jj]
        w2 = self.dram["ffn_w2"][l, jj]
        gcol = self.dvec[:, l, j, 2, :]
        with ExitStack() as fs:
            hid = [P.sb("hid%d" % n, [128, 22, 512], BF16, stack=fs) for n in range(2)]
            sgt = [P.sb("sgt%d" % i, [128, 512], BF16, stack=fs) for i in range(2)]
            it = 0
            side = self.ada1_gen
            if side is not None:
                self.bank_pool = [0, 1, 2, 3, 4, 5, 6]

            def side_step(k):
                for _ in range(k):
                    if self.ada1_gen is None:
                        return
                    try:
                        next(self.ada1_gen)
                    except StopIteration:
                        self.ada1_gen = None
            for hb in range(6):
                side_step(2)
                nm = 4 if hb < 5 else 2
                wg = self.wtile(w1, 0, 8, hb * 512, nm * 128)
                wu = self.wtile(w1, 0, 8, DFF + hb * 512, nm * 128)
                for n in range(2):
                    for m in range(nm):
                        bg, bu = self.bank(), self.bank()
                        for kc in range(8):
                            P.matmul(bg, wg[:, kc, m * 128:(m + 1) * 128], self.h[n][:, kc, :], start=(kc == 0), stop=(kc == 7))
                        for kc in range(8):
                            P.matmul(bu, wu[:, kc, m * 128:(m + 1) * 128], self.h[n][:, kc, :], start=(kc == 0), stop=(kc == 7))
                        s = sgt[it % 2]
                        it += 1
                        P.act(s, bg, AF.Silu)
                        P.tt(DVE, hid[n][:, hb * 4 + m, :], bu, s, ALU.mult)
            for cb in range(4):
                side_step(2)
                wA = self.wtile(w2, 0, 11, cb * 256, 256)
                wB = self.wtile(w2, 11, 11, cb * 256, 256)
                for n in range(2):
                    for m in range(2):
                        c = cb * 2 + m
                        by = self.bank()
                        for kc in range(22):
                            w = wA if kc < 11 else wB
                            P.matmul(by, w[:, kc % 11, m * 128:(m + 1) * 128], hid[n][:, kc, :], start=(kc == 0), stop=(kc == 21))
                        P.stt(DVE, self.xa[n][:, c, :], by, gcol[:, c:c + 1], self.xa[n][:, c, :], ALU.mult, ALU.add)
            if side is not None:
                side_step(99)
                self.bank_pool = None
                self.derive_vectors(self.g, first=False)
            self.ln_tiles(l, j)

    def conv3(self, o, p, wcols, row, nmul):
        P = self.P
        SL = self.segL
        w0, w1, w2 = (wcols[:, j * nmul + row:j * nmul + row + 1] for j in range(3))
        P.act(o, p, AF.Identity, scale=w1)
        o3 = V(o.t, o.ap.rearrange("p (s l) -> p s l", l=SL))
        p3 = V(p.t, p.ap.rearrange("p (s l) -> p s l", l=SL))
        P.stt(DVE, o3[:, :, 1:SL], p3[:, :, 0:SL - 1], w0, o3[:, :, 1:SL], ALU.mult, ALU.add)
        P.stt(DVE, o3[:, :, 0:SL - 1], p3[:, :, 1:SL], w2, o3[:, :, 0:SL - 1], ALU.mult, ALU.add)

    def proj(self, b, wv, col0, n, M=128):
        for kc in range(8):
            self.P.matmul(b[0:M, :], wv[:, kc, col0:col0 + M], self.h[n][:, kc, :], start=(kc == 0), stop=(kc == 7))

    def merge_branch(self, l, bi, br_w, src):
        P = self.P
        w_in = self.dram["w_in"][l]
        wbr = self.wtile(br_w, 0, 4, 0, 1024)
        with ExitStack() as s_:
            sig = [P.sb("sig%d" % i, [128, 512], F32, stack=s_) for i in range(2)]
            tmp = [P.sb("mtmp%d" % i, [128, 512], F32, stack=s_) for i in range(2)]
            it = 0
            for cb in range(2):
                wg = self.wtile(w_in, 0, 8, C_MG + bi * 1024 + cb * 512, 512)
                for n in range(2):
                    for m in range(4):
                        c = cb * 4 + m
                        bb, bg = self.bank(), self.bank()
                        for kc in range(4):
                            P.matmul(bb, wbr[:, kc, c * 128:(c + 1) * 128], src(kc, n), start=(kc == 0), stop=(kc == 3))
                        self.proj(bg, wg, m * 128, n)
                        sg = sig[it % 2]
                        P.act(sg, bg, AF.Sigmoid)
                        if bi == 0:
                            P.tt(DVE, self.merged[n][:, c, :], bb, sg, ALU.mult)
                        else:
                            t = tmp[it % 2]
                            P.tt(DVE, t, bb, sg, ALU.mult)
                            P.tt(DVE, self.merged[n][:, c, :], self.merged[n][:, c, :], t, ALU.add)
                        it += 1
            P.barrier()

    def mixer(self, l):
        P = self.P
        with ExitStack() as ms:
            self.merged = [P.sb("merged%d" % n, [128, 8, 512], F32, stack=ms) for n in range(2)]
            skip = getattr(self, "skip", ())
            if "a" not in skip:
                self.branch_a(l)
            if "d" not in skip:
                self.branch_d(l)
            if "g" not in skip:
                self.branch_g(l)
            for n in range(2):
                for c in range(8):
                    P.copy(ACT if c % 2 else DVE, self.h[n][:, c, :], self.merged[n][:, c, :])
            w_o = self.dram["w_o"][l]
            gcol = self.dvec[:, l, 1, 2, :]
            for cb in range(2):
                wo = self.wtile(w_o, 0, 8, cb * 512, 512)
                for n in range(2):
                    for m in range(4):
                        c = cb * 4 + m
                        by = self.bank()
                        self.proj(by, wo, m * 128, n)
                        P.stt(DVE, self.xa[n][:, c, :], by, gcol[:, c:c + 1], self.xa[n][:, c, :], ALU.mult, ALU.add)
            self.ln_tiles(l, 1)

    def branch_a(self, l):
        P = self.P
        w_in = self.dram["w_in"][l]
        cw = self.vec[l]["conv_a"]
        with ExitStack() as s_:
            ya = [P.sb("ya%d" % n, [128, 4, 512], BF16, stack=s_) for n in range(2)]
            s2_ = ExitStack()
            axs = [P.sb("axs%d" % i, [128, 512], F32, stack=s2_) for i in range(2)]
            pp = [P.sb("app%d" % i, [128, 512], F32, stack=s2_) for i in range(2)]
            oo = [P.sb("aoo%d" % i, [128, 512], F32, stack=s2_) for i in range(2)]
            wx = self.wtile(w_in, 0, 8, C_AX, 512)
            wb = self.wtile(w_in, 0, 8, C_AB, 512)
            wc = self.wtile(w_in, 0, 8, C_AC, 512)
            it = 0
            for n in range(2):
                for m in range(4):
                    bx, bc, bb = self.bank(), self.bank(), self.bank()
                    self.proj(bx, wx, m * 128, n)
                    self.proj(bc, wc, m * 128, n)
                    self.proj(bb, wb, m * 128, n)
                    a_, p_, o_ = axs[it % 2], pp[it % 2], oo[it % 2]
                    it += 1
                    P.copy(ACT, a_, bx)
                    P.tt(DVE, p_, bc, a_, ALU.mult)
                    self.conv3(o_[:, :], p_[:, :], cw, m, 4)
                    P.tt(DVE, ya[n][:, m, :], bb, o_, ALU.mult)
            P.barrier()
            s2_.close()
            self.merge_branch(l, 0, self.dram["w_br_a"][l], lambda kc, n: ya[n][:, kc, :])

    def blkv(self, tiles, blk, lead=None):
        n, b0 = blk // 4, (blk % 4) * 128
        if lead is None:
            return tiles[n][:, b0:b0 + 128]
        return tiles[n][:, lead, b0:b0 + 128]

    def branch_d(self, l):
        P = self.P
        g = self.g
        w_in = self.dram["w_in"][l]
        v = self.vec[l]
        cw = v["conv_qkv"]
        NBAT = 4
        with ExitStack() as s_:
            od = [P.sb("od%d" % n, [128, 4, 512], BF16, stack=s_) for n in range(2)]
            s2_ = ExitStack()
            sb = lambda name, shp, dt=F32: P.sb(name, shp, dt, stack=s2_)
            betaT = sb("betaT", [128, 8, 8]); gT = sb("gT", [128, 8, 8]); Gc = sb("Gc", [128, 8, 8])
            eGL = sb("eGL", [128, 8, 8]); bexpG = sb("bexpG", [128, 8, 8]); eGrev = sb("eGrev", [128, 8, 8])
            t8 = sb("t8", [128, 8, 8])
            wsm = self.wtile(w_in, 0, 8, C_DBETA, 16)
            bsm = self.bank()
            for blk in range(8):
                for kc in range(8):
                    P.matmul(bsm[:, blk * 16:(blk + 1) * 16], self.blkv(self.h, blk, kc), wsm[:, kc, :],
                             start=(kc == 0), stop=(kc == 7))
            bview = V(bsm, bsm.ap[:, 0:128].rearrange("p (b c) -> p b c", c=16))
            P.act(betaT, bview[:, :, 0:8], AF.Sigmoid)
            bc8 = lambda t: V(t, bass.AP(t.ap.tensor, t.ap.offset, [list(t.ap.ap[0]), [0, 8], [1, 8]]))
            P.tt(DVE, t8, bview[:, :, 8:16], bc8(v["dtb"]), ALU.add)
            P.act(t8, t8, AF.Exp)
            P.act(t8, t8, AF.Ln, bias=1.0)
            P.tt(DVE, gT, t8, bc8(v["nega"]), ALU.mult)
            bG = self.bank()
            gTd = [sb("gTd%d" % d, [128, 32]) for d in range(2)]
            for d in range(2):
                P.copy(DVE, V(gTd[d], gTd[d].ap.rearrange("p (b c) -> p b c", c=4)), gT[:, :, d * 4:d * 4 + 4])
                P.matmul(bG[:, d * 32:(d + 1) * 32], self.triI[d], gTd[d])
            for d in range(2):
                P.copy(DVE, Gc[:, :, d * 4:d * 4 + 4], V(bG, bG.ap[:, d * 32:(d + 1) * 32].rearrange("p (b c) -> p b c", c=4)))
            bL = self.bank()
            bLv = V(bL, bL.ap[:, 0:64].rearrange("p (b c) -> p b c", c=8))
            P.matmul(bL[:, 0:64], self.ones, V(gT, gT.ap.rearrange("p b c -> p (b c)")))
            P.act(eGL, bLv, AF.Exp)
            P.tt(DVE, t8, bLv, Gc, ALU.subtract)
            P.act(eGrev, t8, AF.Exp)
            P.act(t8, Gc, AF.Exp)
            P.tt(DVE, bexpG, betaT, t8, ALU.mult)
            qT = [sb("qT%d" % n, [128, 512], BF16) for n in range(2)]
            kT = [sb("kT%d" % n, [128, 512], BF16) for n in range(2)]
            zs = [sb("zs%d" % n, [128, 512], BF16) for n in range(2)]
            wqs = {}
            ktok = sb("ktok", [128, 8, 128], BF16); vtok = sb("vtok", [128, 8, 128], BF16)
            oacc = [sb("oacc%d" % n, [128, 512]) for n in range(2)]
            pre = sb("dpre", [128, 512]); cvo = sb("dcvo", [128, 512]); rr = sb("drr", [128, 512])
            vT = [V(rr, rr.ap.bitcast(BF16)[:, n * 512:(n + 1) * 512]) for n in range(2)]
            gU = [sb("gU%d" % i, [128, 128], BF16) for i in range(NBAT)]
            AB = [[sb("AB%d_%d" % (i, k), [128, 384]) for k in range(2)] for i in range(NBAT)]
            MN = [sb("MN%d" % i, [128, 256]) for i in range(NBAT)]
            CC = [sb("CC%d" % i, [128, 256]) for i in range(NBAT)]
            CCb = [V(CC[i], CC[i].ap.bitcast(BF16)[:, 0:256]) for i in range(NBAT)]
            WWb = [V(CC[i], CC[i].ap.bitcast(BF16)[:, 256:512]) for i in range(NBAT)]
            YY = [[V(AB[i][k], AB[i][k].ap.bitcast(BF16)[:, 0:256]) for k in range(2)] for i in range(NBAT)]
            Tt = [sb("Tt%d" % i, [128, 128], BF16) for i in range(NBAT)]
            vbk = [sb("vbk%d" % i, [128, 256], BF16) for i in range(2)]
            mk2 = lambda nm, dt=BF16, w=128: [[sb("%s%d_%d" % (nm, d, i), [128, w], dt) for i in range(8)] for d in range(2)]
            qdec, kdec, atT = mk2("qdec"), mk2("kdec"), mk2("atT")
            uw = mk2("uw", BF16, 256)
            usb = [[uw[d][i][:, 0:128] for i in range(8)] for d in range(2)]
            wTs = [[uw[d][i][:, 128:256] for i in range(8)] for d in range(2)]
            vnw = [sb("vnw%d" % i, [128, 128], BF16) for i in range(2)]
            nseq = len(self.seqs)
            S = [[sb("S%d_%d" % (q, d), [128, 128]) for d in range(2)] for q in range(nseq)]
            Sb = [[sb("Sb%d_%d" % (q, d), [128, 128], BF16) for d in range(2)] for q in range(nseq)]
            vi = [0]

            def proj_gen(hh):
                wq = self.wtile(w_in, 0, 8, [(C_DQ + hh * 128, 128), (C_DK + hh * 128, 128),
                                             (C_DV + hh * 128, 128), (C_DZ + hh * 128, 128)])
                wqs[hh] = wq
                for wi, dstT in ((0, qT), (1, kT), (2, vT)):
                    for n in range(2):
                        b = self.bank()
                        self.proj(b, wq, wi * 128, n)
                        P.copy(ACT, pre, b)
                        self.conv3(cvo[:, :], pre[:, :], cw, wi * 4 + hh, 12)
                        if wi == 2:
                            P.act(dstT[n], cvo, AF.Silu)
                        else:
                            P.act(cvo, cvo, AF.Silu)
                            P.act(pre, cvo, AF.Square)
                            bs_ = self.bank()
                            P.matmul(bs_, self.ones, pre)
                            P.act(rr, bs_, AF.Sqrt, bias=RMS_EPS)
                            P.op(DVE, (lambda a_=rr.ap: lambda e: e.reciprocal(a_, a_))(), reads=[rr], writes=[rr])
                            if wi == 0:
                                P.stt(DVE, dstT[n], cvo, 128.0 ** -0.5, rr, ALU.mult, ALU.mult)
                            else:
                                P.tt(DVE, dstT[n], cvo, rr, ALU.mult)
                        yield
                for blk in range(8):
                    b = self.bank()
                    bb16 = V(b, b.ap.bitcast(BF16))
                    P.transpose(bb16[:, 0:128], self.blkv(kT, blk), self.identb)
                    P.transpose(bb16[:, 128:256], self.blkv(vT, blk), self.identb)
                    P.copy(ACT, ktok[:, blk, :], bb16[:, 0:128])
                    P.copy(DVE, vtok[:, blk, :], bb16[:, 128:256])
                    if blk % 2:
                        yield

            def bc2(vw):
                a = _ap(vw)
                return V(vw.t, bass.AP(a.tensor, a.offset, [list(a.ap[0]), [0, 2], list(a.ap[1])]))

            def seg2(vw):
                a = _ap(vw)
                return V(vw.t, bass.AP(a.tensor, a.offset, [list(a.ap[0]), [256, 2], [1, 128]]))

            def h2(vw):
                a = _ap(vw)
                return V(vw.t, a.rearrange("p (a b) -> p a b", a=2))

            def prescan_gen(hh, d, batches):
                idx = d * 4 + hh
                for blks in batches:
                    col = lambda t, blk: t[:, blk, idx:idx + 1]
                    st = lambda blk: blk % NBAT
                    pg = {blk: self.banks[blk % NBAT] for blk in blks}
                    for blk in blks:
                        r2 = AB[st(blk)][1][:, 0:128]
                        P.ts(DVE, r2, self.triI[d], col(gT, blk), ALU.mult)
                        P.matmul(pg[blk][:, 0:128], self.ones, r2)
                        P.matmul(pg[blk][:, 128:256], self.blkv(kT, blk), self.blkv(kT, blk))
                    yield
                    for blk in blks:
                        P.stt(DVE, h2(CC[st(blk)][:, 0:256]), bc2(pg[blk][:, 0:128]), col(Gc, blk), h2(self.NP[d][:, 0:256]),
                              ALU.subtract, ALU.add)
                        P.act(AB[st(blk)][0][:, 0:128], pg[blk][:, 0:128], AF.Exp)
                    yield
                    for blk in blks:
                        P.act(gU[st(blk)], CC[st(blk)][:, 0:128], AF.Exp)
                        P.act(CC[st(blk)][:, 128:256], CC[st(blk)][:, 128:256], AF.Exp, scale=-1.0)
                        P.tt(DVE, qdec[d][blk], self.blkv(qT, blk), AB[st(blk)][0][:, 0:128], ALU.mult)
                        P.stt(DVE, MN[st(blk)][:, 0:128], pg[blk][:, 128:256], col(betaT, blk), CC[st(blk)][:, 128:256],
                              ALU.mult, ALU.mult)
                    yield
                    pb = {blk: self.banks[blk % NBAT] for blk in blks}
                    for blk in blks:
                        P.transpose(pb[blk][:, 0:128], MN[st(blk)][:, 0:128], self.ident)
                    for blk in blks:
                        P.copy(DVE, MN[st(blk)][:, 128:256], pb[blk][:, 0:128])
                    yield
                    for blk in blks:
                        P.tt(DVE, h2(CC[st(blk)][:, 0:256]), h2(MN[st(blk)][:, 0:256]), bc2(self.bd16[:, :]), ALU.mult)
                    for blk in blks:
                        M0, N0 = CC[st(blk)][:, 0:128], CC[st(blk)][:, 128:256]
                        P.tt(DVE, AB[st(blk)][1][:, 128:256], self.ident, N0, ALU.subtract)
                        P.matmul(pb[blk][:, 0:128], M0, N0)
                        P.matmul(pb[blk][:, 256:384], N0, M0)
                    yield
                    for blk in blks:
                        P.copy(ACT, seg2(AB[st(blk)][1][:, 0:384]), seg2(pb[blk][:, 0:384]))
                    for k in (1, 2):
                        cur, nxt = k % 2, (k + 1) % 2
                        for blk in blks:
                            A_ = AB[st(blk)][cur]
                            P.matmul(pb[blk][:, 0:256], A_[:, 256:384], A_[:, 0:256])
                            P.matmul(pb[blk][:, 256:384], A_[:, 0:128], A_[:, 256:384])
                        yield
                        for blk in blks:
                            P.copy(ACT, seg2(AB[st(blk)][nxt][:, 0:384]), seg2(pb[blk][:, 0:384]))
                            P.tt(DVE, AB[st(blk)][nxt][:, 128:256], AB[st(blk)][cur][:, 128:256], pb[blk][:, 128:256], ALU.add)
                    for blk in blks:
                        P.matmul(pb[blk][:, 0:128], AB[st(blk)][1][:, 256:384], AB[st(blk)][1][:, 128:256])
                    yield
                    for blk in blks:
                        pbb = V(pb[blk], pb[blk].ap.bitcast(BF16))
                        P.tt(DVE, YY[st(blk)][0][:, 128:256], AB[st(blk)][1][:, 128:256], pb[blk][:, 0:128], ALU.add)
                        P.transpose(pbb[:, 256:384], YY[st(blk)][0][:, 128:256], self.identb)
                    for blk in blks:
                        pbb = V(pb[blk], pb[blk].ap.bitcast(BF16))
                        P.copy(ACT, YY[st(blk)][0][:, 0:128], pbb[:, 256:384])
                    yield
                    for li in range(3):
                        mk = self.mk[li]
                        last = (li == 2)
                        for blk in blks:
                            P.tt(DVE, h2(CCb[st(blk)][:, 0:256]), h2(MN[st(blk)][:, 0:256]), bc2(mk[:, :]), ALU.mult)
                        for blk in blks:
                            cur = YY[st(blk)][li % 2]
                            Y, Yt = cur[:, 0:128], cur[:, 128:256]
                            C, Ct = CCb[st(blk)][:, 0:128], CCb[st(blk)][:, 128:256]
                            if not last:
                                P.matmul(pb[blk][:, 0:128], Ct, Y)
                            P.matmul(pb[blk][:, 128:256], C, Yt)
                        yield
                        for blk in blks:
                            if not last:
                                P.copy(ACT, WWb[st(blk)][:, 0:256], pb[blk][:, 0:256])
                            else:
                                P.copy(ACT, WWb[st(blk)][:, 128:256], pb[blk][:, 128:256])
                        for blk in blks:
                            cur = YY[st(blk)][li % 2]
                            Y, Yt = cur[:, 0:128], cur[:, 128:256]
                            if not last:
                                P.matmul(pb[blk][:, 256:384], Yt, WWb[st(blk)][:, 0:128])
                            P.matmul(pb[blk][:, 384:512], Y, WWb[st(blk)][:, 128:256])
                        yield
                        for blk in blks:
                            cur, nxt = YY[st(blk)][li % 2], YY[st(blk)][(li + 1) % 2]
                            if not last:
                                P.tt(DVE, nxt[:, 0:256], cur[:, 0:256], pb[blk][:, 256:512], ALU.subtract)
                            else:
                                P.tt(DVE, Tt[st(blk)], cur[:, 128:256], pb[blk][:, 384:512], ALU.subtract)
                    pu = {blk: self.banks[blk % NBAT] for blk in blks}
                    for blk in blks:
                        vk = vbk[blk % 2]
                        P.act(vk[:, 0:128], vtok[:, blk, :], AF.Copy, scale=col(betaT, blk))
                        P.act(vk[:, 128:256], ktok[:, blk, :], AF.Copy, scale=col(bexpG, blk))
                        P.act(kdec[d][blk], ktok[:, blk, :], AF.Copy, scale=col(eGrev, blk))
                        P.matmul(pu[blk][:, 0:128], Tt[st(blk)], vk[:, 0:128])
                        P.matmul(pu[blk][:, 128:256], vk[:, 128:256], Tt[st(blk)])
                        P.matmul(pu[blk][:, 256:384], self.blkv(kT, blk), self.blkv(qT, blk))
                    yield
                    for blk in blks:
                        P.copy(ACT, uw[d][blk], pu[blk][:, 0:256])
                        P.tt(DVE, atT[d][blk], pu[blk][:, 256:384], gU[st(blk)], ALU.mult)
                    yield

            def scan_gen(hh, d):
                idx = d * 4 + hh
                for q, (b0, nb) in enumerate(self.seqs):
                    if g == 0:
                        P.memset(DVE, S[q][d], 0.0)
                        P.memset(DVE, Sb[q][d], 0.0)
                    else:
                        P.dma(SP, S[q][d], self.dram["sd"][l, d, hh])
                        P.copy(ACT, Sb[q][d], S[q][d])
                nb = self.seqs[0][1]
                for step in range(nb):
                    for q, (b0, _) in enumerate(self.seqs):
                        blk = b0 + (step if d == 0 else nb - 1 - step)
                        St, Sbt = S[q][d], Sb[q][d]
                        p1 = self.banks[4 + vi[0] % 2]
                        P.matmul(p1[:, 0:128], wTs[d][blk], Sbt)
                        vn = vnw[vi[0] % 2]
                        vi[0] += 1
                        P.tt(DVE, vn, usb[d][blk], p1[:, 0:128], ALU.subtract)
                        yield
                        P.matmul(p1[:, 128:256], Sbt, qdec[d][blk], start=True, stop=False)
                        P.matmul(p1[:, 128:256], vn, atT[d][blk], start=False, stop=True)
                        P.matmul(p1[:, 256:384], kdec[d][blk], vn)
                        ov = self.blkv(oacc, blk)
                        if d == 0:
                            P.copy(ACT, ov, p1[:, 128:256])
                        else:
                            P.tt(DVE, ov, ov, p1[:, 128:256], ALU.add)
                        P.stt(DVE, St, St, eGL[:, blk, idx:idx + 1], p1[:, 256:384], ALU.mult, ALU.add)
                        P.copy(ACT, Sbt, St)
                        yield
                if g == 0:
                    for q in range(nseq):
                        P.dma(SP, self.outs["nsd"][q, l, d, hh], S[q][d])

            def norm(hh):
                for n in range(2):
                    b = self.bank()
                    self.proj(b, wqs[hh], 3 * 128, n)
                    P.act(zs[n], b, AF.Silu)
                for n in range(2):
                    P.act(pre, oacc[n], AF.Square)
                    bs_ = self.bank()
                    P.matmul(bs_, self.ones, pre)
                    P.act(rr, bs_, AF.Sqrt, bias=RMS_EPS, scale=1.0 / 128.0)
                    P.op(DVE, (lambda a_=rr.ap: lambda e: e.reciprocal(a_, a_))(), reads=[rr], writes=[rr])
                    P.tt(DVE, rr, oacc[n], rr, ALU.mult)
                    P.stt(DVE, od[n][:, hh, :], rr, v["dng"][:, 0:1], zs[n], ALU.mult, ALU.mult)

            def prescan2(hh, d, offset=1):
                ga = prescan_gen(hh, d, [[0, 1], [4, 5]])
                gb = prescan_gen(hh, d, [[2, 3], [6, 7]])
                rnd = 0
                while ga is not None or gb is not None:
                    if ga is not None:
                        try:
                            next(ga)
                        except StopIteration:
                            ga = None
                    if gb is not None and rnd >= offset:
                        try:
                            next(gb)
                        except StopIteration:
                            gb = None
                    rnd += 1
                    yield

            def run(main, side=None, ratio=2):
                cnt = 0
                for _ in main:
                    cnt += 1
                    if side is not None and cnt % ratio == 0:
                        try:
                            next(side)
                        except StopIteration:
                            side = None
                if side is not None:
                    for _ in side:
                        pass

            self.bank_pool = [6, 7]
            run(proj_gen(0))
            for hh in range(4):
                run(prescan2(hh, 0))
                run(prescan2(hh, 1), scan_gen(hh, 0), ratio=1)
                if hh < 3:
                    run(scan_gen(hh, 1), proj_gen(hh + 1), ratio=1)
                else:
                    run(scan_gen(hh, 1))
                norm(hh)
            self.bank_pool = None
            P.barrier()
            s2_.close()
            self.merge_branch(l, 1, self.dram["w_br_d"][l], lambda kc, n: od[n][:, kc, :])

    def branch_g(self, l):
        P = self.P
        g = self.g
        w_in = self.dram["w_in"][l]
        v = self.vec[l]
        NB4 = 4
        with ExitStack() as s_:
            og = [P.sb("og%d" % n, [128, 4, 512], BF16, stack=s_) for n in range(2)]
            s2_ = ExitStack()
            sb = lambda name, shp, dt=F32: P.sb(name, shp, dt, stack=s2_)
            gqT = [[sb("gqT%d_%d" % (hp, n), [128, 512], BF16) for n in range(2)] for hp in range(2)]
            gkT = [[sb("gkT%d_%d" % (hp, n), [128, 512], BF16) for n in range(2)] for hp in range(2)]
            gktok = sb("gktok", [128, 8, 256], BF16)
            gvtok = sb("gvtok", [128, 8, 512], BF16)
            lra = [sb("lra%d" % s, [17, 1024], BF16) for s in range(2)]
            ogacc = [[sb("ogacc%d_%d" % (a, n), [128, 512]) for n in range(2)] for a in range(2)]
            sp = sb("gsp", [128, 8, 128])
            kd = [sb("gkd%d" % i, [128, 8, 128], BF16) for i in range(2)]
            ek = [sb("gek_%d" % i, [128, 128], BF16) for i in range(NB4)]
            ebuf = sb("gebuf", [128, NB4, 128])
            eB = [ebuf[:, i, :] for i in range(NB4)]
            enB = [sb("genB_%d" % i, [128, 128], BF16) for i in range(NB4)]
            eBl = [sb("geBl%d" % i, [128, 8]) for i in range(2)]
            qe = [[[sb("gqe%d_%d_%d" % (pp, a, i), [128, 128], BF16) for i in range(8)] for a in range(2)] for pp in range(2)]
            qef = [sb("gqef%d" % i, [128, 128], BF16) for i in range(NB4)]
            rm = sb("grm", [128, 2])
            P.memset(DVE, rm, 0.0)
            P.memset(DVE, rm[0:64, 0:1], 1.0)
            P.memset(DVE, rm[64:128, 1:2], 1.0)
            ke = [sb("gke%d" % i, [128, 128], BF16) for i in range(NB4)]
            atT = [[[sb("gat%d_%d_%d" % (pp, a, i), [128, 128], BF16) for i in range(8)] for a in range(2)] for pp in range(2)]
            nseq = len(self.seqs)
            S = [[sb("gS%d_%d" % (q, pp), [128, 128]) for pp in range(2)] for q in range(nseq)]
            Sb = [[sb("gSb%d_%d" % (q, pp), [128, 128], BF16) for pp in range(2)] for q in range(nseq)]
            ebf = V(ebuf, ebuf.ap.rearrange("p b c -> p (b c)").bitcast(BF16))
            grs = [ebf[:, n * 512:(n + 1) * 512] for n in range(2)]
            spf = V(sp, sp.ap.rearrange("p b c -> p (b c)"))
            sq, rr = spf[:, 0:512], spf[:, 512:1024]

            wqk = self.wtile(w_in, 0, 8, C_GQ, 512)
            for hp in range(2):
                for n in range(2):
                    b = self.bank()
                    self.proj(b, wqk, hp * 128, n)
                    P.ts(DVE, gqT[hp][n], b, 0.125, ALU.mult)
                    b2 = self.bank()
                    self.proj(b2, wqk, 256 + hp * 128, n)
                    P.copy(ACT, gkT[hp][n], b2)
            for blk in range(8):
                b = self.bank()
                for kc in range(8):
                    P.matmul(b[:, 0:256], self.blkv(self.h, blk, kc), wqk[:, kc, 256:512], start=(kc == 0), stop=(kc == 7))
                P.copy(ACT, gktok[:, blk, :], b[:, 0:256])
            wv = self.wtile(w_in, 0, 8, C_GV, 512)
            for blk in range(8):
                b = self.bank()
                for kc in range(8):
                    P.matmul(b, self.blkv(self.h, blk, kc), wv[:, kc, :], start=(kc == 0), stop=(kc == 7))
                P.copy(DVE if blk % 2 else ACT, gvtok[:, blk, :], b)
            wl = self.wtile(w_in, 0, 8, C_GLR, 32)
            for s in range(2):
                P.memset(DVE, lra[s], 1.0)
                for n in range(2):
                    b = self.bank()
                    self.proj(b, wl, s * 16, n, M=16)
                    P.copy(ACT, lra[s][0:16, n * 512:(n + 1) * 512], b[0:16, :])
            wr = self.wtile(w_in, 0, 8, C_GR, 512)

            def prescan_half(hp, s, pp, batches):
                w2b = v["w2b"][s]
                for blks in batches:
                    st = lambda blk: blk % NB4
                    bl = {blk: self.banks[blk % NB4] for blk in blks}
                    for blk in blks:
                        P.matmul(bl[blk][:, 128:256], sp[:, blk, :], self.gtriI[s])
                        P.matmul(bl[blk][:, 256:384], self.gtriX[s], sp[:, blk, :])
                    yield
                    lc = 127 if s == 0 else 0
                    for blk in blks:
                        P.act(eB[st(blk)], bl[blk][:, 128:256], AF.Exp)
                        P.act(enB[st(blk)], bl[blk][:, 128:256], AF.Exp, scale=-1.0)
                        P.act(ek[st(blk)], bl[blk][:, 256:384], AF.Exp)
                    yield
                    for blk in blks:
                        eb, enb = eB[st(blk)], enB[st(blk)]
                        P.copy(DVE, eBl[pp][:, blk:blk + 1], eb[:, lc:lc + 1])
                        P.tt(DVE, qef[st(blk)], self.blkv(gqT[hp], blk), eb, ALU.mult)
                        for a in range(2):
                            P.ts(DVE, qe[pp][a][blk], qef[st(blk)], rm[:, a:a + 1], ALU.mult)
                        P.tt(DVE, ke[st(blk)], self.blkv(gkT[hp], blk), enb, ALU.mult)
                        P.tt(DVE, kd[pp][:, blk, :], gktok[:, blk, hp * 128:(hp + 1) * 128], ek[st(blk)], ALU.mult)
                    yield
                    pa = {blk: self.banks[blk % NB4] for blk in blks}
                    for blk in blks:
                        for a in range(2):
                            P.matmul(pa[blk][:, a * 128:(a + 1) * 128], ke[st(blk)], qe[pp][a][blk])
                    yield
                    for blk in blks:
                        for a in range(2):
                            P.tt(DVE, atT[pp][a][blk], pa[blk][:, a * 128:(a + 1) * 128], self.m01[s], ALU.mult)
                    yield

            sci = [0]

            def prescan_gen(hp, s, pp, offset=1):
                w2b_ = v["w2b"][s]
                for blk in range(8):
                    P.matmul(self.banks[blk // 4][:, (blk % 4) * 128:(blk % 4) * 128 + 128],
                             lra[s][0:17, blk * 128:(blk + 1) * 128], w2b_[0:17, hp * 128:(hp + 1) * 128])
                sph = [V(sp, sp.ap[:, 4 * hf:4 * hf + 4, :].rearrange("p b c -> p (b c)")) for hf in range(2)]
                for hf in range(2):
                    P.act(sph[hf], self.banks[hf][:, 0:512], AF.Exp, scale=-1.0)
                for hf in range(2):
                    P.act(sph[hf], sph[hf], AF.Ln, bias=1.0)
                yield
                ga = prescan_half(hp, s, pp, [[0, 1], [4, 5]])
                gb = prescan_half(hp, s, pp, [[2, 3], [6, 7]])
                rnd = 0
                while ga is not None or gb is not None:
                    if ga is not None:
                        try:
                            next(ga)
                        except StopIteration:
                            ga = None
                    if gb is not None and rnd >= offset:
                        try:
                            next(gb)
                        except StopIteration:
                            gb = None
                    rnd += 1
                    yield

            def scan_gen(hp, s, pp):
                for q in range(nseq):
                    if g == 0:
                        P.memset(DVE, S[q][pp], 0.0)
                        P.memset(DVE, Sb[q][pp], 0.0)
                    else:
                        P.dma(SP, S[q][pp], self.dram["sg"][l, s, hp * 2:hp * 2 + 2].rearrange("h k v -> (h k) v"))
                        P.copy(ACT, Sb[q][pp], S[q][pp])
                nb = self.seqs[0][1]
                for step in range(nb):
                    for q, (b0, _) in enumerate(self.seqs):
                        blk = b0 + (step if s == 0 else nb - 1 - step)
                        St, Sbt = S[q][pp], Sb[q][pp]
                        po = self.banks[4 + sci[0] % 2]
                        sci[0] += 1
                        for a in range(2):
                            head = hp * 2 + a
                            P.matmul(po[:, a * 128:(a + 1) * 128], Sbt, qe[pp][a][blk], start=True, stop=False)
                            P.matmul(po[:, a * 128:(a + 1) * 128], gvtok[:, blk, head * 128:(head + 1) * 128], atT[pp][a][blk],
                                     start=False, stop=True)
                        for a in range(2):
                            head = hp * 2 + a
                            P.matmul(po[:, 256 + a * 128:384 + a * 128], kd[pp][:, blk, :],
                                     gvtok[:, blk, head * 128:(head + 1) * 128])
                        yield
                        for a in range(2):
                            ov = self.blkv(ogacc[a], blk)
                            if s == 0:
                                P.copy(ACT, ov, po[:, a * 128:(a + 1) * 128])
                            else:
                                P.tt(DVE, ov, ov, po[:, a * 128:(a + 1) * 128], ALU.add)
                        for a in range(2):
                            rws = slice(a * 64, a * 64 + 64)
                            P.stt(DVE, St[rws, :], St[rws, :], eBl[pp][rws, blk:blk + 1],
                                  po[rws, 256 + a * 128:384 + a * 128], ALU.mult, ALU.add)
                        P.copy(ACT, Sbt, St)
                        yield
                if g == 0:
                    for q in range(nseq):
                        P.dma(SP, self.outs["nsg"][q, l, s, hp * 2:hp * 2 + 2].rearrange("h k v -> (h k) v"), S[q][pp])

            def norm(hp):
                for a in range(2):
                    head = hp * 2 + a
                    for n in range(2):
                        b = self.bank()
                        self.proj(b, wr, head * 128, n)
                        P.act(grs[n], b, AF.Silu)
                    for n in range(2):
                        P.act(sq, ogacc[a][n], AF.Square)
                        bs_ = self.bank()
                        P.matmul(bs_, self.ones, sq)
                        P.act(rr, bs_, AF.Sqrt, bias=RMS_EPS, scale=1.0 / 128.0)
                        P.op(DVE, (lambda a_=rr.ap: lambda e: e.reciprocal(a_, a_))(), reads=[sp], writes=[sp])
                        P.tt(DVE, rr, ogacc[a][n], rr, ALU.mult)
                        P.stt(DVE, og[n][:, head, :], rr, v["gng"][:, 0:1], grs[n], ALU.mult, ALU.mult)

            def run(main, side=None, ratio=2):
                cnt = 0
                for _ in main:
                    cnt += 1
                    if side is not None and cnt % ratio == 0:
                        try:
                            next(side)
                        except StopIteration:
                            side = None
                if side is not None:
                    for _ in side:
                        pass

            self.bank_pool = [6, 7]
            run(prescan_gen(0, 0, 0))
            run(prescan_gen(0, 1, 1), scan_gen(0, 0, 0), ratio=1)
            run(prescan_gen(1, 0, 0), scan_gen(0, 1, 1), ratio=1)
            norm(0)
            run(prescan_gen(1, 1, 1), scan_gen(1, 0, 0), ratio=1)
            run(scan_gen(1, 1, 1))
            norm(1)
            self.bank_pool = None
            P.barrier()
            s2_.close()
            self.merge_branch(l, 2, self.dram["w_br_g"][l], lambda kc, n: og[n][:, kc, :])


WEIGHT_KEYS = ["w_ada", "b_ada", "ln_g", "ln_b", "ffn_w1", "ffn_w2", "w_in", "conv_a", "conv_qkv",
               "delta_a_log", "delta_dt_bias", "delta_norm_g", "gla_w2", "gla_b", "gla_norm_g",
               "w_br_a", "w_br_d", "w_br_g", "w_o"]


def make_in_maps(inputs, n_cores=8):
    f = lambda a: np.ascontiguousarray(np.asarray(a, dtype=np.float32))
    shared = {k: f(inputs[k]) for k in WEIGHT_KEYS}
    maps = []
    for c in range(n_cores):
        m = dict(shared)
        m["xp"] = f(inputs["x_prompt"][4 * c:4 * c + 4]).reshape(TG, D)
        m["xs"] = f(inputs["x_sample"][c]).reshape(TG, D)
        m["sd"] = f(inputs["state_delta"][c])
        m["sg"] = f(inputs["state_gla"][c])
        m["cond"] = f(np.stack([np.asarray(inputs["c_ctx"]), np.asarray(inputs["c"])[c]], 0))
        maps.append(m)
    return maps


_NC_CACHE = {}


def kernel(**inputs):
    if "nc" not in _NC_CACHE:
        _NC_CACHE["nc"] = Builder().nc
    nc = _NC_CACHE["nc"]
    maps = make_in_maps(inputs)
    res = run_bass_kernel_spmd(nc, maps, core_ids=list(range(8)))
    r = res.results
    y_prompt = np.concatenate([r[c]["yp"].reshape(4, 256, D) for c in range(8)], 0).astype(np.float32)
    y_sample = np.stack([r[c]["ys"].reshape(1024, D) for c in range(8)], 0).astype(np.float32)
    nsd = np.concatenate([r[c]["nsd"] for c in range(8)], 0).astype(np.float32)
    nsg = np.concatenate([r[c]["nsg"] for c in range(8)], 0).astype(np.float32)
    return (y_prompt, y_sample, nsd, nsg)
```

```python
import numpy as np
from contextlib import ExitStack
import concourse.bass as bass
import concourse.mybir as mybir
from concourse.bass_utils import run_bass_kernel_spmd

F32 = mybir.dt.float32
BF16 = mybir.dt.bfloat16
AF = mybir.ActivationFunctionType
ALU = mybir.AluOpType

PE, ACT, DVE, POOL, SP = "pe", "act", "dve", "pool", "sp"
COMPUTE = (PE, ACT, DVE, POOL)
DEBUG_SRC = False
_FW_NAMES = ("_record", "op", "dma", "matmul", "transpose", "act", "tt", "ts", "stt", "copy", "memset", "proj", "conv3", "<lambda>")


class T:
    def __init__(self, prog, h, name, psum=False):
        self.prog, self.h, self.name, self.psum = prog, h, name, psum
        self.last_write = None
        self.reads = []
        self.sem = None
        self.dma_n = 0

    def __getitem__(self, idx):
        return V(self, self.h[idx])

    @property
    def ap(self):
        return self.h[:] if not isinstance(self.h, bass.AP) else self.h


class V:
    def __init__(self, t, ap):
        self.t, self.ap = t, ap

    def __getitem__(self, idx):
        return V(self.t, self.ap[idx])


def _ap(x):
    return x.ap if isinstance(x, (V, T)) else x


def _tiles(xs):
    out = []
    for x in xs:
        if isinstance(x, V):
            out.append(x.t)
        elif isinstance(x, T):
            out.append(x)
    return out


class Ins:
    __slots__ = ("id", "eng", "fn", "deps", "is_dma", "dma_ev", "needed", "tick", "flushed", "src")


class Prog:
    def __init__(self, nc):
        self.nc = nc
        self.stack = ExitStack()
        self.ins = []
        self.pending = []
        self.engs = {PE: nc.tensor, ACT: nc.scalar, DVE: nc.vector, POOL: nc.gpsimd, SP: nc.sync}
        self.sem = {}
        for e in COMPUTE:
            self.sem[e] = self.stack.enter_context(nc.semaphore("s_" + e))
        self.ticks = {e: 0 for e in COMPUTE}
        self.waited = {e: {} for e in self.engs}
        self.nsem = 0
        self.out_events = []
        self.sp_events = {}
        self.last_needed = {e: None for e in COMPUTE}

    def sb(self, name, shape, dtype, stack=None):
        self.nalloc = getattr(self, "nalloc", 0) + 1
        name = "%s_%d" % (name, self.nalloc)
        h = (stack or self.stack).enter_context(self.nc.sbuf_tensor(name, list(shape), dtype))
        return T(self, h, name)

    def ps(self, name, shape, dtype=F32, stack=None):
        h = (stack or self.stack).enter_context(self.nc.psum_tensor(name, list(shape), dtype))
        return T(self, h, name, psum=True)

    def _dma_sem(self, t):
        if t.sem is None:
            t.sem = self.stack.enter_context(self.nc.semaphore("d%d_%s" % (self.nsem, t.name[:12])))
            self.nsem += 1
        return t.sem

    def _record(self, eng, fn, reads, writes, is_dma=False, dma_tile=None):
        i = Ins()
        i.id = len(self.ins)
        i.eng, i.fn, i.is_dma = eng, fn, is_dma
        i.deps = {}
        i.needed = False
        i.tick = None
        i.flushed = False
        i.dma_ev = None
        i.src = ""
        if DEBUG_SRC:
            import sys as _sys
            f = _sys._getframe(1)
            while f is not None and f.f_code.co_name in _FW_NAMES:
                f = f.f_back
            i.src = "L%d" % f.f_lineno if f is not None else ""
        rt, wt = _tiles(reads), _tiles(writes)
        if is_dma:
            sem = self._dma_sem(dma_tile)
            dma_tile.dma_n += 1
            i.dma_ev = ("d", sem, 16 * dma_tile.dma_n)
            ev = i.dma_ev
        else:
            ev = ("i", i.id)

        def add(e, kind):
            if e is None or e == ev:
                return
            if i.deps.get(e) != "raw":
                i.deps[e] = kind

        for t in rt:
            add(t.last_write, "raw")
            if t.psum:
                for r in t.reads:
                    add(r, "war")
        for t in wt:
            lw = t.last_write
            if not (is_dma and lw is not None and lw[0] == "d" and lw[1] is dma_tile.sem
                    and t is dma_tile and not t.reads):
                add(lw, "waw")
            for r in t.reads:
                add(r, "war")
        for t in rt:
            if t not in wt:
                t.reads.append(ev)
        for t in wt:
            t.last_write = ev
            t.reads = []
        self.ins.append(i)
        self.pending.append(i)
        return i

    def op(self, eng, fn, reads=(), writes=()):
        return self._record(eng, fn, reads, writes)

    def dma(self, eng, out, in_, **kw):
        on_chip = out if isinstance(out, (V, T)) else in_
        t = on_chip.t if isinstance(on_chip, V) else on_chip
        o, i_ = _ap(out), _ap(in_)
        sem = self._dma_sem(t)

        def fn(e):
            return e.dma_start(out=o, in_=i_, **kw)
        reads = [in_] if isinstance(in_, (V, T)) else []
        writes = [out] if isinstance(out, (V, T)) else []
        ins = self._record(eng, fn, reads, writes, is_dma=True, dma_tile=t)
        if not isinstance(out, (V, T)):
            self.out_events.append(ins.dma_ev)
        if eng == SP:
            self.sp_events[id(sem)] = (sem, ins.dma_ev[2])
        return ins

    def _resolve(self, e):
        if e[0] == "d":
            return e[1], e[2]
        p = self.ins[e[1]]
        return self.sem[p.eng], p

    def flush(self, final=False):
        pend = self.pending
        self.pending = []
        if not pend:
            return
        filt = {}
        for i in pend:
            keep = []
            for e, kind in i.deps.items():
                if e[0] == "i":
                    p = self.ins[e[1]]
                    if p.eng == i.eng and not i.is_dma:
                        if i.eng == PE:
                            continue
                    if not p.flushed:
                        p.needed = True
                keep.append(e)
            filt[i.id] = keep
        last = {}
        for i in pend:
            if not i.is_dma and i.eng in COMPUTE:
                last[i.eng] = i
        for i in last.values():
            i.needed = True
        for i in pend:
            if not i.is_dma and i.eng in COMPUTE and i.needed:
                self.ticks[i.eng] += 1
                i.tick = self.ticks[i.eng]
        nxt = {}
        for i in reversed(pend):
            if i.is_dma or i.eng not in COMPUTE:
                continue
            if i.tick is None:
                i.tick = nxt[i.eng]
            else:
                nxt[i.eng] = i.tick
        for i in pend:
            eh = self.engs[i.eng]
            need = {}
            for e in filt[i.id]:
                if e[0] == "d":
                    sem, val = e[1], e[2]
                else:
                    p = self.ins[e[1]]
                    sem, val = self.sem[p.eng], p.tick
                k = id(sem)
                if k not in need or need[k][1] < val:
                    need[k] = (sem, val)
            w = self.waited[i.eng]
            for k, (sem, val) in need.items():
                if w.get(k, 0) >= val:
                    continue
                eh.wait_ge(sem, val)
                w[k] = val
            b = i.fn(eh)
            if i.src:
                b.annotate(i.src)
            if i.is_dma:
                b.then_inc(i.dma_ev[1], 16)
            elif i.needed:
                b.then_inc(self.sem[i.eng], 1)
            i.flushed = True
            i.fn = None

    def finish(self):
        self.flush()
        need = {}
        for e in self.out_events:
            k = id(e[1])
            if k not in need or need[k][1] < e[2]:
                need[k] = (e[1], e[2])
        for sem, val in need.values():
            self.nc.sync.wait_ge(sem, val)

    def matmul(self, out, lhsT, rhs, start=True, stop=True, **kw):
        o, l, r = _ap(out), _ap(lhsT), _ap(rhs)
        return self.op(PE, lambda e: e.matmul(o, l, r, start=start, stop=stop, **kw),
                       reads=[lhsT, rhs], writes=[out])

    def transpose(self, out, in_, ident):
        o, i_, d = _ap(out), _ap(in_), _ap(ident)
        return self.op(PE, lambda e: e.transpose(o, i_, d), reads=[in_, ident], writes=[out])

    def act(self, out, in_, func, bias=None, scale=None, accum_out=None, eng=ACT):
        o, i_ = _ap(out), _ap(in_)
        kw = {}
        reads = [in_]
        writes = [out]
        if bias is not None:
            kw["bias"] = _ap(bias)
            reads.append(bias)
        if scale is not None:
            kw["scale"] = _ap(scale)
            reads.append(scale)
        if accum_out is not None:
            kw["accum_out"] = _ap(accum_out)
            writes.append(accum_out)
        return self.op(ACT, lambda e: e.activation(o, i_, func, **kw), reads=reads, writes=writes)

    def tt(self, eng, out, in0, in1, op):
        o, a, b = _ap(out), _ap(in0), _ap(in1)
        return self.op(eng, lambda e: e.tensor_tensor(o, a, b, op), reads=[in0, in1], writes=[out])

    def ts(self, eng, out, in0, s1, op0, s2=None, op1=None, accum_out=None):
        o, a = _ap(out), _ap(in0)
        reads = [in0]
        writes = [out]
        s1a, s2a = _ap(s1), _ap(s2)
        if isinstance(s1, (V, T)):
            reads.append(s1)
        if isinstance(s2, (V, T)):
            reads.append(s2)
        kw = {}
        if op1 is not None:
            kw["op1"] = op1
        if accum_out is not None:
            kw["accum_out"] = _ap(accum_out)
            writes.append(accum_out)
        return self.op(eng, lambda e: e.tensor_scalar(o, a, s1a, s2a, op0, **kw), reads=reads, writes=writes)

    def stt(self, eng, out, in0, scalar, in1, op0, op1):
        o, a, b = _ap(out), _ap(in0), _ap(in1)
        s = _ap(scalar)
        reads = [in0, in1]
        if isinstance(scalar, (V, T)):
            reads.append(scalar)
        return self.op(eng, lambda e: e.scalar_tensor_tensor(o, a, s, b, op0, op1), reads=reads, writes=[out])

    def copy(self, eng, out, in_):
        o, i_ = _ap(out), _ap(in_)
        if eng == ACT:
            return self.op(ACT, lambda e: e.copy(o, i_), reads=[in_], writes=[out])
        return self.op(eng, lambda e: e.tensor_copy(o, i_), reads=[in_], writes=[out])

    def memset(self, eng, out, val):
        o = _ap(out)
        return self.op(eng, lambda e: e.memset(o, val), reads=[], writes=[out])


    def barrier(self):
        self.flush()
        nc = self.nc
        for e in (PE, ACT, DVE, SP):
            eh = self.engs[e]
            w = self.waited[e]
            for e2 in COMPUTE:
                if e2 == e or self.ticks[e2] == 0:
                    continue
                k = id(self.sem[e2])
                if w.get(k, 0) >= self.ticks[e2]:
                    continue
                eh.wait_ge(self.sem[e2], self.ticks[e2])
                w[k] = self.ticks[e2]
            for k, (sem, val) in self.sp_events.items():
                if w.get(k, 0) >= val:
                    continue
                eh.wait_ge(sem, val)
                w[k] = val
D = 1024
DFF = 2816
NKC = 8
TG = 1024
NBLK = 8
DEPTH = 2
ALPHA = float((2 * DEPTH) ** 0.25)
LN_EPS = 1e-5
RMS_EPS = 1e-6
DPROJ = 8240
C_AX, C_AB, C_AC = 0, 512, 1024
C_DQ, C_DK, C_DV, C_DZ, C_DBETA, C_DA = 1536, 2048, 2560, 3072, 3584, 3592
C_GQ, C_GK, C_GV, C_GR, C_GLR, C_MG = 3600, 3856, 4112, 4624, 5136, 5168
NEG = -1.0e9
SLOT = 4096


def sub(v, offset_elems, dims):
    a = _ap(v)
    return bass.AP(a.tensor, a.offset + offset_elems, dims)


class _Mod2(list):
    def __getitem__(self, i):
        return list.__getitem__(self, i % 2)


class _Mod4(list):
    def __getitem__(self, i):
        return list.__getitem__(self, i % 4)


class Builder:
    def __init__(self, debug=None, skip=()):
        self.debug = debug or {}
        self.skip = skip
        nc = bass.Bass("TRN2", target_bir_lowering=False)
        self.nc = nc
        self.P = Prog(nc)
        self.dram = {}
        self.outs = {}
        self.declare()
        self.consts()
        self.vectors()
        self.ada_all()
        for g in range(2):
            self.group(g)
        self.P.finish()

    def din(self, name, shape):
        self.dram[name] = self.nc.dram_tensor(name, list(shape), F32, kind="ExternalInput").ap()

    def dout(self, name, shape):
        self.outs[name] = self.nc.dram_tensor(name, list(shape), F32, kind="ExternalOutput").ap()

    def declare(self):
        L = DEPTH
        self.din("xp", [TG, D]); self.din("xs", [TG, D])
        self.din("sd", [L, 2, 4, 128, 128]); self.din("sg", [L, 2, 4, 64, 128])
        self.din("cond", [2, D])
        self.din("w_ada", [L, D, 9 * D]); self.din("b_ada", [L, 9 * D])
        self.din("ln_g", [L, 3, D]); self.din("ln_b", [L, 3, D])
        self.din("ffn_w1", [L, 2, D, 2 * DFF]); self.din("ffn_w2", [L, 2, DFF, D])
        self.din("w_in", [L, D, DPROJ])
        self.din("conv_a", [L, 3, 512]); self.din("conv_qkv", [L, 3, 1536])
        self.din("delta_a_log", [L, 2, 4]); self.din("delta_dt_bias", [L, 2, 4])
        self.din("delta_norm_g", [L, 128])
        self.din("gla_w2", [L, 2, 16, 256]); self.din("gla_b", [L, 2, 256]); self.din("gla_norm_g", [L, 128])
        self.din("w_br_a", [L, 512, D]); self.din("w_br_d", [L, 512, D]); self.din("w_br_g", [L, 512, D])
        self.din("w_o", [L, D, D])
        self.dout("yp", [TG, D]); self.dout("ys", [TG, D])
        self.dout("nsd", [4, L, 2, 4, 128, 128]); self.dout("nsg", [4, L, 2, 4, 64, 128])
        for k, shp in self.debug.items():
            self.dout(k, shp)

    def bank(self):
        pool = getattr(self, "bank_pool", None)
        if pool:
            b = self.banks[pool[self.bank_i % len(pool)]]
        else:
            b = self.banks[self.bank_i % len(self.banks)]
        self.bank_i += 1
        return b

    def consts(self):
        P = self.P
        self.banks = [P.ps("bank%d" % i, [128, 512], F32) for i in range(8)]
        self.bank_i = 0
        self.slots = [P.sb("wslot%d" % i, [128, SLOT], BF16) for i in range(4)]
        self.slot_i = 0
        self.ident = P.sb("ident", [128, 128], F32)
        self.identb = P.sb("identb", [128, 128], BF16)
        self.onesb = P.sb("onesb", [128, 128], BF16)
        self.ones = P.sb("ones", [128, 128], F32)
        P.memset(DVE, self.ones, 1.0)
        P.memset(DVE, self.onesb, 1.0)

        def sel(out_t, in_t, cmp, fill):
            o, i_ = _ap(out_t), _ap(in_t)
            cm, pat = 1, -1
            if cmp == ALU.is_le:
                cmp, cm, pat = ALU.is_ge, -1, 1
            elif cmp == ALU.is_lt:
                cmp, cm, pat = ALU.is_gt, -1, 1
            P.op(POOL, lambda e: e.affine_select(out=o, in_=i_, pattern=[[pat, 128]], compare_op=cmp,
                                                 fill=fill, base=0, channel_multiplier=cm),
                 reads=[in_t], writes=[out_t])
        sel(self.ident, self.ones, ALU.is_equal, 0.0)
        P.copy(DVE, self.identb, self.ident)
        mk = lambda n: P.sb(n, [128, 128], F32)
        self.NP = [P.sb("NP%d" % d, [128, 256], F32) for d in range(2)]
        self.negU = [self.NP[d][:, 0:128] for d in range(2)]
        self.posL = [self.NP[d][:, 128:256] for d in range(2)]
        self.m01 = [P.sb("m01_0", [128, 128], F32), P.sb("m01_1", [128, 128], F32)]
        self.triI = [mk("triI0"), mk("triI1")]
        self.gtriI = [mk("gtriI0"), mk("gtriI1")]
        self.gtriX = [mk("gtriX0"), mk("gtriX1")]
        self.bd16 = mk("bd16")
        self.mk = [P.sb("mk%d" % i, [128, 128], F32) for i in range(3)]
        cs_ = ExitStack()
        self.zeros = P.sb("zeros", [128, 128], F32, stack=cs_)
        negs = P.sb("negsixteenth", [128, 128], F32, stack=cs_)
        bd32 = P.sb("bd32", [128, 128], F32, stack=cs_)
        bd64 = P.sb("bd64", [128, 128], F32, stack=cs_)
        Et = {bsz: P.sb("E%d" % bsz, [128 // bsz, 128], F32, stack=cs_) for bsz in (16, 32, 64)}
        P.memset(DVE, self.zeros, 0.0)
        sel(self.negU[0], self.zeros, ALU.is_le, NEG)
        sel(self.negU[1], self.zeros, ALU.is_ge, NEG)
        sel(self.posL[0], self.zeros, ALU.is_gt, -NEG)
        sel(self.posL[1], self.zeros, ALU.is_lt, -NEG)
        sel(self.m01[0], self.ones, ALU.is_le, 0.0)
        sel(self.m01[1], self.ones, ALU.is_ge, 0.0)
        sel(self.triI[0], self.ones, ALU.is_le, 0.0)
        sel(self.triI[1], self.ones, ALU.is_ge, 0.0)
        for d in range(2):
            P.ts(DVE, self.gtriI[d], self.triI[d], -1.0 / 16.0, ALU.mult)
        P.memset(DVE, negs, -1.0 / 16.0)
        sel(self.gtriX[0], negs, ALU.is_gt, 0.0)
        sel(self.gtriX[1], negs, ALU.is_lt, 0.0)
        def blockdiag(bsz, t):
            nb_ = 128 // bsz
            E = Et[bsz]
            ea, oa = E.ap, self.ones[0:nb_, :].ap
            P.op(POOL, lambda e: e.affine_select(out=ea, in_=oa, pattern=[[1, 128]], compare_op=ALU.is_ge,
                                                 fill=0.0, base=0, channel_multiplier=-bsz), reads=[self.ones], writes=[E])
            P.op(POOL, lambda e: e.affine_select(out=ea, in_=ea, pattern=[[-1, 128]], compare_op=ALU.is_ge,
                                                 fill=0.0, base=bsz - 1, channel_multiplier=bsz), reads=[E], writes=[E])
            b = self.bank()
            P.matmul(b[:, 0:128], E, E)
            P.copy(DVE, t, b[:, 0:128])
            return t
        blockdiag(16, self.bd16)
        blockdiag(32, bd32)
        blockdiag(64, bd64)
        P.tt(DVE, self.mk[0], bd32, self.bd16, ALU.subtract)
        P.tt(DVE, self.mk[1], bd64, bd32, ALU.subtract)
        P.tt(DVE, self.mk[2], self.ones, bd64, ALU.subtract)
        P.barrier()
        cs_.close()

    def wtile(self, w2d, k0, nk, c0, cw=None):
        P = self.P
        ranges = c0 if isinstance(c0, list) else [(c0, cw)]
        tot = sum(r[1] for r in ranges)
        slot = self.slots[self.slot_i % len(self.slots)]
        self.slot_i += 1
        assert nk * tot <= SLOT
        dst = slot[:, 0:nk * tot].ap.rearrange("p (kc c) -> p kc c", kc=nk)
        off = 0
        for (a, w) in ranges:
            src = w2d[k0 * 128:(k0 + nk) * 128, a:a + w].rearrange("(kc p) c -> p kc c", p=128)
            P.dma(POOL, V(slot, dst[:, :, off:off + w]), src)
            off += w
        return V(slot, dst)

    def load_cols(self, dst_cols, rows_ap, nrows):
        P = self.P
        st = self.stg[self.stg_i % 2]
        self.stg_i += 1
        P.dma(SP, st[0:nrows, :], rows_ap)
        b = self.bank()
        P.transpose(b[:, 0:nrows], st[0:nrows, :], self.ident[0:nrows, 0:nrows])
        P.copy(DVE, dst_cols, b[:, 0:nrows])

    def vectors(self):
        P = self.P
        dr = self.dram
        self.stg = [P.sb("stg0", [128, 128], F32), P.sb("stg1", [128, 128], F32)]
        self.stg_i = 0
        self.vec = []
        for l in range(DEPTH):
            v = {}
            v["b_ada"] = P.sb("b_ada%d" % l, [128, 72], F32)
            self.load_cols(v["b_ada"][:, :], dr["b_ada"][l].rearrange("(r p) -> r p", p=128), 72)
            v["ln_g"] = P.sb("ln_g%d" % l, [128, 24], F32)
            self.load_cols(v["ln_g"][:, :], dr["ln_g"][l].rearrange("j (r p) -> (j r) p", p=128), 24)
            v["ln_b"] = P.sb("ln_b%d" % l, [128, 24], F32)
            self.load_cols(v["ln_b"][:, :], dr["ln_b"][l].rearrange("j (r p) -> (j r) p", p=128), 24)
            v["conv_a"] = P.sb("conv_a%d" % l, [128, 12], F32)
            self.load_cols(v["conv_a"][:, :], dr["conv_a"][l].rearrange("j (r p) -> (j r) p", p=128), 12)
            v["conv_qkv"] = P.sb("conv_qkv%d" % l, [128, 36], F32)
            self.load_cols(v["conv_qkv"][:, :], dr["conv_qkv"][l].rearrange("j (r p) -> (j r) p", p=128), 36)
            v["dng"] = P.sb("dng%d" % l, [128, 1], F32)
            self.load_cols(v["dng"][:, :], dr["delta_norm_g"][l:l + 1, :], 1)
            v["gng"] = P.sb("gng%d" % l, [128, 1], F32)
            self.load_cols(v["gng"][:, :], dr["gla_norm_g"][l:l + 1, :], 1)
            row = P.sb("dprow%d" % l, [1, 16], F32)
            P.dma(SP, row[0:1, 0:8], dr["delta_a_log"][l:l + 1].rearrange("o d h -> o (d h)"))
            P.dma(SP, row[0:1, 8:16], dr["delta_dt_bias"][l:l + 1].rearrange("o d h -> o (d h)"))
            b = self.bank()
            P.matmul(b[:, 0:16], self.ones[0:1, :], row[0:1, :])
            v["nega"] = P.sb("nega%d" % l, [128, 8], F32)
            v["dtb"] = P.sb("dtb%d" % l, [128, 8], F32)
            P.act(v["nega"], b[:, 0:8], AF.Exp)
            P.ts(DVE, v["nega"], v["nega"], -1.0, ALU.mult)
            P.copy(DVE, v["dtb"], b[:, 8:16])
            v["w2b"] = []
            for s_ in range(2):
                t = P.sb("w2b%d_%d" % (l, s_), [17, 256], BF16)
                P.dma(POOL, t[0:16, :], dr["gla_w2"][l, s_])
                P.dma(POOL, t[16:17, :], dr["gla_b"][l, s_:s_ + 1, :])
                v["w2b"].append(t)
            self.vec.append(v)
        cT = P.sb("condT", [128, 16], F32)
        self.load_cols(cT[:, :], dr["cond"].rearrange("i (r p) -> (i r) p", p=128), 16)
        self.condb = P.sb("condb", [128, 16], BF16)
        P.act(self.condb, cT, AF.Silu)
        P.flush()

    def ada_gen(self, l, bank_idx=None):
        P = self.P
        a = self.ada[l]
        w = self.dram["w_ada"][l]
        for cb in range(18):
            wt = self.wtile(w, 0, 8, cb * 512, 512)
            if cb % 4 == 0:
                bk = self.bank() if bank_idx is None else self.banks[bank_idx]
            for m in range(4):
                mi = cb * 4 + m
                o = bk[:, (mi % 16) * 2:(mi % 16) * 2 + 2]
                for kc in range(8):
                    P.matmul(o, wt[:, kc, m * 128:(m + 1) * 128], self.condb[:, kc:16:8],
                             start=(kc == 0), stop=(kc == 7))
            if cb % 4 == 3 or cb == 17:
                nm = 16 if cb % 4 == 3 else 8
                m0 = (cb // 4) * 16
                for i in range(2):
                    P.tt(DVE, a[:, i, m0:m0 + nm], bk[:, i:2 * nm:2], self.vec[l]["b_ada"][:, m0:m0 + nm], ALU.add)
            yield

    def ada_all(self):
        P = self.P
        self.ada = [P.sb("ada%d" % l, [128, 2, 72], F32) for l in range(DEPTH)]
        for _ in self.ada_gen(0):
            pass
        self.ada1_gen = self.ada_gen(1, bank_idx=7)
        P.flush()

    def group(self, g):
        P = self.P
        self.g = g
        x_dram = self.dram["xp" if g == 0 else "xs"]
        self.y_dram = self.outs["yp" if g == 0 else "ys"]
        self.segL = 256 if g == 0 else 64
        self.seqs = [(i * 2, 2) for i in range(4)] if g == 0 else [(0, 8)]
        with ExitStack() as gs:
            self.xa = [P.sb("xa%d" % n, [128, 8, 512], F32, stack=gs) for n in range(2)]
            self.h = [P.sb("h%d" % n, [128, 8, 512], BF16, stack=gs) for n in range(2)]
            self.dvec = P.sb("dvec", [128, 2, 3, 5, 8], F32, stack=gs)
            self.gs = gs
            self.derive_vectors(g, first=True)
            with ExitStack() as ps:
                xin = [P.sb("xin%d" % i, [128, D], F32, stack=ps) for i in range(2)]
                A0, B0 = self.first[:, 0, :], self.first[:, 1, :]
                for blk in range(8):
                    t = xin[blk % 2]
                    n, bs = blk // 4, slice((blk % 4) * 128, (blk % 4) * 128 + 128)
                    P.dma(SP, t, x_dram[blk * 128:(blk + 1) * 128, :])
                    for half in range(2):
                        b = self.bank()
                        for j in range(4):
                            c = half * 4 + j
                            P.transpose(b[:, j * 128:(j + 1) * 128], t[:, c * 128:(c + 1) * 128], self.ident)
                        for j in range(4):
                            c = half * 4 + j
                            P.ts(DVE, self.h[n][:, c, bs], b[:, j * 128:(j + 1) * 128], A0[:, c:c + 1], ALU.mult,
                                 B0[:, c:c + 1], ALU.add)
                        P.op(ACT, (lambda o_=self.xa[n][:, half * 4:half * 4 + 4, bs].ap,
                                   i_=b[:, 0:512].ap.rearrange("p (c t) -> p c t", c=4):
                                   lambda e: e.mul(o_, i_, ALPHA))(), reads=[b], writes=[self.xa[n]])
                P.barrier()
            for l in range(DEPTH):
                self.ffn(l, 0, 0)
                self.mixer(l)
                self.ffn(l, 1, 2)
            P.barrier()

    def derive_vectors(self, g, first):
        P = self.P
        dv = self.dvec
        if first:
            self.dvtmp = P.sb("dvtmp%d" % g, [128, 8], F32, stack=self.gs)
            self.first = P.sb("dvfirst%d" % g, [128, 2, 8], F32, stack=self.gs)
        tmp = self.dvtmp
        ada1_ready = (self.ada1_gen is None)
        for l in range(DEPTH):
            ad = self.ada[l]
            v = self.vec[l]
            for j in range(3):
                needs1 = (l == 1) or (l == 0 and j == 2)
                if first and needs1 and not ada1_ready:
                    if l == 0:
                        pass
                    else:
                        continue
                if (not first) and not needs1:
                    continue
                if (not first) and l == 0 and j == 2:
                    lg = v["ln_g"][:, j * 8:(j + 1) * 8]
                    lb = v["ln_b"][:, j * 8:(j + 1) * 8]
                    ad2 = self.ada[1]
                    sh = ad2[:, g, 0:8]
                    sc = ad2[:, g, 8:16]
                    P.ts(DVE, tmp, sc, 1.0, ALU.add)
                    P.tt(DVE, dv[:, l, j, 3, :], lg, tmp, ALU.mult)
                    P.tt(DVE, dv[:, l, j, 4, :], lb, tmp, ALU.mult)
                    P.tt(DVE, dv[:, l, j, 4, :], dv[:, l, j, 4, :], sh, ALU.add)
                    continue
                gt = ad[:, g, (3 * j + 2) * 8:(3 * j + 2) * 8 + 8]
                P.ts(DVE, dv[:, l, j, 2, :], gt, 0.5 if j != 1 else 1.0, ALU.mult)
                lg = v["ln_g"][:, j * 8:(j + 1) * 8]
                lb = v["ln_b"][:, j * 8:(j + 1) * 8]
                last = (l == DEPTH - 1 and j == 2)
                P.ts(DVE, dv[:, l, j, 0, :], lg, 1.0 if last else ALPHA, ALU.mult)
                P.ts(DVE, dv[:, l, j, 1, :], lb, 1.0 if last else ALPHA, ALU.mult)
                if last:
                    continue
                if first and l == 0 and j == 2 and not ada1_ready:
                    continue
                l2, j2 = (l, j + 1) if j < 2 else (l + 1, 0)
                ad2 = self.ada[l2]
                sh = ad2[:, g, (3 * j2) * 8:(3 * j2) * 8 + 8]
                sc = ad2[:, g, (3 * j2 + 1) * 8:(3 * j2 + 1) * 8 + 8]
                P.ts(DVE, tmp, sc, 1.0, ALU.add)
                P.tt(DVE, dv[:, l, j, 3, :], lg, tmp, ALU.mult)
                P.tt(DVE, dv[:, l, j, 4, :], lb, tmp, ALU.mult)
                P.tt(DVE, dv[:, l, j, 4, :], dv[:, l, j, 4, :], sh, ALU.add)
        if first:
            ad = self.ada[0]
            P.ts(DVE, self.first[:, 0, :], ad[:, g, 8:16], 1.0, ALU.add)
            P.copy(DVE, self.first[:, 1, :], ad[:, g, 0:8])
        P.flush()

    def ln_tiles(self, l, j):
        P = self.P
        dv = self.dvec
        last = (l == DEPTH - 1 and j == 2)
        with ExitStack() as ls:
            sq = [P.sb("lnsq%d" % i, [128, 512], BF16, stack=ls) for i in range(2)]
            onesb = self.onesb
            st = [P.sb("lnst%d" % i, [128, 512], F32, stack=ls) for i in range(4)]
            yt = [P.sb("lnyt%d" % i, [128, D], F32, stack=ls) for i in range(2)] if last else None
            for n in range(2):
                xa = self.xa[n]
                s1, s2 = self.bank(), self.bank()
                for c in range(8):
                    P.matmul(s1, self.ones, xa[:, c, :], start=(c == 0), stop=(c == 7))
                for c in range(8):
                    q = sq[c % 2]
                    P.act(q, xa[:, c, :], AF.Square)
                    P.matmul(s2, onesb, q, start=(c == 0), stop=(c == 7))
                mean, m2, var, nmr = st
                P.op(ACT, (lambda o_=mean.ap, i_=s1.ap: lambda e: e.mul(o_, i_, 1.0 / D))(), reads=[s1], writes=[mean])
                P.tt(DVE, m2, mean, mean, ALU.mult)
                P.stt(DVE, var, s2, 1.0 / D, m2, ALU.mult, ALU.subtract)
                P.act(var, var, AF.Sqrt, bias=LN_EPS)
                P.op(DVE, (lambda a_=var.ap: lambda e: e.reciprocal(a_, a_))(), reads=[var], writes=[var])
                rstd = var
                P.stt(DVE, nmr, mean, -1.0, rstd, ALU.mult, ALU.mult)
                for c in range(8):
                    xc = xa[:, c, :]
                    P.tt(DVE, xc, xc, rstd, ALU.mult)
                    P.tt(DVE, xc, xc, nmr, ALU.add)
                    if not last:
                        P.act(self.h[n][:, c, :], xc, AF.Identity, scale=dv[:, l, j, 3, c:c + 1], bias=dv[:, l, j, 4, c:c + 1])
                    if c % 2:
                        P.act(xc, xc, AF.Identity, scale=dv[:, l, j, 0, c:c + 1], bias=dv[:, l, j, 1, c:c + 1])
                    else:
                        P.ts(DVE, xc, xc, dv[:, l, j, 0, c:c + 1], ALU.mult, dv[:, l, j, 1, c:c + 1], ALU.add)
                if last:
                    for tb in range(4):
                        y = yt[tb % 2]
                        for half in range(2):
                            b = self.bank()
                            for jj in range(4):
                                c = half * 4 + jj
                                P.transpose(b[:, jj * 128:(jj + 1) * 128], xa[:, c, tb * 128:(tb + 1) * 128], self.ident)
                            P.copy(ACT if half == 0 else DVE, y[:, half * 512:(half + 1) * 512], b)
                        r0 = n * 512 + tb * 128
                        P.dma(SP, self.y_dram[r0:r0 + 128, :], y)
            P.barrier()

    def ffn(self, l, jj, j):
        P = self.P
        w1 = self.dram["ffn_w1"][l, jj]
        w2 = self.dram["ffn_w2"][l, jj]
        gcol = self.dvec[:, l, j, 2, :]
        with ExitStack() as fs:
            hid = [P.sb("hid%d" % n, [128, 22, 512], BF16, stack=fs) for n in range(2)]
            sgt = [P.sb("sgt%d" % i, [128, 512], BF16, stack=fs) for i in range(2)]
            it = 0
            side = self.ada1_gen
            if side is not None:
                self.bank_pool = [0, 1, 2, 3, 4, 5, 6]

            def side_step(k):
                for _ in range(k):
                    if self.ada1_gen is None:
                        return
                    try:
                        next(self.ada1_gen)
                    except StopIteration:
                        self.ada1_gen = None
            for hb in range(6):
                side_step(2)
                nm = 4 if hb < 5 else 2
                wg = self.wtile(w1, 0, 8, hb * 512, nm * 128)
                wu = self.wtile(w1, 0, 8, DFF + hb * 512, nm * 128)
                for n in range(2):
                    for m in range(nm):
                        bg, bu = self.bank(), self.bank()
                        for kc in range(8):
                            P.matmul(bg, wg[:, kc, m * 128:(m + 1) * 128], self.h[n][:, kc, :], start=(kc == 0), stop=(kc == 7))
                        for kc in range(8):
                            P.matmul(bu, wu[:, kc, m * 128:(m + 1) * 128], self.h[n][:, kc, :], start=(kc == 0), stop=(kc == 7))
                        s = sgt[it % 2]
                        it += 1
                        P.act(s, bg, AF.Silu)
                        P.tt(DVE, hid[n][:, hb * 4 + m, :], bu, s, ALU.mult)
            for cb in range(4):
                side_step(2)
                wA = self.wtile(w2, 0, 11, cb * 256, 256)
                wB = self.wtile(w2, 11, 11, cb * 256, 256)
                for n in range(2):
                    for m in range(2):
                        c = cb * 2 + m
                        by = self.bank()
                        for kc in range(22):
                            w = wA if kc < 11 else wB
                            P.matmul(by, w[:, kc % 11, m * 128:(m + 1) * 128], hid[n][:, kc, :], start=(kc == 0), stop=(kc == 21))
                        P.stt(DVE, self.xa[n][:, c, :], by, gcol[:, c:c + 1], self.xa[n][:, c, :], ALU.mult, ALU.add)
            if side is not None:
                side_step(99)
                self.bank_pool = None
                self.derive_vectors(self.g, first=False)
            self.ln_tiles(l, j)

    def conv3(self, o, p, wcols, row, nmul):
        P = self.P
        SL = self.segL
        w0, w1, w2 = (wcols[:, j * nmul + row:j * nmul + row + 1] for j in range(3))
        P.act(o, p, AF.Identity, scale=w1)
        o3 = V(o.t, o.ap.rearrange("p (s l) -> p s l", l=SL))
        p3 = V(p.t, p.ap.rearrange("p (s l) -> p s l", l=SL))
        P.stt(DVE, o3[:, :, 1:SL], p3[:, :, 0:SL - 1], w0, o3[:, :, 1:SL], ALU.mult, ALU.add)
        P.stt(DVE, o3[:, :, 0:SL - 1], p3[:, :, 1:SL], w2, o3[:, :, 0:SL - 1], ALU.mult, ALU.add)

    def proj(self, b, wv, col0, n, M=128):
        for kc in range(8):
            self.P.matmul(b[0:M, :], wv[:, kc, col0:col0 + M], self.h[n][:, kc, :], start=(kc == 0), stop=(kc == 7))

    def merge_branch(self, l, bi, br_w, src):
        P = self.P
        w_in = self.dram["w_in"][l]
        wbr = self.wtile(br_w, 0, 4, 0, 1024)
        with ExitStack() as s_:
            sig = [P.sb("sig%d" % i, [128, 512], F32, stack=s_) for i in range(2)]
            tmp = [P.sb("mtmp%d" % i, [128, 512], F32, stack=s_) for i in range(2)]
            it = 0
            for cb in range(2):
                wg = self.wtile(w_in, 0, 8, C_MG + bi * 1024 + cb * 512, 512)
                for n in range(2):
                    for m in range(4):
                        c = cb * 4 + m
                        bb, bg = self.bank(), self.bank()
                        for kc in range(4):
                            P.matmul(bb, wbr[:, kc, c * 128:(c + 1) * 128], src(kc, n), start=(kc == 0), stop=(kc == 3))
                        self.proj(bg, wg, m * 128, n)
                        sg = sig[it % 2]
                        P.act(sg, bg, AF.Sigmoid)
                        if bi == 0:
                            P.tt(DVE, self.merged[n][:, c, :], bb, sg, ALU.mult)
                        else:
                            t = tmp[it % 2]
                            P.tt(DVE, t, bb, sg, ALU.mult)
                            P.tt(DVE, self.merged[n][:, c, :], self.merged[n][:, c, :], t, ALU.add)
                        it += 1
            P.barrier()

    def mixer(self, l):
        P = self.P
        with ExitStack() as ms:
            self.merged = [P.sb("merged%d" % n, [128, 8, 512], F32, stack=ms) for n in range(2)]
            skip = getattr(self, "skip", ())
            if "a" not in skip:
                self.branch_a(l)
            if "d" not in skip:
                self.branch_d(l)
            if "g" not in skip:
                self.branch_g(l)
            for n in range(2):
                for c in range(8):
                    P.copy(ACT if c % 2 else DVE, self.h[n][:, c, :], self.merged[n][:, c, :])
            w_o = self.dram["w_o"][l]
            gcol = self.dvec[:, l, 1, 2, :]
            for cb in range(2):
                wo = self.wtile(w_o, 0, 8, cb * 512, 512)
                for n in range(2):
                    for m in range(4):
                        c = cb * 4 + m
                        by = self.bank()
                        self.proj(by, wo, m * 128, n)
                        P.stt(DVE, self.xa[n][:, c, :], by, gcol[:, c:c + 1], self.xa[n][:, c, :], ALU.mult, ALU.add)
            self.ln_tiles(l, 1)

    def branch_a(self, l):
        P = self.P
        w_in = self.dram["w_in"][l]
        cw = self.vec[l]["conv_a"]
        with ExitStack() as s_:
            ya = [P.sb("ya%d" % n, [128, 4, 512], BF16, stack=s_) for n in range(2)]
            s2_ = ExitStack()
            axs = [P.sb("axs%d" % i, [128, 512], F32, stack=s2_) for i in range(2)]
            pp = [P.sb("app%d" % i, [128, 512], F32, stack=s2_) for i in range(2)]
            oo = [P.sb("aoo%d" % i, [128, 512], F32, stack=s2_) for i in range(2)]
            wx = self.wtile(w_in, 0, 8, C_AX, 512)
            wb = self.wtile(w_in, 0, 8, C_AB, 512)
            wc = self.wtile(w_in, 0, 8, C_AC, 512)
            it = 0
            for n in range(2):
                for m in range(4):
                    bx, bc, bb = self.bank(), self.bank(), self.bank()
                    self.proj(bx, wx, m * 128, n)
                    self.proj(bc, wc, m * 128, n)
                    self.proj(bb, wb, m * 128, n)
                    a_, p_, o_ = axs[it % 2], pp[it % 2], oo[it % 2]
                    it += 1
                    P.copy(ACT, a_, bx)
                    P.tt(DVE, p_, bc, a_, ALU.mult)
                    self.conv3(o_[:, :], p_[:, :], cw, m, 4)
                    P.tt(DVE, ya[n][:, m, :], bb, o_, ALU.mult)
            P.barrier()
            s2_.close()
            self.merge_branch(l, 0, self.dram["w_br_a"][l], lambda kc, n: ya[n][:, kc, :])

    def blkv(self, tiles, blk, lead=None):
        n, b0 = blk // 4, (blk % 4) * 128
        if lead is None:
            return tiles[n][:, b0:b0 + 128]
        return tiles[n][:, lead, b0:b0 + 128]

    def branch_d(self, l):
        P = self.P
        g = self.g
        w_in = self.dram["w_in"][l]
        v = self.vec[l]
        cw = v["conv_qkv"]
        NBAT = 4
        with ExitStack() as s_:
            od = [P.sb("od%d" % n, [128, 4, 512], BF16, stack=s_) for n in range(2)]
            s2_ = ExitStack()
            sb = lambda name, shp, dt=F32: P.sb(name, shp, dt, stack=s2_)
            betaT = sb("betaT", [128, 8, 8]); gT = sb("gT", [128, 8, 8]); Gc = sb("Gc", [128, 8, 8])
            eGL = sb("eGL", [128, 8, 8]); bexpG = sb("bexpG", [128, 8, 8]); eGrev = sb("eGrev", [128, 8, 8])
            t8 = sb("t8", [128, 8, 8])
            wsm = self.wtile(w_in, 0, 8, C_DBETA, 16)
            bsm = self.bank()
            for blk in range(8):
                for kc in range(8):
                    P.matmul(bsm[:, blk * 16:(blk + 1) * 16], self.blkv(self.h, blk, kc), wsm[:, kc, :],
                             start=(kc == 0), stop=(kc == 7))
            bview = V(bsm, bsm.ap[:, 0:128].rearrange("p (b c) -> p b c", c=16))
            P.act(betaT, bview[:, :, 0:8], AF.Sigmoid)
            bc8 = lambda t: V(t, bass.AP(t.ap.tensor, t.ap.offset, [list(t.ap.ap[0]), [0, 8], [1, 8]]))
            P.tt(DVE, t8, bview[:, :, 8:16], bc8(v["dtb"]), ALU.add)
            P.act(t8, t8, AF.Exp)
            P.act(t8, t8, AF.Ln, bias=1.0)
            P.tt(DVE, gT, t8, bc8(v["nega"]), ALU.mult)
            bG = self.bank()
            t8f = V(t8, t8.ap.rearrange("p b c -> p (b c)"))
            gTd = [t8f[:, d * 32:(d + 1) * 32] for d in range(2)]
            for d in range(2):
                P.copy(DVE, V(gTd[d].t, gTd[d].ap.rearrange("p (b c) -> p b c", c=4)), gT[:, :, d * 4:d * 4 + 4])
                P.matmul(bG[:, d * 32:(d + 1) * 32], self.triI[d], gTd[d])
            for d in range(2):
                P.copy(DVE, Gc[:, :, d * 4:d * 4 + 4], V(bG, bG.ap[:, d * 32:(d + 1) * 32].rearrange("p (b c) -> p b c", c=4)))
            bL = self.bank()
            bLv = V(bL, bL.ap[:, 0:64].rearrange("p (b c) -> p b c", c=8))
            P.matmul(bL[:, 0:64], self.ones, V(gT, gT.ap.rearrange("p b c -> p (b c)")))
            P.act(eGL, bLv, AF.Exp)
            P.tt(DVE, t8, bLv, Gc, ALU.subtract)
            P.act(eGrev, t8, AF.Exp)
            P.act(t8, Gc, AF.Exp)
            P.tt(DVE, bexpG, betaT, t8, ALU.mult)
            qT = [sb("qT%d" % n, [128, 512], BF16) for n in range(2)]
            kT = [sb("kT%d" % n, [128, 512], BF16) for n in range(2)]
            zs = [sb("zs%d" % n, [128, 512], BF16) for n in range(2)]
            wqs = {}
            ktok = sb("ktok", [128, 8, 128], BF16); vtok = sb("vtok", [128, 8, 128], BF16)
            oacc = [sb("oacc%d" % n, [128, 512]) for n in range(2)]
            pre = sb("dpre", [128, 512]); cvo = sb("dcvo", [128, 512]); rr = sb("drr", [128, 512])
            vT = [V(rr, rr.ap.bitcast(BF16)[:, n * 512:(n + 1) * 512]) for n in range(2)]
            gU = [sb("gU%d" % i, [128, 128], BF16) for i in range(NBAT)]
            AB = [[sb("AB%d_%d" % (i, k), [128, 384]) for k in range(2)] for i in range(NBAT)]
            MN = [sb("MN%d" % i, [128, 256]) for i in range(NBAT)]
            CC = [sb("CC%d" % i, [128, 256]) for i in range(NBAT)]
            CCb = [V(CC[i], CC[i].ap.bitcast(BF16)[:, 0:256]) for i in range(NBAT)]
            WWb = [V(CC[i], CC[i].ap.bitcast(BF16)[:, 256:512]) for i in range(NBAT)]
            YY = [[V(AB[i][k], AB[i][k].ap.bitcast(BF16)[:, 0:256]) for k in range(2)] for i in range(NBAT)]
            Tt = [sb("Tt%d" % i, [128, 128], BF16) for i in range(NBAT)]
            vbk = [sb("vbk%d" % i, [128, 256], BF16) for i in range(2)]
            mk2 = lambda nm, dt=BF16, w=128: [[sb("%s%d_%d" % (nm, d, i), [128, w], dt) for i in range(8)] for d in range(2)]
            qdec, kdec, atT = mk2("qdec"), mk2("kdec"), mk2("atT")
            uw = mk2("uw", BF16, 256)
            usb = [[uw[d][i][:, 0:128] for i in range(8)] for d in range(2)]
            wTs = [[uw[d][i][:, 128:256] for i in range(8)] for d in range(2)]
            vnw = [sb("vnw%d" % i, [128, 128], BF16) for i in range(2)]
            nseq = len(self.seqs)
            S = [[sb("S%d_%d" % (q, d), [128, 128]) for d in range(2)] for q in range(nseq)]
            Sb = [[sb("Sb%d_%d" % (q, d), [128, 128], BF16) for d in range(2)] for q in range(nseq)]
            vi = [0]

            def proj_gen(hh):
                wq = self.wtile(w_in, 0, 8, [(C_DQ + hh * 128, 128), (C_DK + hh * 128, 128),
                                             (C_DV + hh * 128, 128), (C_DZ + hh * 128, 128)])
                wqs[hh] = wq
                for wi, dstT in ((0, qT), (1, kT), (2, vT)):
                    for n in range(2):
                        b = self.bank()
                        self.proj(b, wq, wi * 128, n)
                        P.copy(ACT, pre, b)
                        self.conv3(cvo[:, :], pre[:, :], cw, wi * 4 + hh, 12)
                        if wi == 2:
                            P.act(dstT[n], cvo, AF.Silu)
                        else:
                            P.act(cvo, cvo, AF.Silu)
                            preb = V(pre, pre.ap.bitcast(BF16)[:, 0:512])
                            P.act(preb, cvo, AF.Square)
                            bs_ = self.bank()
                            P.matmul(bs_, self.onesb, preb)
                            P.act(rr, bs_, AF.Sqrt, bias=RMS_EPS)
                            P.op(DVE, (lambda a_=rr.ap: lambda e: e.reciprocal(a_, a_))(), reads=[rr], writes=[rr])
                            if wi == 0:
                                P.stt(DVE, dstT[n], cvo, 128.0 ** -0.5, rr, ALU.mult, ALU.mult)
                            else:
                                P.tt(DVE, dstT[n], cvo, rr, ALU.mult)
                        yield
                for blk in range(8):
                    b = self.bank()
                    bb16 = V(b, b.ap.bitcast(BF16))
                    P.transpose(bb16[:, 0:128], self.blkv(kT, blk), self.identb)
                    P.transpose(bb16[:, 128:256], self.blkv(vT, blk), self.identb)
                    P.copy(ACT, ktok[:, blk, :], bb16[:, 0:128])
                    P.copy(DVE, vtok[:, blk, :], bb16[:, 128:256])
                    if blk % 2:
                        yield

            def bc2(vw):
                a = _ap(vw)
                return V(vw.t, bass.AP(a.tensor, a.offset, [list(a.ap[0]), [0, 2], list(a.ap[1])]))

            def seg2(vw):
                a = _ap(vw)
                return V(vw.t, bass.AP(a.tensor, a.offset, [list(a.ap[0]), [256, 2], [1, 128]]))

            def h2(vw):
                a = _ap(vw)
                return V(vw.t, a.rearrange("p (a b) -> p a b", a=2))

            def prescan_gen(hh, d, batches):
                idx = d * 4 + hh
                for blks in batches:
                    col = lambda t, blk: t[:, blk, idx:idx + 1]
                    st = lambda blk: blk % NBAT
                    pg = {blk: self.banks[blk % NBAT] for blk in blks}
                    for blk in blks:
                        r2 = AB[st(blk)][1][:, 0:128]
                        P.ts(DVE, r2, self.triI[d], col(gT, blk), ALU.mult)
                        P.matmul(pg[blk][:, 0:128], self.ones, r2)
                        P.matmul(pg[blk][:, 128:256], self.blkv(kT, blk), self.blkv(kT, blk))
                    yield
                    for blk in blks:
                        P.stt(DVE, h2(CC[st(blk)][:, 0:256]), bc2(pg[blk][:, 0:128]), col(Gc, blk), h2(self.NP[d][:, 0:256]),
                              ALU.subtract, ALU.add)
                        P.act(AB[st(blk)][0][:, 0:128], pg[blk][:, 0:128], AF.Exp)
                    yield
                    for blk in blks:
                        P.act(gU[st(blk)], CC[st(blk)][:, 0:128], AF.Exp)
                        P.act(CC[st(blk)][:, 128:256], CC[st(blk)][:, 128:256], AF.Exp, scale=-1.0)
                        P.tt(DVE, qdec[d][blk], self.blkv(qT, blk), AB[st(blk)][0][:, 0:128], ALU.mult)
                        P.stt(DVE, MN[st(blk)][:, 0:128], pg[blk][:, 128:256], col(betaT, blk), CC[st(blk)][:, 128:256],
                              ALU.mult, ALU.mult)
                    yield
                    pb = {blk: self.banks[blk % NBAT] for blk in blks}
                    for blk in blks:
                        P.transpose(pb[blk][:, 0:128], MN[st(blk)][:, 0:128], self.ident)
                    for blk in blks:
                        P.copy(DVE, MN[st(blk)][:, 128:256], pb[blk][:, 0:128])
                    yield
                    for blk in blks:
                        P.tt(DVE, h2(CC[st(blk)][:, 0:256]), h2(MN[st(blk)][:, 0:256]), bc2(self.bd16[:, :]), ALU.mult)
                    for blk in blks:
                        M0, N0 = CC[st(blk)][:, 0:128], CC[st(blk)][:, 128:256]
                        P.tt(DVE, AB[st(blk)][1][:, 128:256], self.ident, N0, ALU.subtract)
                        P.matmul(pb[blk][:, 0:128], M0, N0)
                        P.matmul(pb[blk][:, 256:384], N0, M0)
                    yield
                    for blk in blks:
                        P.copy(ACT, seg2(AB[st(blk)][1][:, 0:384]), seg2(pb[blk][:, 0:384]))
                    for k in (1, 2):
                        cur, nxt = k % 2, (k + 1) % 2
                        for blk in blks:
                            A_ = AB[st(blk)][cur]
                            P.matmul(pb[blk][:, 0:256], A_[:, 256:384], A_[:, 0:256])
                            P.matmul(pb[blk][:, 256:384], A_[:, 0:128], A_[:, 256:384])
                        yield
                        for blk in blks:
                            P.copy(ACT, seg2(AB[st(blk)][nxt][:, 0:384]), seg2(pb[blk][:, 0:384]))
                            P.tt(DVE, AB[st(blk)][nxt][:, 128:256], AB[st(blk)][cur][:, 128:256], pb[blk][:, 128:256], ALU.add)
                    for blk in blks:
                        P.matmul(pb[blk][:, 0:128], AB[st(blk)][1][:, 256:384], AB[st(blk)][1][:, 128:256])
                    yield
                    for blk in blks:
                        pbb = V(pb[blk], pb[blk].ap.bitcast(BF16))
                        P.tt(DVE, YY[st(blk)][0][:, 128:256], AB[st(blk)][1][:, 128:256], pb[blk][:, 0:128], ALU.add)
                        P.transpose(pbb[:, 256:384], YY[st(blk)][0][:, 128:256], self.identb)
                    for blk in blks:
                        pbb = V(pb[blk], pb[blk].ap.bitcast(BF16))
                        P.copy(ACT, YY[st(blk)][0][:, 0:128], pbb[:, 256:384])
                    yield
                    for li in range(3):
                        mk = self.mk[li]
                        last = (li == 2)
                        for blk in blks:
                            P.tt(DVE, h2(CCb[st(blk)][:, 0:256]), h2(MN[st(blk)][:, 0:256]), bc2(mk[:, :]), ALU.mult)
                        for blk in blks:
                            cur = YY[st(blk)][li % 2]
                            Y, Yt = cur[:, 0:128], cur[:, 128:256]
                            C, Ct = CCb[st(blk)][:, 0:128], CCb[st(blk)][:, 128:256]
                            if not last:
                                P.matmul(pb[blk][:, 0:128], Ct, Y)
                            P.matmul(pb[blk][:, 128:256], C, Yt)
                        yield
                        for blk in blks:
                            if not last:
                                P.copy(ACT, WWb[st(blk)][:, 0:256], pb[blk][:, 0:256])
                            else:
                                P.copy(ACT, WWb[st(blk)][:, 128:256], pb[blk][:, 128:256])
                        for blk in blks:
                            cur = YY[st(blk)][li % 2]
                            Y, Yt = cur[:, 0:128], cur[:, 128:256]
                            if not last:
                                P.matmul(pb[blk][:, 256:384], Yt, WWb[st(blk)][:, 0:128])
                            P.matmul(pb[blk][:, 384:512], Y, WWb[st(blk)][:, 128:256])
                        yield
                        for blk in blks:
                            cur, nxt = YY[st(blk)][li % 2], YY[st(blk)][(li + 1) % 2]
                            if not last:
                                P.tt(DVE, nxt[:, 0:256], cur[:, 0:256], pb[blk][:, 256:512], ALU.subtract)
                            else:
                                P.tt(DVE, Tt[st(blk)], cur[:, 128:256], pb[blk][:, 384:512], ALU.subtract)
                    pu = {blk: self.banks[blk % NBAT] for blk in blks}
                    for blk in blks:
                        vk = vbk[blk % 2]
                        P.act(vk[:, 0:128], vtok[:, blk, :], AF.Copy, scale=col(betaT, blk))
                        P.act(vk[:, 128:256], ktok[:, blk, :], AF.Copy, scale=col(bexpG, blk))
                        P.act(kdec[d][blk], ktok[:, blk, :], AF.Copy, scale=col(eGrev, blk))
                        P.matmul(pu[blk][:, 0:128], Tt[st(blk)], vk[:, 0:128])
                        P.matmul(pu[blk][:, 128:256], vk[:, 128:256], Tt[st(blk)])
                        P.matmul(pu[blk][:, 256:384], self.blkv(kT, blk), self.blkv(qT, blk))
                    yield
                    for blk in blks:
                        P.copy(ACT, uw[d][blk], pu[blk][:, 0:256])
                        P.tt(DVE, atT[d][blk], pu[blk][:, 256:384], gU[st(blk)], ALU.mult)
                    yield

            def scan_gen(hh, d):
                idx = d * 4 + hh
                for q, (b0, nb) in enumerate(self.seqs):
                    if g == 0:
                        P.memset(DVE, S[q][d], 0.0)
                        P.memset(DVE, Sb[q][d], 0.0)
                    else:
                        P.dma(SP, S[q][d], self.dram["sd"][l, d, hh])
                        P.copy(ACT, Sb[q][d], S[q][d])
                nb = self.seqs[0][1]
                for step in range(nb):
                    for q, (b0, _) in enumerate(self.seqs):
                        blk = b0 + (step if d == 0 else nb - 1 - step)
                        St, Sbt = S[q][d], Sb[q][d]
                        p1 = self.banks[4 + vi[0] % 2]
                        P.matmul(p1[:, 0:128], wTs[d][blk], Sbt)
                        vn = vnw[vi[0] % 2]
                        vi[0] += 1
                        P.tt(DVE, vn, usb[d][blk], p1[:, 0:128], ALU.subtract)
                        yield
                        P.matmul(p1[:, 128:256], Sbt, qdec[d][blk], start=True, stop=False)
                        P.matmul(p1[:, 128:256], vn, atT[d][blk], start=False, stop=True)
                        P.matmul(p1[:, 256:384], kdec[d][blk], vn)
                        ov = self.blkv(oacc, blk)
                        if d == 0:
                            P.copy(ACT, ov, p1[:, 128:256])
                        else:
                            P.tt(DVE, ov, ov, p1[:, 128:256], ALU.add)
                        P.stt(DVE, St, St, eGL[:, blk, idx:idx + 1], p1[:, 256:384], ALU.mult, ALU.add)
                        P.copy(ACT, Sbt, St)
                        yield
                if g == 0:
                    for q in range(nseq):
                        P.dma(SP, self.outs["nsd"][q, l, d, hh], S[q][d])

            def norm(hh):
                for n in range(2):
                    b = self.bank()
                    self.proj(b, wqs[hh], 3 * 128, n)
                    P.act(zs[n], b, AF.Silu)
                for n in range(2):
                    preb = V(pre, pre.ap.bitcast(BF16)[:, 0:512])
                    P.act(preb, oacc[n], AF.Square)
                    bs_ = self.bank()
                    P.matmul(bs_, self.onesb, preb)
                    P.act(rr, bs_, AF.Sqrt, bias=RMS_EPS, scale=1.0 / 128.0)
                    P.op(DVE, (lambda a_=rr.ap: lambda e: e.reciprocal(a_, a_))(), reads=[rr], writes=[rr])
                    P.tt(DVE, rr, oacc[n], rr, ALU.mult)
                    P.stt(DVE, od[n][:, hh, :], rr, v["dng"][:, 0:1], zs[n], ALU.mult, ALU.mult)

            def prescan2(hh, d, offset=1):
                ga = prescan_gen(hh, d, [[0, 1], [4, 5]])
                gb = prescan_gen(hh, d, [[2, 3], [6, 7]])
                rnd = 0
                while ga is not None or gb is not None:
                    if ga is not None:
                        try:
                            next(ga)
                        except StopIteration:
                            ga = None
                    if gb is not None and rnd >= offset:
                        try:
                            next(gb)
                        except StopIteration:
                            gb = None
                    rnd += 1
                    yield

            def run(main, side=None, ratio=2):
                cnt = 0
                for _ in main:
                    cnt += 1
                    if side is not None and cnt % ratio == 0:
                        try:
                            next(side)
                        except StopIteration:
                            side = None
                if side is not None:
                    for _ in side:
                        pass

            self.bank_pool = [6, 7]
            run(proj_gen(0))
            for hh in range(4):
                run(prescan2(hh, 0))
                run(prescan2(hh, 1), scan_gen(hh, 0), ratio=1)
                if hh < 3:
                    run(scan_gen(hh, 1), proj_gen(hh + 1), ratio=1)
                else:
                    run(scan_gen(hh, 1))
                norm(hh)
            self.bank_pool = None
            P.barrier()
            s2_.close()
            self.merge_branch(l, 1, self.dram["w_br_d"][l], lambda kc, n: od[n][:, kc, :])

    def branch_g(self, l):
        P = self.P
        g = self.g
        w_in = self.dram["w_in"][l]
        v = self.vec[l]
        NB4 = 4
        with ExitStack() as s_:
            og = [P.sb("og%d" % n, [128, 4, 512], BF16, stack=s_) for n in range(2)]
            s2_ = ExitStack()
            sb = lambda name, shp, dt=F32: P.sb(name, shp, dt, stack=s2_)
            gqT = [[sb("gqT%d_%d" % (hp, n), [128, 512], BF16) for n in range(2)] for hp in range(2)]
            gkT = [[sb("gkT%d_%d" % (hp, n), [128, 512], BF16) for n in range(2)] for hp in range(2)]
            gktok = sb("gktok", [128, 8, 256], BF16)
            gvtok = sb("gvtok", [128, 8, 512], BF16)
            lra = [sb("lra%d" % s, [17, 1024], BF16) for s in range(2)]
            ogacc = [[sb("ogacc%d_%d" % (a, n), [128, 512]) for n in range(2)] for a in range(2)]
            sp = sb("gsp", [128, 8, 128])
            kd = [sb("gkd%d" % i, [128, 8, 128], BF16) for i in range(2)]
            ek = [sb("gek_%d" % i, [128, 128], BF16) for i in range(NB4)]
            ebuf = sb("gebuf", [128, NB4, 128])
            eB = [ebuf[:, i, :] for i in range(NB4)]
            enB = [sb("genB_%d" % i, [128, 128], BF16) for i in range(NB4)]
            eBl = [sb("geBl%d" % i, [128, 8]) for i in range(2)]
            qe = [[[sb("gqe%d_%d_%d" % (pp, a, i), [128, 128], BF16) for i in range(8)] for a in range(2)] for pp in range(2)]
            qef = [sb("gqef%d" % i, [128, 128], BF16) for i in range(NB4)]
            rm = sb("grm", [128, 2])
            P.memset(DVE, rm, 0.0)
            P.memset(DVE, rm[0:64, 0:1], 1.0)
            P.memset(DVE, rm[64:128, 1:2], 1.0)
            ke = [sb("gke%d" % i, [128, 128], BF16) for i in range(NB4)]
            atT = [[[sb("gat%d_%d_%d" % (pp, a, i), [128, 128], BF16) for i in range(8)] for a in range(2)] for pp in range(2)]
            nseq = len(self.seqs)
            S = [[sb("gS%d_%d" % (q, pp), [128, 128]) for pp in range(2)] for q in range(nseq)]
            Sb = [[sb("gSb%d_%d" % (q, pp), [128, 128], BF16) for pp in range(2)] for q in range(nseq)]
            ebf = V(ebuf, ebuf.ap.rearrange("p b c -> p (b c)").bitcast(BF16))
            grs = [ebf[:, n * 512:(n + 1) * 512] for n in range(2)]
            spf = V(sp, sp.ap.rearrange("p b c -> p (b c)"))
            sq, rr = spf[:, 0:512], spf[:, 512:1024]

            wqk = self.wtile(w_in, 0, 8, C_GQ, 512)
            for hp in range(2):
                for n in range(2):
                    b = self.bank()
                    self.proj(b, wqk, hp * 128, n)
                    P.ts(DVE, gqT[hp][n], b, 0.125, ALU.mult)
                    b2 = self.bank()
                    self.proj(b2, wqk, 256 + hp * 128, n)
                    P.copy(ACT, gkT[hp][n], b2)
            for blk in range(8):
                b = self.bank()
                for kc in range(8):
                    P.matmul(b[:, 0:256], self.blkv(self.h, blk, kc), wqk[:, kc, 256:512], start=(kc == 0), stop=(kc == 7))
                P.copy(ACT, gktok[:, blk, :], b[:, 0:256])
            wv = self.wtile(w_in, 0, 8, C_GV, 512)
            for blk in range(8):
                b = self.bank()
                for kc in range(8):
                    P.matmul(b, self.blkv(self.h, blk, kc), wv[:, kc, :], start=(kc == 0), stop=(kc == 7))
                P.copy(DVE if blk % 2 else ACT, gvtok[:, blk, :], b)
            wl = self.wtile(w_in, 0, 8, C_GLR, 32)
            for s in range(2):
                P.memset(DVE, lra[s], 1.0)
                for n in range(2):
                    b = self.bank()
                    self.proj(b, wl, s * 16, n, M=16)
                    P.copy(ACT, lra[s][0:16, n * 512:(n + 1) * 512], b[0:16, :])
            wr = self.wtile(w_in, 0, 8, C_GR, 512)

            def prescan_half(hp, s, pp, batches):
                w2b = v["w2b"][s]
                for blks in batches:
                    st = lambda blk: blk % NB4
                    bl = {blk: self.banks[blk % NB4] for blk in blks}
                    for blk in blks:
                        P.matmul(bl[blk][:, 128:256], sp[:, blk, :], self.gtriI[s])
                        P.matmul(bl[blk][:, 256:384], self.gtriX[s], sp[:, blk, :])
                    yield
                    lc = 127 if s == 0 else 0
                    for blk in blks:
                        P.act(eB[st(blk)], bl[blk][:, 128:256], AF.Exp)
                        P.act(enB[st(blk)], bl[blk][:, 128:256], AF.Exp, scale=-1.0)
                        P.act(ek[st(blk)], bl[blk][:, 256:384], AF.Exp)
                    yield
                    for blk in blks:
                        eb, enb = eB[st(blk)], enB[st(blk)]
                        P.copy(DVE, eBl[pp][:, blk:blk + 1], eb[:, lc:lc + 1])
                        P.tt(DVE, qef[st(blk)], self.blkv(gqT[hp], blk), eb, ALU.mult)
                        for a in range(2):
                            P.ts(DVE, qe[pp][a][blk], qef[st(blk)], rm[:, a:a + 1], ALU.mult)
                        P.tt(DVE, ke[st(blk)], self.blkv(gkT[hp], blk), enb, ALU.mult)
                        P.tt(DVE, kd[pp][:, blk, :], gktok[:, blk, hp * 128:(hp + 1) * 128], ek[st(blk)], ALU.mult)
                    yield
                    pa = {blk: self.banks[blk % NB4] for blk in blks}
                    for blk in blks:
                        for a in range(2):
                            P.matmul(pa[blk][:, a * 128:(a + 1) * 128], ke[st(blk)], qe[pp][a][blk])
                    yield
                    for blk in blks:
                        for a in range(2):
                            P.tt(DVE, atT[pp][a][blk], pa[blk][:, a * 128:(a + 1) * 128], self.m01[s], ALU.mult)
                    yield

            sci = [0]

            def prescan_gen(hp, s, pp, offset=1):
                w2b_ = v["w2b"][s]
                for blk in range(8):
                    P.matmul(self.banks[blk // 4][:, (blk % 4) * 128:(blk % 4) * 128 + 128],
                             lra[s][0:17, blk * 128:(blk + 1) * 128], w2b_[0:17, hp * 128:(hp + 1) * 128])
                sph = [V(sp, sp.ap[:, 4 * hf:4 * hf + 4, :].rearrange("p b c -> p (b c)")) for hf in range(2)]
                for hf in range(2):
                    P.act(sph[hf], self.banks[hf][:, 0:512], AF.Exp, scale=-1.0)
                for hf in range(2):
                    P.act(sph[hf], sph[hf], AF.Ln, bias=1.0)
                yield
                ga = prescan_half(hp, s, pp, [[0, 1], [4, 5]])
                gb = prescan_half(hp, s, pp, [[2, 3], [6, 7]])
                rnd = 0
                while ga is not None or gb is not None:
                    if ga is not None:
                        try:
                            next(ga)
                        except StopIteration:
                            ga = None
                    if gb is not None and rnd >= offset:
                        try:
                            next(gb)
                        except StopIteration:
                            gb = None
                    rnd += 1
                    yield

            def scan_gen(hp, s, pp):
                for q in range(nseq):
                    if g == 0:
                        P.memset(DVE, S[q][pp], 0.0)
                        P.memset(DVE, Sb[q][pp], 0.0)
                    else:
                        P.dma(SP, S[q][pp], self.dram["sg"][l, s, hp * 2:hp * 2 + 2].rearrange("h k v -> (h k) v"))
                        P.copy(ACT, Sb[q][pp], S[q][pp])
                nb = self.seqs[0][1]
                for step in range(nb):
                    for q, (b0, _) in enumerate(self.seqs):
                        blk = b0 + (step if s == 0 else nb - 1 - step)
                        St, Sbt = S[q][pp], Sb[q][pp]
                        po = self.banks[4 + sci[0] % 2]
                        sci[0] += 1
                        for a in range(2):
                            head = hp * 2 + a
                            P.matmul(po[:, a * 128:(a + 1) * 128], Sbt, qe[pp][a][blk], start=True, stop=False)
                            P.matmul(po[:, a * 128:(a + 1) * 128], gvtok[:, blk, head * 128:(head + 1) * 128], atT[pp][a][blk],
                                     start=False, stop=True)
                        for a in range(2):
                            head = hp * 2 + a
                            P.matmul(po[:, 256 + a * 128:384 + a * 128], kd[pp][:, blk, :],
                                     gvtok[:, blk, head * 128:(head + 1) * 128])
                        yield
                        for a in range(2):
                            ov = self.blkv(ogacc[a], blk)
                            if s == 0:
                                P.copy(ACT, ov, po[:, a * 128:(a + 1) * 128])
                            else:
                                P.tt(DVE, ov, ov, po[:, a * 128:(a + 1) * 128], ALU.add)
                        for a in range(2):
                            rws = slice(a * 64, a * 64 + 64)
                            P.stt(DVE, St[rws, :], St[rws, :], eBl[pp][rws, blk:blk + 1],
                                  po[rws, 256 + a * 128:384 + a * 128], ALU.mult, ALU.add)
                        P.copy(ACT, Sbt, St)
                        yield
                if g == 0:
                    for q in range(nseq):
                        P.dma(SP, self.outs["nsg"][q, l, s, hp * 2:hp * 2 + 2].rearrange("h k v -> (h k) v"), S[q][pp])

            def norm(hp):
                for a in range(2):
                    head = hp * 2 + a
                    for n in range(2):
                        b = self.bank()
                        self.proj(b, wr, head * 128, n)
                        P.act(grs[n], b, AF.Silu)
                    for n in range(2):
                        sqb = V(sq.t, sq.ap.bitcast(BF16)[:, 0:512])
                        P.act(sqb, ogacc[a][n], AF.Square)
                        bs_ = self.bank()
                        P.matmul(bs_, self.onesb, sqb)
                        P.act(rr, bs_, AF.Sqrt, bias=RMS_EPS, scale=1.0 / 128.0)
                        P.op(DVE, (lambda a_=rr.ap: lambda e: e.reciprocal(a_, a_))(), reads=[sp], writes=[sp])
                        P.tt(DVE, rr, ogacc[a][n], rr, ALU.mult)
                        P.stt(DVE, og[n][:, head, :], rr, v["gng"][:, 0:1], grs[n], ALU.mult, ALU.mult)

            def run(main, side=None, ratio=2):
                cnt = 0
                for _ in main:
                    cnt += 1
                    if side is not None and cnt % ratio == 0:
                        try:
                            next(side)
                        except StopIteration:
                            side = None
                if side is not None:
                    for _ in side:
                        pass

            self.bank_pool = [6, 7]
            run(prescan_gen(0, 0, 0))
            run(prescan_gen(0, 1, 1), scan_gen(0, 0, 0), ratio=1)
            run(prescan_gen(1, 0, 0), scan_gen(0, 1, 1), ratio=1)
            norm(0)
            run(prescan_gen(1, 1, 1), scan_gen(1, 0, 0), ratio=1)
            run(scan_gen(1, 1, 1))
            norm(1)
            self.bank_pool = None
            P.barrier()
            s2_.close()
            self.merge_branch(l, 2, self.dram["w_br_g"][l], lambda kc, n: og[n][:, kc, :])


WEIGHT_KEYS = ["w_ada", "b_ada", "ln_g", "ln_b", "ffn_w1", "ffn_w2", "w_in", "conv_a", "conv_qkv",
               "delta_a_log", "delta_dt_bias", "delta_norm_g", "gla_w2", "gla_b", "gla_norm_g",
               "w_br_a", "w_br_d", "w_br_g", "w_o"]


def make_in_maps(inputs, n_cores=8):
    f = lambda a: np.ascontiguousarray(np.asarray(a, dtype=np.float32))
    shared = {k: f(inputs[k]) for k in WEIGHT_KEYS}
    maps = []
    for c in range(n_cores):
        m = dict(shared)
        m["xp"] = f(inputs["x_prompt"][4 * c:4 * c + 4]).reshape(TG, D)
        m["xs"] = f(inputs["x_sample"][c]).reshape(TG, D)
        m["sd"] = f(inputs["state_delta"][c])
        m["sg"] = f(inputs["state_gla"][c])
        m["cond"] = f(np.stack([np.asarray(inputs["c_ctx"]), np.asarray(inputs["c"])[c]], 0))
        maps.append(m)
    return maps


_NC_CACHE = {}


def kernel(**inputs):
    if "nc" not in _NC_CACHE:
        _NC_CACHE["nc"] = Builder().nc
    nc = _NC_CACHE["nc"]
    maps = make_in_maps(inputs)
    res = run_bass_kernel_spmd(nc, maps, core_ids=list(range(8)))
    r = res.results
    y_prompt = np.concatenate([r[c]["yp"].reshape(4, 256, D) for c in range(8)], 0).astype(np.float32)
    y_sample = np.stack([r[c]["ys"].reshape(1024, D) for c in range(8)], 0).astype(np.float32)
    nsd = np.concatenate([r[c]["nsd"] for c in range(8)], 0).astype(np.float32)
    nsg = np.concatenate([r[c]["nsg"] for c in range(8)], 0).astype(np.float32)
    return (y_prompt, y_sample, nsd, nsg)
```
